# Optimizing a Trainium2 kernel written in Bass

```python
import math
import jax
import jax.numpy as jnp
from jax import lax
import numpy as np

D_MODEL = 1024
BATCH = 4
SEQ = 4096
DEPTH = 2
DEC_BATCH = 32
DEC_SEQ = 8
PAST_LEN = 8192
PAGE_SIZE = 128

HG_HEADS = 4
HG_DK = 128
HG_DV = 128
DA_HEADS = 4
DA_DQK = 64
DA_DV = 2 * DA_DQK
ROPE_THETA = 10000.0
GD_HEADS = 4
GD_DK = 128
GD_DV = 128
GD_CONV = 4
GD_CONV_CH = GD_HEADS * (2 * GD_DK + GD_DV)
D_FF = 2816
FFN_CONV = 3
CHUNK = 64
Q_BLOCK = 128
EPS = 1e-6
NEG_INF = -1e30

IN_SIZES = (HG_HEADS * HG_DK, HG_HEADS * HG_DK, HG_HEADS * HG_DV, HG_HEADS * HG_DV,
            DA_HEADS * 2 * DA_DQK, DA_HEADS * 2 * DA_DQK, DA_HEADS * DA_DV,
            GD_HEADS * GD_DK, GD_HEADS * GD_DK, GD_HEADS * GD_DV, GD_HEADS * GD_DV,
            GD_HEADS, GD_HEADS,
            D_MODEL, D_MODEL, D_MODEL)
N_IN = sum(IN_SIZES)

kernel_name = 'hybrid_hgrn2_diffattn_gdn_convffn_step'


def _split_points(sizes):
    pts, acc = [], 0
    for s in sizes[:-1]:
        acc += s
        pts.append(acc)
    return pts


def rmsnorm(x, g):
    xf = x.astype(jnp.float32)
    y = xf * lax.rsqrt(jnp.mean(xf * xf, axis=-1, keepdims=True) + EPS)
    return (y * g.astype(jnp.float32)).astype(x.dtype)


def l2norm(x):
    xf = x.astype(jnp.float32)
    return xf * lax.rsqrt(jnp.sum(xf * xf, axis=-1, keepdims=True) + EPS)


def rope(x, pos):
    d = x.shape[-1]
    half = d // 2
    inv = 1.0 / (ROPE_THETA ** (jnp.arange(0, d, 2, dtype=jnp.float32) / d))
    ang = pos.astype(jnp.float32)[:, None] * inv[None, :]
    cos = jnp.cos(ang)[None, :, None, None, :]
    sin = jnp.sin(ang)[None, :, None, None, :]
    xf = x.astype(jnp.float32)
    x1, x2 = xf[..., :half], xf[..., half:]
    return jnp.concatenate([x1 * cos - x2 * sin, x2 * cos + x1 * sin], axis=-1).astype(x.dtype)


def causal_dwconv(x, buf, w):
    K = w.shape[0]
    L = x.shape[1]
    xp = jnp.concatenate([buf.astype(x.dtype), x], axis=1)
    out = sum(xp[:, j:j + L] * w[j].astype(x.dtype) for j in range(K))
    return out, xp[:, L:]


def _to_chunks(a, c, n):
    b, L = a.shape[:2]
    a = jnp.pad(a, [(0, 0), (0, n * c - L)] + [(0, 0)] * (a.ndim - 2))
    a = a.reshape((b, n, c) + a.shape[2:])
    return jnp.swapaxes(jnp.moveaxis(a, 1, 0), 2, 3)


def _from_chunks(o, L):
    n, b, h, c, dv = o.shape
    return o.transpose(1, 0, 3, 2, 4).reshape(b, n * c, h, dv)[:, :L]


def gla_chunked(q, k, v, logf, s0):
    f32 = jnp.float32
    L = q.shape[1]
    c = min(CHUNK, L)
    n = -(-L // c)
    qs, ks, vs, gs = [_to_chunks(a.astype(f32), c, n) for a in (q, k, v, logf)]
    incl = jnp.tril(jnp.ones((c, c), dtype=bool))

    def step(S, xs):
        qc, kc, vc, gc = xs
        G = jnp.cumsum(gc, axis=2)
        diff = jnp.where(incl[:, :, None], G[:, :, :, None, :] - G[:, :, None, :, :], -jnp.inf)
        A = jnp.einsum('bhtd,bhsd,bhtsd->bhts', qc, kc, jnp.exp(diff))
        o = jnp.einsum('bhtd,bhdv->bhtv', qc * jnp.exp(G), S) + jnp.einsum('bhts,bhsv->bhtv', A, vc)
        GL = G[:, :, -1:, :]
        S = jnp.exp(GL[:, :, 0, :])[..., None] * S + jnp.einsum('bhsd,bhsv->bhdv', kc * jnp.exp(GL - G), vc)
        return S, o

    S, o = lax.scan(step, s0.astype(f32), (qs, ks, vs, gs))
    return _from_chunks(o, L), S


def gdn_chunked(q, k, v, beta, loga, s0):
    f32 = jnp.float32
    L = q.shape[1]
    c = min(CHUNK, L)
    n = -(-L // c)
    qs, ks, vs, bs, gs = [_to_chunks(a.astype(f32), c, n) for a in (q, k, v, beta, loga)]
    incl = jnp.tril(jnp.ones((c, c), dtype=bool))
    strict = jnp.tril(jnp.ones((c, c), dtype=bool), -1)

    def step(S, xs):
        qc, kc, vc, bc, gc = xs
        G = jnp.cumsum(gc, axis=-1)
        D = jnp.exp(jnp.where(incl, G[..., :, None] - G[..., None, :], -jnp.inf))
        KK = jnp.einsum('bhtd,bhsd->bhts', kc, kc)
        A = jnp.where(strict, bc[..., :, None] * D * KK, 0.0)
        eg = jnp.exp(G)[..., None]
        rhs = bc[..., None] * (vc - eg * jnp.einsum('bhtd,bhdv->bhtv', kc, S))
        U = lax.linalg.triangular_solve(A, rhs, left_side=True, lower=True, unit_diagonal=True)
        QK = jnp.einsum('bhtd,bhsd->bhts', qc, kc) * D
        o = eg * jnp.einsum('bhtd,bhdv->bhtv', qc, S) + jnp.einsum('bhts,bhsv->bhtv', QK, U)
        GL = G[..., -1]
        S = jnp.exp(GL)[..., None, None] * S + jnp.einsum(
            'bhsd,bhsv->bhdv', kc * jnp.exp(GL[..., None] - G)[..., None], U)
        return S, o

    S, o = lax.scan(step, s0.astype(f32), (qs, ks, vs, bs, gs))
    return _from_chunks(o, L), S


def diff_attention(q, k, v, lam, q_pos, k_pos):
    b, lq, h, _, d = q.shape
    blk = Q_BLOCK if lq % Q_BLOCK == 0 else lq
    nb = lq // blk
    qb = jnp.moveaxis(q.reshape(b, nb, blk, h, 2, d), 1, 0)
    pb = q_pos.reshape(nb, blk)
    scale = d ** -0.5

    def one_block(args):
        qi, pi = args
        s = jnp.einsum('bqhcd,bkhcd->bhcqk', qi, k, preferred_element_type=jnp.float32) * scale
        s = jnp.where(k_pos[None, :] <= pi[:, None], s, NEG_INF)
        p = jax.nn.softmax(s, axis=-1)
        wts = p[:, :, 0] - lam * p[:, :, 1]
        return jnp.einsum('bhqk,bkhd->bqhd', wts.astype(v.dtype), v)

    o = lax.map(one_block, (qb, pb))
    return jnp.moveaxis(o, 0, 1).reshape(b, lq, h, v.shape[-1])


def decoder_layer(x, pos0, layer_idx, lb, hg_s0, past_k, past_v, gd_s0, gd_buf, ffn_buf, w):
    f32 = jnp.float32
    b, L, _ = x.shape
    pos = pos0 + jnp.arange(L)
    h = rmsnorm(x, w['ln_mix'])
    (hq, hf, hi, hog, dq, dk, dv, gq, gk, gv, gz, gb, ga,
     gate_a, gate_b, gate_c) = jnp.split(h @ w['w_in'], _split_points(IN_SIZES), axis=-1)

    fg = lb + (1.0 - lb) * jax.nn.sigmoid(hf.astype(f32))
    o_h, hg_s = gla_chunked(
        jax.nn.silu(hq.astype(f32)).reshape(b, L, HG_HEADS, HG_DK),
        (1.0 - fg).reshape(b, L, HG_HEADS, HG_DK),
        hi.reshape(b, L, HG_HEADS, HG_DV),
        jnp.log(fg).reshape(b, L, HG_HEADS, HG_DK),
        hg_s0)
    og = jax.nn.silu(hog.astype(f32)).reshape(b, L, HG_HEADS, HG_DV)
    o_a = (rmsnorm(o_h, w['hgrn_norm']) * og).reshape(b, L, -1).astype(x.dtype)

    qd = rope(rmsnorm(dq.reshape(b, L, DA_HEADS, 2, DA_DQK), w['diff_qk_norm'][0]), pos)
    kd = rope(rmsnorm(dk.reshape(b, L, DA_HEADS, 2, DA_DQK), w['diff_qk_norm'][1]), pos)
    vd = dv.reshape(b, L, DA_HEADS, DA_DV)
    if past_k is None:
        k_all, v_all = kd, vd
    else:
        k_all = jnp.concatenate([past_k.reshape(b, -1, DA_HEADS, 2, DA_DQK).astype(kd.dtype), kd], axis=1)
        v_all = jnp.concatenate([past_v.reshape(b, -1, DA_HEADS, DA_DV).astype(vd.dtype), vd], axis=1)
    lam_init = 0.8 - 0.6 * math.exp(-0.3 * layer_idx)
    lp = w['diff_lambda'].astype(f32)
    lam = jnp.exp(jnp.sum(lp[0] * lp[1])) - jnp.exp(jnp.sum(lp[2] * lp[3])) + lam_init
    o_d = diff_attention(qd, k_all, v_all, lam, pos, jnp.arange(k_all.shape[1]))
    o_b = (rmsnorm(o_d, w['diff_subln']) * (1.0 - lam_init)).reshape(b, L, -1).astype(x.dtype)

    qkv, gd_buf_new = causal_dwconv(jnp.concatenate([gq, gk, gv], axis=-1), gd_buf, w['gdn_conv'])
    qkv = jax.nn.silu(qkv)
    cq, ck, cv = jnp.split(qkv, [GD_HEADS * GD_DK, 2 * GD_HEADS * GD_DK], axis=-1)
    q_g = l2norm(cq.reshape(b, L, GD_HEADS, GD_DK)) * (GD_DK ** -0.5)
    k_g = l2norm(ck.reshape(b, L, GD_HEADS, GD_DK))
    v_g = cv.reshape(b, L, GD_HEADS, GD_DV)
    beta = jax.nn.sigmoid(gb.astype(f32))
    loga = -jnp.exp(w['gdn_a_log'].astype(f32)) * jax.nn.softplus(ga.astype(f32) + w['gdn_dt_bias'].astype(f32))
    o_g, gd_s = gdn_chunked(q_g, k_g, v_g, beta, loga, gd_s0)
    oz = jax.nn.silu(gz.astype(f32)).reshape(b, L, GD_HEADS, GD_DV)
    o_c = (rmsnorm(o_g, w['gdn_norm']) * oz).reshape(b, L, -1).astype(x.dtype)

    mix = (jax.nn.sigmoid(gate_a) * (o_a @ w['w_branch_a'])
           + jax.nn.sigmoid(gate_b) * (o_b @ w['w_branch_b'])
           + jax.nn.sigmoid(gate_c) * (o_c @ w['w_branch_c']))
    x = x + mix @ w['w_out']

    h2 = rmsnorm(x, w['ln_ffn'])
    g_ff, u_ff = jnp.split(h2 @ w['w_up'], 2, axis=-1)
    g_ff, ffn_buf_new = causal_dwconv(g_ff, ffn_buf, w['ffn_conv'])
    x = x + (jax.nn.silu(g_ff) * u_ff) @ w['w_down']

    k_rows = kd.reshape(b, L, DA_HEADS, 2 * DA_DQK)
    return x, hg_s, k_rows, vd, gd_s, gd_buf_new, ffn_buf_new


def setup_inputs(seed: int = 0) -> dict:
    key = jax.random.key(seed)
    ks = jax.random.split(key, 32)
    f32 = jnp.float32
    n_pages = PAST_LEN // PAGE_SIZE
    n_used = DEC_BATCH * n_pages
    n_pool = n_used + n_used // 4

    def nrm(k, shape, s):
        return jax.random.normal(k, shape, f32) * s

    dt = jnp.exp(jax.random.uniform(ks[18], (DEPTH, GD_HEADS), f32, math.log(1e-3), math.log(1e-1)))
    return {
        'x_prompt': nrm(ks[0], (BATCH, SEQ, D_MODEL), 1.0),
        'x_sample': nrm(ks[1], (DEC_BATCH, DEC_SEQ, D_MODEL), 1.0),
        'state_hgrn': nrm(ks[2], (DEPTH, DEC_BATCH, HG_HEADS, HG_DK, HG_DV), 0.1),
        'cache_k': nrm(ks[3], (DEPTH, n_pool, PAGE_SIZE, DA_HEADS, 2 * DA_DQK), 1.0),
        'cache_v': nrm(ks[4], (DEPTH, n_pool, PAGE_SIZE, DA_HEADS, DA_DV), 1.0),
        'state_gdn': nrm(ks[5], (DEPTH, DEC_BATCH, GD_HEADS, GD_DK, GD_DV), 0.1),
        'state_gdn_conv': nrm(ks[6], (DEPTH, DEC_BATCH, GD_CONV - 1, GD_CONV_CH), 1.0),
        'state_ffn_conv': nrm(ks[7], (DEPTH, DEC_BATCH, FFN_CONV - 1, D_FF), 1.0),
        'page_table': jax.random.permutation(ks[8], n_pool)[:n_used].reshape(DEC_BATCH, n_pages).astype(jnp.int32),
        'ln_mix': 1.0 + nrm(ks[9], (DEPTH, D_MODEL), 0.02),
        'w_in': nrm(ks[10], (DEPTH, D_MODEL, N_IN), D_MODEL ** -0.5),
        'hgrn_lb': 1.0 + nrm(ks[11], (DEPTH, HG_HEADS * HG_DK), 0.1),
        'hgrn_norm': 1.0 + nrm(ks[12], (DEPTH, HG_DV), 0.02),
        'diff_qk_norm': 1.0 + nrm(ks[13], (DEPTH, 2, DA_DQK), 0.02),
        'diff_lambda': nrm(ks[14], (DEPTH, 4, DA_DQK), 0.1),
        'diff_subln': 1.0 + nrm(ks[15], (DEPTH, DA_DV), 0.02),
        'gdn_conv': nrm(ks[16], (DEPTH, GD_CONV, GD_CONV_CH), GD_CONV ** -0.5),
        'gdn_a_log': jnp.log(jax.random.uniform(ks[17], (DEPTH, GD_HEADS), f32, 1.0, 16.0)),
        'gdn_dt_bias': dt + jnp.log(-jnp.expm1(-dt)),
        'gdn_norm': 1.0 + nrm(ks[19], (DEPTH, GD_DV), 0.02),
        'w_branch_a': nrm(ks[20], (DEPTH, HG_HEADS * HG_DV, D_MODEL), (HG_HEADS * HG_DV) ** -0.5),
        'w_branch_b': nrm(ks[21], (DEPTH, DA_HEADS * DA_DV, D_MODEL), (DA_HEADS * DA_DV) ** -0.5),
        'w_branch_c': nrm(ks[22], (DEPTH, GD_HEADS * GD_DV, D_MODEL), (GD_HEADS * GD_DV) ** -0.5),
        'w_out': nrm(ks[23], (DEPTH, D_MODEL, D_MODEL), D_MODEL ** -0.5),
        'ln_ffn': 1.0 + nrm(ks[24], (DEPTH, D_MODEL), 0.02),
        'w_up': nrm(ks[25], (DEPTH, D_MODEL, 2 * D_FF), D_MODEL ** -0.5),
        'ffn_conv': nrm(ks[26], (DEPTH, FFN_CONV, D_FF), FFN_CONV ** -0.5),
        'w_down': nrm(ks[27], (DEPTH, D_FF, D_MODEL), D_FF ** -0.5),
    }


def reference(x_prompt, x_sample, state_hgrn, cache_k, cache_v, state_gdn, state_gdn_conv,
              state_ffn_conv, page_table, ln_mix, w_in, hgrn_lb, hgrn_norm, diff_qk_norm,
              diff_lambda, diff_subln, gdn_conv, gdn_a_log, gdn_dt_bias, gdn_norm, w_branch_a,
              w_branch_b, w_branch_c, w_out, ln_ffn, w_up, ffn_conv, w_down):
    f32 = jnp.float32
    lb_w = jax.nn.softmax(hgrn_lb.astype(f32), axis=0)
    lbs = jnp.cumsum(lb_w, axis=0) - lb_w[0]

    bp = x_prompt.shape[0]
    yp, ys = x_prompt, x_sample
    p_hgrn, p_k, p_v, p_gdn, p_gconv, p_fconv = [], [], [], [], [], []
    s_hgrn, s_k, s_v, s_gdn, s_gconv, s_fconv = [], [], [], [], [], []
    for l in range(DEPTH):
        w = {'ln_mix': ln_mix[l], 'w_in': w_in[l], 'hgrn_norm': hgrn_norm[l],
             'diff_qk_norm': diff_qk_norm[l], 'diff_lambda': diff_lambda[l],
             'diff_subln': diff_subln[l], 'gdn_conv': gdn_conv[l], 'gdn_a_log': gdn_a_log[l],
             'gdn_dt_bias': gdn_dt_bias[l], 'gdn_norm': gdn_norm[l],
             'w_branch_a': w_branch_a[l], 'w_branch_b': w_branch_b[l], 'w_branch_c': w_branch_c[l],
             'w_out': w_out[l], 'ln_ffn': ln_ffn[l], 'w_up': w_up[l], 'ffn_conv': ffn_conv[l],
             'w_down': w_down[l]}
        yp, hs, kr, vr, gs, gbuf, fbuf = decoder_layer(
            yp, 0, l, lbs[l],
            jnp.zeros((bp, HG_HEADS, HG_DK, HG_DV), f32), None, None,
            jnp.zeros((bp, GD_HEADS, GD_DK, GD_DV), f32),
            jnp.zeros((bp, GD_CONV - 1, GD_CONV_CH), yp.dtype),
            jnp.zeros((bp, FFN_CONV - 1, D_FF), yp.dtype), w)
        p_hgrn.append(hs); p_k.append(kr); p_v.append(vr)
        p_gdn.append(gs); p_gconv.append(gbuf); p_fconv.append(fbuf)
        past_k = cache_k[l, page_table]
        past_v = cache_v[l, page_table]
        ys, hs, kr, vr, gs, gbuf, fbuf = decoder_layer(
            ys, PAST_LEN, l, lbs[l], state_hgrn[l], past_k, past_v,
            state_gdn[l], state_gdn_conv[l], state_ffn_conv[l], w)
        s_hgrn.append(hs); s_k.append(kr); s_v.append(vr)
        s_gdn.append(gs); s_gconv.append(gbuf); s_fconv.append(fbuf)

    return (yp, ys,
            jnp.stack(p_hgrn), jnp.stack(p_k), jnp.stack(p_v), jnp.stack(p_gdn),
            jnp.stack(p_gconv), jnp.stack(p_fconv),
            jnp.stack(s_hgrn), jnp.stack(s_k), jnp.stack(s_v), jnp.stack(s_gdn),
            jnp.stack(s_gconv), jnp.stack(s_fconv))
```

```python
import contextlib
import os
import sys

import numpy as np
import concourse.bass as bass
import concourse.mybir as mybir
from concourse.bass_utils import run_bass_kernel_spmd

F32 = mybir.dt.float32
BF16 = mybir.dt.bfloat16
I32 = mybir.dt.int32
AF = mybir.ActivationFunctionType
ALU = mybir.AluOpType
AX = mybir.AxisListType

SEM_CAP = 2048
SEM_POOL_LIMIT = 96

ENGINES = ("pe", "act", "dve", "pool", "sp")


class View:
    __slots__ = ("ap", "toks")

    def __init__(self, ap, toks):
        self.ap = ap
        self.toks = tuple(toks)

    def __getitem__(self, idx):
        return View(self.ap[idx], self.toks)

    def with_ap(self, ap):
        return View(ap, self.toks)

    def bc(self, shape):
        return View(self.ap.broadcast_to(list(shape)), self.toks)

    def r3(self, b):
        return View(self.ap.rearrange("p (a b) -> p a b", b=b), self.toks)

    def bitcast(self, dt):
        return View(self.ap.bitcast(dt), self.toks)

    def rs(self, pattern, **kw):
        return View(self.ap.rearrange(pattern, **kw), self.toks)


class _Op:
    __slots__ = ("eng", "fn", "deps", "chan", "ndma", "event", "needed", "idx", "where")

    def __init__(self, eng, fn, chan=None, ndma=0):
        self.eng = eng
        self.fn = fn
        self.deps = set()
        self.chan = chan
        self.ndma = ndma
        self.event = None
        self.needed = False
        self.idx = -1


class Prog:
    def __init__(self, nc):
        self.nc = nc
        self.ops = []
        self.streams = {e: [] for e in ENGINES}
        self.last_w = {}
        self.readers = {}
        self.chan_last = {}
        self.barrier_chans = set()
        self.ntok = 0

    def tok(self, name):
        self.ntok += 1
        return (name, self.ntok)

    def view(self, ap, name, n=1):
        return View(ap, [self.tok(name) for _ in range(n)])

    def op(self, eng, fn, reads=(), writes=(), chan=None, ndma=0):
        o = _Op(eng, fn, chan, ndma)
        for v in reads:
            if v is None or not isinstance(v, View):
                continue
            for t in v.toks:
                w = self.last_w.get(t)
                if w is not None:
                    o.deps.add(w)
        for v in writes:
            for t in v.toks:
                w = self.last_w.get(t)
                if w is not None:
                    o.deps.add(w)
                for r in self.readers.get(t, ()):
                    o.deps.add(r)
        if chan is not None:
            prev = self.chan_last.get(chan)
            if prev is not None and chan not in self.barrier_chans:
                o.deps.add(prev)
            self.chan_last[chan] = o
        o.deps.discard(o)
        if eng == "pe":
            o.deps = {d for d in o.deps if d.eng != "pe"}
        for v in reads:
            if v is None or not isinstance(v, View):
                continue
            for t in v.toks:
                self.readers.setdefault(t, []).append(o)
        for v in writes:
            for t in v.toks:
                self.last_w[t] = o
                self.readers[t] = []
        o.idx = len(self.ops)
        fr = sys._getframe(2)
        o.where = f"{fr.f_code.co_name}:{fr.f_lineno}"
        self.ops.append(o)
        self.streams[eng].append(o)
        return o

    def emit(self, final_chans=()):
        nc = self.nc
        for o in self.ops:
            for d in o.deps:
                d.needed = True
        tails = [self.chan_last[c] for c in final_chans if c in self.chan_last]
        for t in tails:
            t.needed = True

        semkeys = []
        cnt = {}
        gen = {}

        def bump(base, step):
            g = gen.get(base, 0)
            c = cnt.get(base, 0)
            if c + step > SEM_CAP:
                g += 1
                c = 0
            c += step
            gen[base] = g
            cnt[base] = c
            key = (base, g)
            if not semkeys or key not in seen:
                seen.add(key)
                semkeys.append(key)
            return key, c

        seen = set()
        for e in ENGINES:
            for o in self.streams[e]:
                if o.chan is not None:
                    key = None
                    val = 0
                    if cnt.get(("c", o.chan), 0) + 16 * o.ndma > SEM_CAP:
                        cnt[("c", o.chan)] = SEM_CAP
                    for _ in range(o.ndma):
                        key, val = bump(("c", o.chan), 16)
                    o.event = (key, val)
                elif o.needed:
                    o.event = bump(("e", e), 1)
        finals = {}
        for o in self.ops:
            if o.chan in self.barrier_chans:
                k = o.event[0]
                finals[k] = max(finals.get(k, 0), o.event[1])
        assert len(semkeys) <= SEM_POOL_LIMIT, f"semaphore pool exhausted: {len(semkeys)}"
        self.n_sems = len(semkeys)

        with contextlib.ExitStack() as st:
            sems = {}
            for i, k in enumerate(semkeys):
                sems[k] = st.enter_context(nc.semaphore(f"s{i}"))
            block = st.enter_context(nc.Block())
            handles = {"pe": block.tensor, "act": block.scalar, "dve": block.vector,
                       "pool": block.gpsimd, "sp": block.sync}

            def make(ename):
                stream = self.streams[ename]

                def body(eng):
                    known = {}
                    for o in stream:
                        need = {}
                        for d in o.deps:
                            k, v = d.event
                            if d.chan in self.barrier_chans:
                                v = finals[k]
                            if known.get(k, 0) < v:
                                need[k] = max(need.get(k, 0), v)
                        for k, v in need.items():
                            eng.wait_ge(sems[k], v)
                            known[k] = v
                        try:
                            if o.chan is not None:
                                o.fn(eng, sems[o.event[0]])
                            else:
                                ins = o.fn(eng)
                                if o.needed:
                                    ins.then_inc(sems[o.event[0]], 1)
                        except Exception as exc:
                            raise RuntimeError(f"op #{o.idx} on {ename} recorded at {o.where}: {exc}") from exc
                    if ename == "sp":
                        for t in tails:
                            k, v = t.event
                            if known.get(k, 0) < v:
                                eng.wait_ge(sems[k], v)
                                known[k] = v
                return body

            for ename in ENGINES:
                if self.streams[ename] or ename == "sp":
                    handles[ename](make(ename))

    @staticmethod
    def _a(x):
        return x.ap if isinstance(x, View) else x

    def mm(self, out, pairs, extra_reads=(), start=True, stop=True):
        aps = [(l.ap, r.ap) for l, r in pairs]
        n = len(aps)
        oap = out.ap

        def fn(eng):
            ins = None
            for i, (l, r) in enumerate(aps):
                ins = eng.matmul(oap, l, r, start=(start and i == 0), stop=(stop and i == n - 1))
            return ins
        return self.op("pe", fn, reads=[v for p in pairs for v in p] + list(extra_reads),
                       writes=[out])

    def transpose(self, out, in_, ident):
        def fn(eng):
            return eng.transpose(out.ap, in_.ap, ident.ap)
        return self.op("pe", fn, reads=[in_, ident], writes=[out])

    def act(self, out, in_, func, bias=0.0, scale=1.0, accum=None):
        b, s = self._a(bias), self._a(scale)
        kw = {}
        if accum is not None:
            kw["accum_out"] = accum.ap

        def fn(eng):
            return eng.activation(out.ap, in_.ap, func, bias=b, scale=s, **kw)
        return self.op("act", fn, reads=[in_, bias, scale],
                       writes=[out] + ([accum] if accum is not None else []))

    def tt(self, eng_name, out, in0, in1, op):
        def fn(eng):
            return eng.tensor_tensor(out.ap, in0.ap, in1.ap, op)
        return self.op(eng_name, fn, reads=[in0, in1], writes=[out])

    def ts(self, eng_name, out, in0, s1, op0, s2=None, op1=None, accum=None):
        a1, a2 = self._a(s1), self._a(s2)
        kw = {}
        if op1 is not None:
            kw["op1"] = op1
        if accum is not None:
            kw["accum_out"] = accum.ap

        def fn(eng):
            return eng.tensor_scalar(out.ap, in0.ap, a1, a2, op0, **kw)
        return self.op(eng_name, fn, reads=[in0, s1, s2],
                       writes=[out] + ([accum] if accum is not None else []))

    def stt(self, eng_name, out, in0, scalar, in1, op0, op1):
        sc = self._a(scalar)

        def fn(eng):
            return eng.scalar_tensor_tensor(out.ap, in0.ap, sc, in1.ap, op0, op1)
        return self.op(eng_name, fn, reads=[in0, scalar, in1], writes=[out])

    def copy(self, eng_name, out, in_):
        if eng_name == "act":
            def fn(eng):
                return eng.copy(out.ap, in_.ap)
        else:
            def fn(eng):
                return eng.tensor_copy(out.ap, in_.ap)
        return self.op(eng_name, fn, reads=[in_], writes=[out])

    def memset(self, eng_name, out, val):
        def fn(eng):
            return eng.memset(out.ap, val)
        return self.op(eng_name, fn, writes=[out])

    def reduce(self, eng_name, out, in_, op, axis=None):
        ax = AX.X if axis is None else axis

        def fn(eng):
            return eng.tensor_reduce(out.ap, in_.ap, ax, op)
        return self.op(eng_name, fn, reads=[in_], writes=[out])

    def dma(self, queue, chan, out, in_, **kw):
        oap, iap = self._a(out), self._a(in_)

        def fn(eng, sem):
            return eng.dma_start(out=oap, in_=iap, **kw).then_inc(sem, 16)
        return self.op(queue, fn, reads=[in_] if isinstance(in_, View) else [],
                       writes=[out] if isinstance(out, View) else [], chan=chan, ndma=1)

    def recip(self, out, in_):
        def fn(eng):
            return eng.reciprocal(out.ap, in_.ap)
        return self.op("dve", fn, reads=[in_], writes=[out])

    def rsqrt(self, out, in_):
        self.recip(out, in_)
        return self.act(out, out, AF.Sqrt)

    def mm_multi(self, out, groups, reads, start=True, stop=True):
        def fn(eng):
            ins = None
            for oap, pairs in groups:
                n = len(pairs)
                for i, (l, r) in enumerate(pairs):
                    ins = eng.matmul(oap, l, r, start=(start and i == 0), stop=(stop and i == n - 1))
            return ins
        return self.op("pe", fn, reads=list(reads), writes=[out])

    def transpose_multi(self, out, items, reads):
        def fn(eng):
            ins = None
            for oap, iap, idap in items:
                ins = eng.transpose(oap, iap, idap)
            return ins
        return self.op("pe", fn, reads=list(reads), writes=[out])

    def scan(self, out, data0, data1, initial=0.0, op0=None, op1=None):
        o0 = ALU.mult if op0 is None else op0
        o1 = ALU.add if op1 is None else op1

        def fn(eng):
            return eng.tensor_tensor_scan(out.ap, data0.ap, data1.ap, initial, o0, o1)
        return self.op("dve", fn, reads=[data0, data1], writes=[out])


D = 1024
SEQ = 4096
DEPTH = 2
NB = 4
SB = 32
SL = 8
PAST = 8192
PAGE = 128
NPAGES = PAST // PAGE
NPOOL = 2560
NH = 4
DFF = 2816
NIN = 8712
GCH = 1536
EPS = 1e-6
TT = 512
NCORES = 8
SPC = SB // NCORES
ST = SPC * SL
O_HQ, O_HF, O_HI, O_HOG = 0, 512, 1024, 1536
O_DQ, O_DK, O_DV = 2048, 2560, 3072
O_GQ, O_GK, O_GV, O_GZ = 3584, 4096, 4608, 5120
O_GB, O_GA = 5632, 5636
O_GA_, O_GTA, O_GTB, O_GTC = 5636, 5640, 6664, 7688


def _decl(nc, name, shape, dt, kind):
    return nc.dram_tensor(name, list(shape), dt, kind=kind).ap()


IN_SPECS = [
    ("xT_p", (D, SEQ)), ("xT_s", (D, ST)),
    ("st_hgrn", (DEPTH, SPC, NH, 128, 128)), ("st_gdn", (DEPTH, SPC, NH, 128, 128)),
    ("st_gconv", (DEPTH, SPC, GCH, 3)), ("st_fconv", (DEPTH, SPC, DFF, 2)),
    ("cache_k", (DEPTH, NPOOL, PAGE, NH, 128)), ("cache_v", (DEPTH, NPOOL, PAGE, NH, 128)),
    ("w_in", (DEPTH, D, NIN)),
    ("w_branch_a", (DEPTH, 512, D)), ("w_branch_b", (DEPTH, 512, D)), ("w_branch_c", (DEPTH, 512, D)),
    ("w_out", (DEPTH, D, D)), ("w_up", (DEPTH, D, 2 * DFF)), ("w_down", (DEPTH, DFF, D)),
]
OUT_SPECS = [
    ("yT_p", (D, SEQ)), ("yT_s", (D, ST)),
    ("p_hgrn", (DEPTH, NH, 128, 128)), ("p_kT", (DEPTH, NH, 128, SEQ)), ("p_v", (DEPTH, SEQ, 512)),
    ("p_gdn", (DEPTH, NH, 128, 128)), ("p_gconv", (DEPTH, GCH, 3)), ("p_fconv", (DEPTH, DFF, 2)),
    ("s_hgrn", (DEPTH, SPC, NH, 128, 128)), ("s_kT", (DEPTH, NH, 128, ST)), ("s_v", (DEPTH, ST, 512)),
    ("s_gdn", (DEPTH, SPC, NH, 128, 128)), ("s_gconv", (DEPTH, SPC, GCH, 3)),
    ("s_fconv", (DEPTH, SPC, DFF, 2)),
]


C_ID, C_ONES, C_BLK, C_ROT, C_M128 = 0, 128, 256, 384, 512
C_MI64, C_MS64, C_MI8, C_MS8, C_RST512, C_RST32, C_IOTA = 640, 1152, 1664, 1696, 1728, 2240, 2272
C_LS64, C_LS8, C_BD32 = 2273, 2785, 2817
NCONST = 2849


def _const_table():
    t = np.zeros((128, NCONST), np.float32)
    p = np.arange(128)
    t[:, C_ID:C_ID + 128] = np.eye(128)
    t[:, C_ONES:C_ONES + 128] = 1.0
    t[:, C_BLK:C_BLK + 128] = (p[:, None] // 64 == p[None, :] // 64)
    rot = np.zeros((128, 128), np.float32)
    for m in range(128):
        if m % 64 < 32:
            rot[m + 32, m] = -1.0
        else:
            rot[m - 32, m] = 1.0
    t[:, C_ROT:C_ROT + 128] = rot
    t[:, C_M128:C_M128 + 128] = (p[None, :] >= p[:, None])
    s64 = np.arange(64)
    mi = (s64[None, :] >= s64[:, None]).astype(np.float32)
    ms = (s64[None, :] > s64[:, None]).astype(np.float32)
    t[:64, C_MI64:C_MI64 + 512] = np.tile(mi, (1, 8))
    t[:64, C_MS64:C_MS64 + 512] = np.tile(ms, (1, 8))
    s8 = np.arange(8)
    t[:8, C_MI8:C_MI8 + 32] = np.tile((s8[None, :] >= s8[:, None]).astype(np.float32), (1, 4))
    t[:8, C_MS8:C_MS8 + 32] = np.tile((s8[None, :] > s8[:, None]).astype(np.float32), (1, 4))
    t[:64, C_LS64:C_LS64 + 512] = np.tile((s64[None, :] < s64[:, None]).astype(np.float32), (1, 8))
    t[:8, C_LS8:C_LS8 + 32] = np.tile((s8[None, :] < s8[:, None]).astype(np.float32), (1, 4))
    i32 = np.arange(32)
    t[:32, C_BD32:C_BD32 + 32] = ((i32[:, None] // 8 == i32[None, :] // 8) & (i32[:, None] % 8 <= i32[None, :] % 8))
    t[:, C_RST512:C_RST512 + 512] = (np.arange(512) % 64 != 0)
    t[:, C_RST32:C_RST32 + 32] = (np.arange(32) % 8 != 0)
    t[:, C_IOTA] = p
    return t


def _rope_table(pos):
    inv = 1.0 / (10000.0 ** (np.arange(0, 64, 2, dtype=np.float32) / 64.0))
    f = inv[(np.arange(128) % 64) % 32].astype(np.float32)
    ang = f[:, None] * np.asarray(pos, np.float32)[None, :]
    return np.stack([np.cos(ang), np.sin(ang)]).astype(np.float32)


P_LNM, P_LNF, P_LBR, P_HN, P_SUBLN, P_GN, P_QKG = 0, 16, 32, 40, 42, 44, 46
P_LAMR, P_GCW, P_FCW, P_GA, P_GDT, NPRM = 50, 562, 658, 790, 798, 806


def _pack_params(inp):
    f = np.float32
    t = np.zeros((128, NPRM), f)
    g = lambda k: np.asarray(inp[k], f)
    t[:, P_LNM:P_LNM + 16] = g("ln_mix").reshape(DEPTH, 8, 128).transpose(2, 0, 1).reshape(128, 16)
    t[:, P_LNF:P_LNF + 16] = g("ln_ffn").reshape(DEPTH, 8, 128).transpose(2, 0, 1).reshape(128, 16)
    t[:, P_LBR:P_LBR + 8] = g("hgrn_lb").reshape(DEPTH, 4, 128).transpose(2, 0, 1).reshape(128, 8)
    t[:, P_HN:P_HN + 2] = g("hgrn_norm").T
    t[:, P_SUBLN:P_SUBLN + 2] = g("diff_subln").T
    t[:, P_GN:P_GN + 2] = g("gdn_norm").T
    qk = g("diff_qk_norm").transpose(2, 0, 1).reshape(64, 4)
    t[:, P_QKG:P_QKG + 4] = np.concatenate([qk, qk], 0)
    t[:, P_LAMR:P_LAMR + 512] = g("diff_lambda").reshape(1, 512)
    t[:, P_GCW:P_GCW + 96] = g("gdn_conv").reshape(DEPTH, 4, 12, 128).transpose(3, 0, 2, 1).reshape(128, 96)
    t[:, P_FCW:P_FCW + 132] = g("ffn_conv").reshape(DEPTH, 3, 22, 128).transpose(3, 0, 2, 1).reshape(128, 132)
    t[:, P_GA:P_GA + 8] = g("gdn_a_log").reshape(1, 8)
    t[:, P_GDT:P_GDT + 8] = g("gdn_dt_bias").reshape(1, 8)
    return t


class Cfg:
    def __init__(self, seq=SEQ, npool=NPOOL, depth=DEPTH, prompt=True, samples=True, upto="all",
                 dbg=()):
        self.seq, self.npool, self.depth = seq, npool, depth
        self.prompt, self.samples, self.upto, self.dbg = prompt, samples, upto, tuple(dbg)


class Grp:
    def __init__(self, kind, n, nseq, c, nch, tile=0):
        self.kind, self.n, self.nseq, self.c, self.nch, self.tile = kind, n, nseq, c, nch, tile
        self.L = n // nseq


class Builder:
    def __init__(self, cfg):
        self.cfg = cfg
        nc = bass.Bass("TRN2", target_bir_lowering=False)
        nc.allow_low_precision("bf16 matmul operands with fp32 PSUM accumulation (problem statement)")
        self.nc = nc
        self.P = Prog(nc)
        self.P.barrier_chans.add("const")
        specs = dict(IN_SPECS)
        specs["xT_p"] = (D, cfg.seq)
        specs["cache_k"] = (DEPTH, cfg.npool, PAGE, NH, 128)
        specs["cache_v"] = (DEPTH, cfg.npool, PAGE, NH, 128)
        self.I = {n: _decl(nc, n, s, F32, "ExternalInput") for n, s in specs.items()}
        self.I["page_table"] = _decl(nc, "page_table", (SPC, NPAGES), I32, "ExternalInput")
        self.I["consts"] = _decl(nc, "consts", (128, NCONST), F32, "ExternalInput")
        self.I["params"] = _decl(nc, "params", (128, NPRM), F32, "ExternalInput")
        self.I["rope_p"] = _decl(nc, "rope_p", (2, 128, cfg.seq), F32, "ExternalInput")
        self.I["rope_s"] = _decl(nc, "rope_s", (2, 128, ST), F32, "ExternalInput")
        ospecs = dict(OUT_SPECS)
        ospecs["yT_p"] = (D, cfg.seq)
        ospecs["p_kT"] = (DEPTH, NH, 128, cfg.seq)
        ospecs["p_v"] = (DEPTH, cfg.seq, 512)
        self.O = {n: _decl(nc, n, s, F32, "ExternalOutput") for n, s in ospecs.items()}
        self.dbg = {}
        self.out_chans = []
        self.wslot = 0
        self.psi = 0

    def sb(self, name, shape, dt=F32):
        return self.P.view(self.st.enter_context(self.nc.sbuf_tensor(name, list(shape), dt))[:], name)

    def psum(self, name, shape, dt=F32):
        return self.P.view(self.st.enter_context(self.nc.psum_tensor(name, list(shape), dt))[:], name)

    def ps(self):
        v = self.PS[self.psi % 4]
        self.psi += 1
        return v

    def dbg_out(self, name, view, shape):
        if name not in self.cfg.dbg:
            return
        t = _decl(self.nc, "dbg_" + name, shape, view.ap.dtype, "ExternalOutput")
        self.P.dma("sp", "dbg_" + name, t, view)
        self.out_chans.append("dbg_" + name)

    def out_dma(self, chan, dst, src, queue="sp"):
        if chan not in self.out_chans:
            self.out_chans.append(chan)
        self.P.dma(queue, chan, dst, src)

    def setup(self):
        P, I = self.P, self.I
        dp = DEPTH
        self.CONST = self.sb("CONST", [128, NCONST])
        P.dma("sp", "const", self.CONST, I["consts"])
        C = self.CONST
        self.ID = C[:, C_ID:C_ID + 128]
        self.ONES = C[:, C_ONES:C_ONES + 128]
        self.BLK = C[:, C_BLK:C_BLK + 128]
        self.ROT = C[:, C_ROT:C_ROT + 128]
        self.M128 = C[:, C_M128:C_M128 + 128]
        self.IDB = self.sb("IDB", [128, 128], BF16)
        P.copy("dve", self.IDB, self.ID)
        self.ONESB = self.sb("ONESB", [128, 128], BF16)
        P.copy("dve", self.ONESB, self.ONES)
        self.PS = [self.psum(f"PS{i}", [128, 512]) for i in range(8)]
        PRM = self.sb("PRM", [128, NPRM])
        P.dma("sp", "const", PRM, I["params"])
        self.LNM = PRM[:, P_LNM:P_LNM + 16].rs("p (l c) -> p l c", l=dp)
        self.LNF = PRM[:, P_LNF:P_LNF + 16].rs("p (l c) -> p l c", l=dp)
        LBR = PRM[:, P_LBR:P_LBR + 8].rs("p (l h) -> p l h", l=dp)
        self.HN = PRM[:, P_HN:P_HN + 2]
        self.SUBLN = PRM[:, P_SUBLN:P_SUBLN + 2]
        self.GN = PRM[:, P_GN:P_GN + 2]
        self.QKG = PRM[:, P_QKG:P_QKG + 4].rs("p (l j) -> p l j", l=dp)
        LAMR = PRM[:, P_LAMR:P_LAMR + 512].rs("p (l j d) -> p l j d", l=dp, j=4)
        self.GCW = PRM[:, P_GCW:P_GCW + 96].rs("p (l c j) -> p l c j", l=dp, c=12)
        self.FCW = PRM[:, P_FCW:P_FCW + 132].rs("p (l c j) -> p l c j", l=dp, c=22)
        self.GDT = PRM[:, P_GDT:P_GDT + 8]
        self.GA = self.sb("GA", [128, dp * 4])
        self.LB = self.sb("LB", [128, dp, 4])
        self.LB1M = self.sb("LB1M", [128, dp, 4])
        E = self.sb("LBE", [128, dp, 4])
        P.act(E, LBR, AF.Exp)
        S = self.sb("LBS", [128, 4])
        P.tt("dve", S, E[:, 0, :], E[:, 1, :], ALU.add)
        P.recip(S, S)
        P.memset("dve", self.LB[:, 0, :], 0.0)
        P.tt("dve", self.LB[:, 1, :], E[:, 1, :], S, ALU.mult)
        P.ts("dve", self.LB1M, self.LB, -1.0, ALU.mult, 1.0, ALU.add)
        self.LAM = self.sb("LAM", [128, dp])
        self.NLAM = self.sb("NLAM", [128, dp])
        PR = self.sb("LAMP", [128, dp, 2, 64])
        SM = self.sb("LAMS", [128, dp, 2])
        for l in range(dp):
            P.tt("dve", PR[:, l, 0, :], LAMR[:, l, 0, :], LAMR[:, l, 1, :], ALU.mult)
            P.tt("dve", PR[:, l, 1, :], LAMR[:, l, 2, :], LAMR[:, l, 3, :], ALU.mult)
            for j in range(2):
                P.reduce("dve", SM[:, l, j:j + 1], PR[:, l, j, :], ALU.add)
        P.act(SM, SM, AF.Exp)
        for l in range(dp):
            lam_init = 0.8 - 0.6 * float(np.exp(-0.3 * l))
            P.tt("dve", self.LAM[:, l:l + 1], SM[:, l, 0:1], SM[:, l, 1:2], ALU.subtract)
            P.ts("dve", self.LAM[:, l:l + 1], self.LAM[:, l:l + 1], lam_init, ALU.add)
        P.ts("dve", self.NLAM, self.LAM, -1.0, ALU.mult)
        P.act(self.GA, PRM[:, P_GA:P_GA + 8], AF.Exp)
        P.ts("dve", self.GA, self.GA, -1.0, ALU.mult)
        self.W32 = [self.sb("W32_0", [128, 4096])] * 2
        self.W16 = [self.sb(f"W16_{i}", [128, 4096], BF16) for i in range(2)]
        self.H = self.sb("H", [128, 8, TT], BF16)
        self.FT = [self.sb(f"FT{i}", [128, TT]) for i in range(10)]
        self.BT = [self.sb(f"BT{i}", [128, TT], BF16) for i in range(6)]
        scr = self.st.enter_context(self.nc.sbuf_tensor("SCR", [128, 3 * 4 * TT], F32))[:]
        stoks = [P.tok(f"SCR{i}") for i in range(3)]
        self.F4 = [View(scr[:, i * 4 * TT:(i + 1) * 4 * TT].rearrange("p (a b) -> p a b", a=4), [stoks[i]])
                   for i in range(3)]
        self.A16 = View(scr.bitcast(BF16)[:, 0:22 * TT].rearrange("p (a b) -> p a b", a=22), stoks)
        self.HST = View(scr[:, 2 * 4 * TT:3 * 4 * TT], [stoks[2]])
        self.OABC = [self.sb(f"O{x}", [128, 4, TT], BF16) for x in "abc"]

    def wpanel(self, wap, r0, nk, c0, ncols):
        P = self.P
        slot = self.wslot
        self.wslot ^= 1
        w32, w16 = self.W32[slot], self.W16[slot]
        v32 = w32.with_ap(w32.ap[:, 0:nk * ncols].rearrange("p (k n) -> p k n", k=nk))
        v16 = w16.with_ap(w16.ap[:, 0:nk * ncols].rearrange("p (k n) -> p k n", k=nk))
        P.dma("sp", "w0", v32, wap[r0:r0 + nk * 128, c0:c0 + ncols].rearrange("(k p) n -> p k n", p=128))
        P.copy("pool", v16, v32)
        return v16

    def gemm_fm(self, w16, nk, rhs, n, consume, nm=None):
        ncols = w16.ap.shape[2]
        for m in range(nm if nm is not None else ncols // 128):
            pv = self.ps()[:, 0:n]
            self.P.mm(pv, [(w16[:, k, m * 128:(m + 1) * 128], rhs(k)) for k in range(nk)])
            consume(m, pv)

    def gemm_tm(self, w16, nk, lhs, g, consume):
        ncols = w16.ap.shape[2]
        for ch in range(g.nch):
            pv = self.ps()[0:g.c, 0:ncols]
            self.P.mm(pv, [(lhs(k, ch), w16[:, k, :]) for k in range(nk)])
            consume(ch, pv)

    def rmsnorm_fm(self, X, gain, Hout, n, nk=8, dim=D):
        P = self.P
        pv = self.ps()[:, 0:n]
        for k in range(nk):
            sq = self.FT[8 + k % 2][:, 0:n]
            P.act(sq, X[:, k, 0:n], AF.Square)
            P.mm(pv, [(self.ONES, sq)], start=(k == 0), stop=(k == nk - 1))
        rs = self.FT[7][:, 0:n]
        P.ts("dve", rs, pv, 1.0 / dim, ALU.mult, EPS, ALU.add)
        P.rsqrt(rs, rs)
        for k in range(nk):
            P.stt("dve", Hout[:, k, 0:n], X[:, k, 0:n], gain[:, k:k + 1], rs, ALU.mult, ALU.mult)


    def alloc_mixers(self):
        self.VT16 = self.sb("VT16", [64, 8, 512], BF16)
        self.KHT = self.sb("KHT", [64, 8, 128], BF16)
        dp = self.cfg.depth
        self.HS32 = self.sb("HS32", [128, dp, 4, 128])
        self.HS16 = self.sb("HS16", [128, dp, 4, 128], BF16)
        self.GS32 = self.sb("GS32", [128, dp, 4, 128])
        self.GS16 = self.sb("GS16", [128, dp, 4, 128], BF16)
        for t in (self.HS32, self.HS16, self.GS32, self.GS16):
            self.P.memset("pool", t, 0.0)
        self.KH16 = self.sb("KH16", [128, SEQ], BF16)
        self.VH16 = self.sb("VH16", [128, SEQ // 128, 128], BF16)
        self.ROPE = self.sb("ROPE", [128, 2, TT])
        self.VST = [self.sb(f"VST{i}", [128, 512]) for i in range(2)]
        self.KST = [self.sb(f"KST{i}", [128, TT]) for i in range(2)]
        self.Q16 = self.sb("Q16", [128, TT], BF16)
        self.vsti = 0
        self.TAILG = self.sb("TAILG", [128, DEPTH, 12, 3])
        self.TAILF = self.sb("TAILF", [128, DEPTH, 22, 2])
        self.P.memset("pool", self.TAILG, 0.0)
        self.P.memset("pool", self.TAILF, 0.0)
        self.CB = self.sb("CB", [128, 3 + TT])
        self.BG = self.sb("BG", [64, 8, 8])
        self.BETA = self.sb("BETA", [64, 8, 4])
        self.GTT = self.sb("GTT", [64, 8, 4])
        self.GT = self.sb("GT", [64, 8, 4])
        self.EGT = self.sb("EGT", [64, 8, 4])
        self.NBEG = self.sb("NBEG", [64, 8, 4])
        self.EH = self.sb("EH", [64, 8])
        self.KT16 = self.sb("KT16", [64, 8, 128], BF16)
        self.KHAT = self.sb("KHAT", [64, 8, 128], BF16)
        self.BV = self.sb("BV", [64, 8, 128])
        self.QKM = self.sb("QKM", [64, TT], BF16)
        self.RHS = [self.sb(f"RHS{i}", [64, 128]) for i in range(2)]
        self.U16 = [self.sb(f"U16_{i}", [64, 128], BF16) for i in range(2)]
        self.D_PK = [[View(self.O["p_kT"][l, h], [self.P.tok("pk")]) for h in range(4)] for l in range(DEPTH)]
        self.D_PV = [View(self.O["p_v"][l], [self.P.tok("pv")]) for l in range(DEPTH)]

    def hgrn(self, g, l, S32, S16, pre=None, post=None):
        P = self.P
        n, c, nch = g.n, g.c, g.nch
        H, W = self.H, self.I["w_in"][l]
        Q, SG, OG = self.F4
        VT, OA = self.VT16, self.OABC[0]
        C = self.CONST
        rst = C[:, C_RST512:C_RST512 + n] if g.kind == "p" else C[:, C_RST32:C_RST32 + n]
        mi0 = C_MI64 if g.kind == "p" else C_MI8
        maskT = C[0:c, mi0:mi0 + n]

        def Hn(k):
            return H[:, k, 0:n]

        def Hc(k, ch):
            return H[:, k, ch * c:(ch + 1) * c]

        w = self.wpanel(W, 0, 8, O_HI, 512)
        self.gemm_tm(w, 8, Hc, g, lambda ch, pv: P.copy("act", VT[0:c, ch, :], pv))
        for off, dst, fn in ((O_HQ, Q, AF.Silu), (O_HF, SG, AF.Sigmoid), (O_HOG, OG, AF.Silu)):
            w = self.wpanel(W, 0, 8, off, 512)
            self.gemm_fm(w, 8, Hn, n, lambda m, pv, dst=dst, fn=fn: P.act(dst[:, m, 0:n], pv, fn))
        cm = c // 2 - 1
        for h in range(4):
            fg, kk, lf, G, d1, EQ, e2 = [self.FT[i][:, 0:n] for i in range(7)]
            QG, QC, KC, KH = [self.BT[i][:, 0:n] for i in range(4)]
            AT = self.BT[4][0:c, 0:n]
            P.ts("dve", fg, SG[:, h, 0:n], self.LB1M[:, l, h:h + 1], ALU.mult, self.LB[:, l, h:h + 1], ALU.add)
            P.ts("dve", kk, fg, -1.0, ALU.mult, 1.0, ALU.add)
            P.act(lf, fg, AF.Ln)
            P.scan(G, rst, lf)
            G3 = G.r3(c)
            P.act(EQ, G, AF.Exp)
            P.tt("dve", QG, Q[:, h, 0:n], EQ, ALU.mult)
            P.tt("dve", d1.r3(c), G3, G3[:, :, cm:cm + 1].bc([128, nch, c]), ALU.subtract)
            P.act(e2, d1, AF.Exp)
            P.tt("dve", QC, Q[:, h, 0:n], e2, ALU.mult)
            P.act(e2, d1, AF.Exp, scale=-1.0)
            P.tt("dve", KC, kk, e2, ALU.mult)
            P.tt("dve", d1.r3(c), G3, G3[:, :, c - 1:c].bc([128, nch, c]), ALU.subtract)
            P.act(e2, d1, AF.Exp, scale=-1.0)
            P.tt("dve", KH, kk, e2, ALU.mult)
            PA = self.PS[4]
            P.mm_multi(PA, [(PA.ap[0:c, ch * c:(ch + 1) * c],
                             [(KC.ap[:, ch * c:(ch + 1) * c], QC.ap[:, ch * c:(ch + 1) * c])])
                            for ch in range(nch)], reads=[KC, QC])
            P.tt("dve", AT, PA[0:c, 0:n], maskT, ALU.mult)
            PTB = self.PS[7].bitcast(BF16)
            P.transpose_multi(PTB, [(PTB.ap[0:c, ch * 128:(ch + 1) * 128], KH.ap[:, ch * c:(ch + 1) * c],
                                     self.IDB.ap) for ch in range(nch)], reads=[KH, self.IDB])
            P.copy("act", self.KHT[0:c, 0:nch, :], PTB[0:c, 0:nch * 128].r3(128))
            PO = self.PS[5]
            if pre:
                pre(h)
            for ch in range(nch):
                cs = slice(ch * c, (ch + 1) * c)
                vch = VT[0:c, ch, h * 128:(h + 1) * 128]
                P.mm(PO[:, cs], [(S16(ch, h), QG[:, cs]), (vch, AT[:, cs])])
                su = self.PS[6][:, (ch % 4) * 128:(ch % 4 + 1) * 128]
                P.mm(su, [(self.KHT[0:c, ch, :], vch)])
                P.stt("dve", S32(ch, h), S32(ch, h), EQ[:, ch * c + c - 1:ch * c + c], su, ALU.mult, ALU.add)
                P.copy("act", S16(ch, h), S32(ch, h))
            if post:
                post(h)
            sq, rs, t1 = self.FT[8][:, 0:n], self.FT[7][:, 0:n], self.FT[9][:, 0:n]
            P.act(sq, PO[:, 0:n], AF.Square)
            pss = self.ps()[:, 0:n]
            P.mm(pss, [(self.ONES, sq)])
            P.ts("dve", rs, pss, 1.0 / 128, ALU.mult, EPS, ALU.add)
            P.rsqrt(rs, rs)
            P.stt("dve", t1, PO[:, 0:n], self.HN[:, l:l + 1], rs, ALU.mult, ALU.mult)
            P.tt("dve", OA[:, h, 0:n], t1, OG[:, h, 0:n], ALU.mult)

    def qknorm_rope(self, x, gain, out):
        P = self.P
        n = x.ap.shape[1]
        sq, rs, xn, t1 = [self.FT[i][:, 0:n] for i in (8, 7, 6, 5)]
        P.act(sq, x, AF.Square)
        pss = self.ps()[:, 0:n]
        P.mm(pss, [(self.BLK, sq)])
        P.ts("dve", rs, pss, 1.0 / 64, ALU.mult, EPS, ALU.add)
        P.rsqrt(rs, rs)
        P.stt("dve", xn, x, gain, rs, ALU.mult, ALU.mult)
        pr = self.ps()[:, 0:n]
        P.mm(pr, [(self.ROT, xn)])
        P.tt("dve", t1, xn, self.ROPE[:, 0, 0:n], ALU.mult)
        P.tt("dve", xn, pr, self.ROPE[:, 1, 0:n], ALU.mult)
        P.tt("dve", out, t1, xn, ALU.add)

    def attn_prompt(self, g, l):
        P = self.P
        n, t = g.n, g.tile
        H, W = self.H, self.I["w_in"][l]
        DQ, DK = self.F4[0], self.F4[1]
        OB = self.OABC[1]
        tok0 = t * TT

        def Hn(k):
            return H[:, k, 0:n]

        P.dma("pool", "rope", self.ROPE[:, :, 0:n], self.I["rope_p"][:, :, tok0:tok0 + n].rearrange("a p t -> p a t"))
        for off, dst in ((O_DQ, DQ), (O_DK, DK)):
            w = self.wpanel(W, 0, 8, off, 512)
            self.gemm_fm(w, 8, Hn, n, lambda m, pv, dst=dst: P.copy("act", dst[:, m, 0:n], pv))
        w = self.wpanel(W, 0, 8, O_DV, 512)
        for b in range(n // 128):
            pv = self.ps()
            P.mm(pv, [(H[:, k, b * 128:(b + 1) * 128], w[:, k, :]) for k in range(8)])
            vs = self.VST[self.vsti % 2]
            self.vsti += 1
            P.copy("act", vs, pv)
            self.out_dma("pv", self.D_PV[l][tok0 + b * 128:tok0 + (b + 1) * 128, :], vs, queue="pool")
        nkb = (tok0 + n) // 128
        for h in range(4):
            ks = self.KST[h % 2][:, 0:n]
            self.qknorm_rope(DK[:, h, 0:n], self.QKG[:, l, 1:2], ks)
            self.out_dma(f"pk{h}", self.D_PK[l][h][:, tok0:tok0 + n], ks, queue="pool")
            self.qknorm_rope(DQ[:, h, 0:n], self.QKG[:, l, 0:1], self.Q16[:, 0:n])
            for p0 in range(0, tok0 + n, 2048):
                p1 = min(tok0 + n, p0 + 2048)
                st = self.HST[:, 0:p1 - p0]
                P.dma("pool", "hist", st, self.D_PK[l][h][:, p0:p1])
                P.copy("pool", self.KH16[:, p0:p1], st)
                nb = (p1 - p0) // 128
                st3 = self.HST[:, 0:nb * 128].rs("p (b d) -> p b d", d=128)
                P.dma("pool", "hist", st3, self.D_PV[l][p0:p1, h * 128:(h + 1) * 128].rs("(b p) d -> p b d", p=128))
                P.copy("pool", self.VH16[:, p0 // 128:p0 // 128 + nb, :], st3)
            acc = [self.PS[4], self.PS[5], self.PS[6], self.PS[7]]
            ei = 0
            for half in range(2):
                hs = slice(half * 64, half * 64 + 64)
                O_, Z_ = acc[2 * half], acc[2 * half + 1]
                for kb in range(nkb):
                    r = kb - tok0 // 128
                    qlo = max(0, r * 128)
                    sp = self.ps()[:, 0:n - qlo]
                    P.mm(sp, [(self.KH16[hs, kb * 128:(kb + 1) * 128], self.Q16[hs, qlo:n])])
                    e = self.BT[ei % 6][:, 0:n - qlo]
                    ei += 1
                    P.act(e, sp, AF.Exp, scale=0.125)
                    if r >= 0:
                        P.tt("dve", e[:, 0:128], e[:, 0:128], self.M128, ALU.mult)
                    P.mm(O_[:, qlo:n], [(self.VH16[:, kb, :], e)], start=(kb == 0), stop=(kb == nkb - 1))
                    P.mm(Z_[:, qlo:n], [(self.ONESB, e)], start=(kb == 0), stop=(kb == nkb - 1))
            r1, r2, o1, o2 = [self.FT[i][:, 0:n] for i in range(4)]
            P.recip(r1, acc[1][:, 0:n])
            P.recip(r2, acc[3][:, 0:n])
            P.tt("dve", o1, acc[0][:, 0:n], r1, ALU.mult)
            P.tt("dve", o2, acc[2][:, 0:n], r2, ALU.mult)
            P.stt("dve", o1, o2, self.NLAM[:, l:l + 1], o1, ALU.mult, ALU.add)
            self.subln(o1, l, OB[:, h, 0:n])

    def subln(self, od, l, out):
        P = self.P
        n = od.ap.shape[1]
        sq, rs, t1 = self.FT[8][:, 0:n], self.FT[7][:, 0:n], self.FT[9][:, 0:n]
        P.act(sq, od, AF.Square)
        pss = self.ps()[:, 0:n]
        P.mm(pss, [(self.ONES, sq)])
        P.ts("dve", rs, pss, 1.0 / 128, ALU.mult, EPS, ALU.add)
        P.rsqrt(rs, rs)
        P.stt("dve", t1, od, self.SUBLN[:, l:l + 1], rs, ALU.mult, ALU.mult)
        lam_init = 0.8 - 0.6 * float(np.exp(-0.3 * l))
        P.ts("dve", out, t1, 1.0 - lam_init, ALU.mult)

    def conv_fm(self, g, pv, wcol, tail, ntap, dst, hist_src=None, tail_dst=None):
        P = self.P
        L, ns, hl = g.L, g.nseq, ntap - 1
        cb = self.CB[:, 0:ns * (hl + L)].rs("p (s j) -> p s j", s=ns)
        if g.kind == "p":
            P.copy("act", cb[:, 0, 0:hl], tail)
        else:
            P.dma("pool", "chist", cb[:, :, 0:hl], hist_src)
        P.copy("act", cb[:, :, hl:hl + L], pv.rs("p (s j) -> p s j", s=ns))
        d3 = dst.rs("p (s j) -> p s j", s=ns)
        P.ts("dve", d3, cb[:, :, 0:L], wcol(0), ALU.mult)
        for j in range(1, ntap):
            P.stt("dve", d3, cb[:, :, j:j + L], wcol(j), d3, ALU.mult, ALU.add)
        if g.kind == "p":
            P.copy("act", tail, cb[:, 0, L:L + hl])
        else:
            self.out_dma(tail_dst[0], tail_dst[1], cb[:, :, L:L + hl], queue="pool")

    def gdn(self, g, l, S32, S16, hist=None, tails=None, pre=None, post=None):
        P = self.P
        n, c, nch = g.n, g.c, g.nch
        H, W = self.H, self.I["w_in"][l]
        C = self.CONST
        isp = g.kind == "p"
        MI = C[0:c, (C_MI64 if isp else C_MI8):][:, 0:n]
        MS = C[0:c, (C_MS64 if isp else C_MS8):][:, 0:n]
        LS = C[0:c, (C_LS64 if isp else C_LS8):][:, 0:n]
        IDb = C[0:c, C_ID:C_ID + c].rs("p (a t) -> p a t", a=1).bc([c, nch, c])
        OC = self.OABC[2]
        QF, KF, VF = self.F4

        def Hn(k):
            return H[:, k, 0:n]

        def Hc(k, ch):
            return H[:, k, ch * c:(ch + 1) * c]

        def cols(v, ch):
            return v.ap[:, ch * c:(ch + 1) * c]

        for j3, (off, dst) in enumerate(((O_GQ, QF), (O_GK, KF), (O_GV, VF))):
            w = self.wpanel(W, 0, 8, off, 512)

            def cons(m, pv, j3=j3, dst=dst):
                kc = j3 * 4 + m
                pre = self.FT[9][:, 0:n]
                self.conv_fm(g, pv, lambda j: self.GCW[:, l, kc, j:j + 1], self.TAILG[:, l, kc, :], 4, pre,
                             hist_src=None if isp else hist(kc), tail_dst=None if isp else tails(kc))
                P.act(dst[:, m, 0:n], pre, AF.Silu)
            self.gemm_fm(w, 8, Hn, n, cons)
        w = self.wpanel(W, 0, 8, O_GZ, 512)
        self.gemm_fm(w, 8, Hn, n, lambda m, pv: P.act(OC[:, m, 0:n], pv, AF.Silu))
        w = self.wpanel(W, 0, 8, O_GB, 8)
        BG, BETA, GTT, GT, EGT, NBEG = [x[0:c, 0:nch, :] for x in
                                        (self.BG, self.BETA, self.GTT, self.GT, self.EGT, self.NBEG)]
        self.gemm_tm(w, 8, Hc, g, lambda ch, pv: P.copy("act", BG[:, ch, :], pv))
        P.act(BETA, BG[:, :, 0:4], AF.Sigmoid)
        P.tt("dve", GTT, BG[:, :, 4:8], self.GDT[0:c, l * 4:l * 4 + 4].rs("p (a h) -> p a h", a=1).bc([c, nch, 4]), ALU.add)
        P.act(GTT, GTT, AF.Exp)
        P.act(GTT, GTT, AF.Ln, bias=1.0)
        P.tt("dve", GTT, GTT, self.GA[0:c, l * 4:l * 4 + 4].rs("p (a h) -> p a h", a=1).bc([c, nch, 4]), ALU.mult)
        pg = self.ps()[0:c, 0:nch * 4]
        P.mm(pg, [(MI[:, 0:c], GTT.rs("p a h -> p (a h)"))])
        P.copy("act", GT.rs("p a h -> p (a h)"), pg)
        P.act(EGT, GT, AF.Exp)
        P.stt("dve", NBEG, BETA, -1.0, EGT, ALU.mult, ALU.mult)
        nlev = int(round(np.log2(c)))
        EH2 = self.EH[0:c, 0:nch]
        EH3 = EH2.rs("p (a b) -> p a b", b=1)
        for h in range(4):
            qn, kn = self.BT[0][:, 0:n], self.BT[1][:, 0:n]
            for src, dst, sc in ((QF, qn, 128 ** -0.5), (KF, kn, 1.0)):
                sq, rs = self.FT[8][:, 0:n], self.FT[7][:, 0:n]
                P.act(sq, src[:, h, 0:n], AF.Square)
                pss = self.ps()[:, 0:n]
                P.mm(pss, [(self.ONES, sq)])
                P.ts("dve", rs, pss, EPS, ALU.add)
                P.rsqrt(rs, rs)
                P.stt("dve", dst, src[:, h, 0:n], sc, rs, ALU.mult, ALU.mult)
            v16 = self.BT[2][:, 0:n]
            P.copy("act", v16, VF[:, h, 0:n])
            PTB = self.ps().bitcast(BF16)
            P.transpose_multi(PTB, [(PTB.ap[0:c, ch * 128:(ch + 1) * 128], cols(kn, ch), self.IDB.ap)
                                    for ch in range(nch)], reads=[kn, self.IDB])
            KT = self.KT16[0:c, 0:nch, :]
            P.copy("act", KT, PTB[0:c, 0:nch * 128].r3(128))
            PTB = self.ps().bitcast(BF16)
            P.transpose_multi(PTB, [(PTB.ap[0:c, ch * 128:(ch + 1) * 128], cols(v16, ch), self.IDB.ap)
                                    for ch in range(nch)], reads=[v16, self.IDB])
            BV = self.BV[0:c, 0:nch, :]
            P.tt("dve", BV, PTB[0:c, 0:nch * 128].r3(128), BETA[:, :, h:h + 1].bc([c, nch, 128]), ALU.mult)
            GU, raw, dT, dL, Y, Yt = [self.FT[i][0:c, 0:n] for i in (1, 2, 3, 4, 5, 6)]
            tmp, Wm = GU, dL
            P.tt("dve", GU.r3(c), GTT[:, :, h:h + 1].bc([c, nch, c]), MI.r3(c), ALU.mult)
            R = self.PS[4][:, 0:n]
            P.mm(R, [(self.ONES[0:c, :], GU)])
            EG = self.FT[0][:, 0:n]
            P.act(EG, R, AF.Exp)
            qg = self.BT[3][:, 0:n]
            P.tt("dve", qg, qn, EG, ALU.mult)
            Rc = R[0:c, :].r3(c)
            P.tt("dve", raw.r3(c), Rc, GT[:, :, h:h + 1].bc([c, nch, c]), ALU.subtract)
            P.tt("dve", EH3, Rc[:, :, c - 1:c], GT[:, :, h:h + 1], ALU.subtract)
            P.act(EH2, EH2, AF.Exp)
            P.tt("dve", self.KHAT[0:c, 0:nch, :], KT, EH3.bc([c, nch, 128]), ALU.mult)
            P.ts("dve", dT, raw, 0.0, ALU.min)
            P.act(dT, dT, AF.Exp)
            P.tt("dve", dT, dT, MI, ALU.mult)
            P.ts("dve", dL, raw, -1.0, ALU.mult, 0.0, ALU.min)
            P.act(dL, dL, AF.Exp)
            P.tt("dve", dL, dL, LS, ALU.mult)
            PKK, PQK = self.PS[5], self.PS[6]
            P.mm_multi(PKK, [(PKK.ap[0:c, ch * c:(ch + 1) * c], [(cols(kn, ch), cols(kn, ch))]) for ch in range(nch)], reads=[kn])
            P.mm_multi(PQK, [(PQK.ap[0:c, ch * c:(ch + 1) * c], [(cols(kn, ch), cols(qn, ch))]) for ch in range(nch)], reads=[kn, qn])
            QKM = self.QKM[0:c, 0:n]
            P.tt("dve", QKM, PQK[0:c, 0:n], dT, ALU.mult)
            P.tt("dve", tmp.r3(c), BETA[:, :, h:h + 1].bc([c, nch, c]), IDb, ALU.mult)
            PB = self.PS[7][0:c, 0:n]
            P.mm(PB, [(self.ONES[0:c, 0:c], tmp)])
            P.tt("dve", Y, PKK[0:c, 0:n], dT, ALU.mult)
            P.tt("dve", Y, Y, MS, ALU.mult)
            P.stt("dve", Y, PB, -1.0, Y, ALU.mult, ALU.mult)
            P.tt("dve", Yt, PKK[0:c, 0:n], dL, ALU.mult)
            P.stt("dve", Yt.r3(c), BETA[:, :, h:h + 1].bc([c, nch, c]), -1.0, Yt.r3(c), ALU.mult, ALU.mult)
            P.tt("dve", Wm.r3(c), Y.r3(c), IDb, ALU.add)
            for lev in range(1, nlev):
                last = lev == nlev - 1
                pb = self.ps()
                P.mm_multi(pb, [(pb.ap[0:c, ch * c:(ch + 1) * c], [(cols(Y, ch), cols(Yt, ch))]) for ch in range(nch)], reads=[Y, Yt])
                if not last:
                    pa = self.ps()
                    P.mm_multi(pa, [(pa.ap[0:c, ch * c:(ch + 1) * c], [(cols(Yt, ch), cols(Y, ch))]) for ch in range(nch)], reads=[Y, Yt])
                    P.copy("act", Y, pa[0:c, 0:n])
                P.copy("act", Yt, pb[0:c, 0:n])
                pw = self.ps()
                P.mm_multi(pw, [(pw.ap[0:c, ch * c:(ch + 1) * c], [(cols(Yt, ch), cols(Wm, ch))]) for ch in range(nch)], reads=[Yt, Wm])
                P.tt("dve", Wm, Wm, pw[0:c, 0:n], ALU.add)
            PO = self.PS[4]
            if pre:
                pre(h)
            for ch in range(nch):
                cs = slice(ch * c, (ch + 1) * c)
                pk = self.ps()[0:c, 0:128]
                P.mm(pk, [(kn[:, cs], S16(ch, h))])
                rh = self.RHS[ch % 2][0:c, :]
                P.stt("dve", rh, pk, NBEG[:, ch, h:h + 1], BV[:, ch, :], ALU.mult, ALU.add)
                pu = self.ps()[0:c, 0:128]
                P.mm(pu, [(Wm[:, cs], rh)])
                u16 = self.U16[ch % 2][0:c, :]
                P.copy("act", u16, pu)
                P.mm(PO[:, cs], [(S16(ch, h), qg[:, cs]), (u16, QKM[:, cs])])
                su = self.ps()[:, 0:128]
                P.mm(su, [(self.KHAT[0:c, ch, :], u16)])
                P.stt("dve", S32(ch, h), S32(ch, h), EG[:, ch * c + c - 1:ch * c + c], su, ALU.mult, ALU.add)
                P.copy("act", S16(ch, h), S32(ch, h))
            if post:
                post(h)
            sq, rs, t1 = self.FT[8][:, 0:n], self.FT[7][:, 0:n], self.FT[9][:, 0:n]
            P.act(sq, PO[:, 0:n], AF.Square)
            pss = self.ps()[:, 0:n]
            P.mm(pss, [(self.ONES, sq)])
            P.ts("dve", rs, pss, 1.0 / 128, ALU.mult, EPS, ALU.add)
            P.rsqrt(rs, rs)
            P.stt("dve", t1, PO[:, 0:n], self.GN[:, l:l + 1], rs, ALU.mult, ALU.mult)
            P.tt("dve", OC[:, h, 0:n], t1, OC[:, h, 0:n], ALU.mult)

    def merge_out(self, g, l, XR):
        P = self.P
        n = g.n
        H, I = self.H, self.I
        MIX = [self.F4[0], self.F4[1]]

        def Hn(k):
            return H[:, k, 0:n]

        for c0 in (0, 512):
            for bi, (wname, goff) in enumerate((("w_branch_a", O_GTA), ("w_branch_b", O_GTB), ("w_branch_c", O_GTC))):
                O_x = self.OABC[bi]
                wg = self.wpanel(I["w_in"][l], 0, 8, goff + c0, 512)
                wb = self.wpanel(I[wname][l], 0, 4, c0, 512)
                for m in range(4):
                    pg = self.ps()[:, 0:n]
                    P.mm(pg, [(wg[:, k, m * 128:(m + 1) * 128], Hn(k)) for k in range(8)])
                    sg = self.FT[m % 2][:, 0:n]
                    P.act(sg, pg, AF.Sigmoid)
                    pb = self.ps()[:, 0:n]
                    P.mm(pb, [(wb[:, k, m * 128:(m + 1) * 128], O_x[:, k, 0:n]) for k in range(4)])
                    dst = MIX[c0 // 512][:, m, 0:n]
                    if bi == 0:
                        P.tt("dve", dst, pb, sg, ALU.mult)
                    else:
                        tmp = self.FT[2 + m % 2][:, 0:n]
                        P.tt("dve", tmp, pb, sg, ALU.mult)
                        P.tt("pool", dst, dst, tmp, ALU.add)
        for k in range(8):
            P.copy("act", H[:, k, 0:n], MIX[k // 4][:, k % 4, 0:n])
        for c0 in (0, 512):
            w = self.wpanel(I["w_out"][l], 0, 8, c0, 512)
            self.gemm_fm(w, 8, Hn, n, lambda m, pv, c0=c0: P.tt("dve", XR[:, c0 // 128 + m, 0:n], XR[:, c0 // 128 + m, 0:n], pv, ALU.add))

    def ffn(self, g, l, XR, hist=None, tails=None):
        P = self.P
        n = g.n
        H, I = self.H, self.I
        isp = g.kind == "p"
        A16 = self.A16

        def Hn(k):
            return H[:, k, 0:n]

        self.rmsnorm_fm(XR, self.LNF[:, l, :], H, n)
        for c0 in range(0, DFF, 512):
            wd = min(512, DFF - c0)
            wg = self.wpanel(I["w_up"][l], 0, 8, c0, wd)
            wu = self.wpanel(I["w_up"][l], 0, 8, DFF + c0, wd)
            for m in range(wd // 128):
                j = c0 // 128 + m
                pg = self.ps()[:, 0:n]
                P.mm(pg, [(wg[:, k, m * 128:(m + 1) * 128], Hn(k)) for k in range(8)])
                pre = self.FT[9][:, 0:n]
                self.conv_fm(g, pg, lambda t, j=j: self.FCW[:, l, j, t:t + 1], self.TAILF[:, l, j, :], 3, pre,
                             hist_src=None if isp else hist(j), tail_dst=None if isp else tails(j))
                sg = self.FT[m % 2][:, 0:n]
                P.act(sg, pre, AF.Silu)
                pu = self.ps()[:, 0:n]
                P.mm(pu, [(wu[:, k, m * 128:(m + 1) * 128], Hn(k)) for k in range(8)])
                P.tt("dve", A16[:, j, 0:n], pu, sg, ALU.mult)
        for m in range(8):
            w = self.wpanel(I["w_down"][l], 0, 22, m * 128, 128)
            pv = self.ps()[:, 0:n]
            P.mm(pv, [(w[:, k, :], A16[:, k, 0:n]) for k in range(22)])
            P.tt("dve", XR[:, m, 0:n], XR[:, m, 0:n], pv, ALU.add)

    def alloc_samples(self):
        P = self.P
        self.XS = self.sb("XS", [128, 8, ST])
        self.SST32 = self.sb("SST32", [128, SPC, 128])
        self.SST16 = self.sb("SST16", [128, SPC, 128], BF16)
        self.QS16 = self.sb("QS16", [128, 4, ST], BF16)
        self.ZB = self.sb("ZB", [128, 256], BF16)
        P.memset("dve", self.ZB, 0.0)
        self.KZ16 = self.sb("KZ16", [128, 2, 4, ST], BF16)
        P.memset("dve", self.KZ16, 0.0)
        self.VA16 = self.sb("VA16", [ST, 512], BF16)
        self.KN16 = self.sb("KN16", [128, 4, ST], BF16)
        npt = SPC * NPAGES
        self.IDX = self.sb("IDX", [128, DEPTH, npt], I32)
        pti = self.FT[2][:, 0:npt].bitcast(I32)
        ptf, idf = self.FT[0][:, 0:npt], self.FT[1][:, 0:npt]
        P.dma("sp", "pt", pti, self.I["page_table"].rearrange("s j -> (s j)").partition_broadcast(128))
        P.copy("dve", ptf, pti)
        P.ts("dve", idf, ptf, float(PAGE), ALU.mult, self.CONST[:, C_IOTA:C_IOTA + 1], ALU.add)
        for l in range(DEPTH):
            if l:
                P.ts("dve", idf, idf, float(self.cfg.npool * PAGE), ALU.add)
            P.copy("dve", self.IDX[:, l, :], idf)
        self.dbg_out("idx", self.IDX, [128, DEPTH, npt])

    def gather(self, chan, dst, rows, idx_col, nrows):
        iap, dap = idx_col.ap, dst.ap

        def fn(eng, sem):
            return eng.indirect_dma_start(
                out=dap, out_offset=None, in_=rows,
                in_offset=bass.IndirectOffsetOnAxis(ap=iap, axis=0)).then_inc(sem, 16)
        return self.P.op("pool", fn, reads=[idx_col], writes=[dst], chan=chan, ndma=1)

    def attn_sample(self, g, l):
        P = self.P
        n = g.n
        H, W = self.H, self.I["w_in"][l]
        DQ, DK = self.F4[0], self.F4[1]
        OB = self.OABC[1]
        cfg = self.cfg

        def Hn(k):
            return H[:, k, 0:n]

        P.dma("pool", "rope", self.ROPE[:, :, 0:n], self.I["rope_s"].rearrange("a p t -> p a t"))
        for off, dst in ((O_DQ, DQ), (O_DK, DK)):
            w = self.wpanel(W, 0, 8, off, 512)
            self.gemm_fm(w, 8, Hn, n, lambda m, pv, dst=dst: P.copy("act", dst[:, m, 0:n], pv))
        w = self.wpanel(W, 0, 8, O_DV, 512)

        pv = self.ps()[0:n, :]
        P.mm(pv, [(H[:, k, 0:n], w[:, k, :]) for k in range(8)])
        vs = self.VST[0][0:n, :]
        P.copy("act", vs, pv)
        P.copy("act", self.VA16, pv)
        self.out_dma("sv", self.O["s_v"][l], vs, queue="pool")
        for h in range(4):
            ks = self.KST[h % 2][:, 0:n]
            self.qknorm_rope(DK[:, h, 0:n], self.QKG[:, l, 1:2], ks)
            self.out_dma(f"sk{h % 2}", self.O["s_kT"][l, h], ks, queue="pool")
            P.copy("act", self.KZ16[0:64, 0, h, :], ks[0:64, :])
            P.copy("act", self.KZ16[64:128, 1, h, :], ks[64:128, :])
            self.qknorm_rope(DQ[:, h, 0:n], self.QKG[:, l, 0:1], self.QS16[:, h, :])
        step = int(os.environ.get("ATT_STEP", 6))
        if step <= 1:
            return
        krows = self.I["cache_k"].rearrange("l a s h d -> (l a s) (h d)")
        vrows = self.I["cache_v"].rearrange("l a s h d -> (l a s) (h d)")
        nrows = DEPTH * cfg.npool * PAGE
        PO, PZ = self.PS[4], self.PS[5]

        def col(s_, h_, half):
            return ((h_ * 2 + half) * SPC + s_) * SL

        for acc in (PO, PZ):
            P.mm(acc[:, 0:256], [(self.ZB[:, 0:128], self.ZB)], start=True, stop=False)
        ei = 0
        dbg_ns = int(os.environ.get("ATT_SEQS", SPC))
        dbg_np = int(os.environ.get("ATT_PAGES", NPAGES))
        for s_ in range(dbg_ns):
            qs = slice(s_ * SL, (s_ + 1) * SL)
            for j in range(dbg_np):
                kp, vp = self.KST[j % 2], self.VST[j % 2]
                ic = self.IDX[:, l, s_ * NPAGES + j:s_ * NPAGES + j + 1]
                self.gather(f"gk{j % 2}", kp, krows, ic, nrows)
                self.gather(f"gv{j % 2}", vp, vrows, ic, nrows)
                pt = self.ps()
                P.transpose_multi(pt, [(pt.ap[:, hh * 128:(hh + 1) * 128], kp.ap[:, hh * 128:(hh + 1) * 128], self.ID.ap)
                                       for hh in range(4)], reads=[kp, self.ID])
                ktp = self.KH16[:, (j % 2) * 512:(j % 2) * 512 + 512]
                P.copy("act", ktp, pt)
                vp16 = self.VH16[:, (j % 2) * 4:(j % 2) * 4 + 4, :].rs("p a d -> p (a d)")
                P.copy("dve", vp16, vp)
                if step <= 2:
                    continue
                px = self.ps()[:, 0:64]
                P.mm_multi(px, [(px.ap[:, (hh * 2 + hf) * SL:(hh * 2 + hf + 1) * SL],
                                 [(ktp.ap[hf * 64:(hf + 1) * 64, hh * 128:(hh + 1) * 128],
                                   self.QS16.ap[hf * 64:(hf + 1) * 64, hh, qs])])
                                for hh in range(4) for hf in range(2)], reads=[ktp, self.QS16])
                e = self.BT[ei % 6][:, 0:64]
                ei += 1
                P.act(e, px, AF.Exp, scale=0.125)
                if step <= 3:
                    continue
                P.mm_multi(PO, [(PO.ap[:, col(s_, hh, hf):col(s_, hh, hf) + SL],
                                 [(vp16.ap[:, hh * 128:(hh + 1) * 128], e.ap[:, (hh * 2 + hf) * SL:(hh * 2 + hf + 1) * SL])])
                                for hh in range(4) for hf in range(2)], reads=[vp16, e], start=False, stop=False)
                P.mm_multi(PZ, [(PZ.ap[:, col(s_, hh, hf):col(s_, hh, hf) + SL],
                                 [(self.ONESB.ap, e.ap[:, (hh * 2 + hf) * SL:(hh * 2 + hf + 1) * SL])])
                                for hh in range(4) for hf in range(2)], reads=[self.ONESB, e], start=False, stop=False)
        if step >= 5:
            px = self.ps()[0:n, 0:256]
            P.mm_multi(px, [(px.ap[:, (hh * 2 + hf) * n:(hh * 2 + hf + 1) * n],
                             [(self.KZ16.ap[:, hf, hh, :], self.QS16.ap[:, hh, :])])
                            for hh in range(4) for hf in range(2)], reads=[self.KZ16, self.QS16])
            e = self.BT[ei % 6][0:n, 0:256]
            ei += 1
            P.act(e, px, AF.Exp, scale=0.125)
            bd = self.CONST[0:n, C_BD32:C_BD32 + n].rs("p (a t) -> p a t", a=1).bc([n, 8, n])
            P.tt("dve", e.r3(n), e.r3(n), bd, ALU.mult)
            if os.environ.get("ATT_SUB") != "a":
                P.mm_multi(PO, [(PO.ap[:, col(0, hh, hf):col(0, hh, hf) + n],
                                 [(self.VA16.ap[:, hh * 128:(hh + 1) * 128], e.ap[:, (hh * 2 + hf) * n:(hh * 2 + hf + 1) * n])])
                                for hh in range(4) for hf in range(2)], reads=[self.VA16, e], start=False, stop=False)
                P.mm_multi(PZ, [(PZ.ap[:, col(0, hh, hf):col(0, hh, hf) + n],
                                 [(self.ONESB.ap[0:n, :], e.ap[:, (hh * 2 + hf) * n:(hh * 2 + hf + 1) * n])])
                                for hh in range(4) for hf in range(2)], reads=[self.ONESB, e], start=False, stop=False)
        for acc in (PO, PZ):
            P.mm(acc[:, 0:256], [(self.ZB[:, 0:128], self.ZB)], start=False, stop=True)
        if step <= 5:
            return
        for h in range(4):
            r1, r2, o1, o2 = [self.FT[i][:, 0:n] for i in range(4)]
            c0, c1 = col(0, h, 0), col(0, h, 1)
            P.recip(r1, PZ[:, c0:c0 + n])
            P.recip(r2, PZ[:, c1:c1 + n])
            P.tt("dve", o1, PO[:, c0:c0 + n], r1, ALU.mult)
            P.tt("dve", o2, PO[:, c1:c1 + n], r2, ALU.mult)
            P.stt("dve", o1, o2, self.NLAM[:, l:l + 1], o1, ALU.mult, ALU.add)
            self.subln(o1, l, OB[:, h, 0:n])

    def run_samples(self):
        cfg, P, I, O = self.cfg, self.P, self.I, self.O
        g = Grp("s", ST, SPC, SL, SPC)
        XS = self.XS
        P.dma("sp", "xs", XS, I["xT_s"].rearrange("(c p) t -> p c t", p=128))

        def S32(ch, h):
            return self.SST32[:, ch, :]

        def S16(ch, h):
            return self.SST16[:, ch, :]

        for l in range(cfg.depth):
            def mk(src, dst, chan):
                def pre(h):
                    P.dma("pool", "sst", self.SST32, I[src][l, :, h].rearrange("s p v -> p s v"))
                    P.copy("act", self.SST16, self.SST32)

                def post(h):
                    self.out_dma(chan, O[dst][l, :, h].rearrange("s p v -> p s v"), self.SST32, queue="pool")
                return pre, post
            self.rmsnorm_fm(XS, self.LNM[:, l, :], self.H, g.n)
            pre, post = mk("st_hgrn", "s_hgrn", "sst_o")
            self.hgrn(g, l, S32, S16, pre, post)
            if cfg.upto == "hgrn":
                self.dbg_out(f"soa_{l}", self.OABC[0], [128, 4, TT])
                continue
            self.attn_sample(g, l)
            if cfg.upto == "attn":
                self.dbg_out(f"sob_{l}", self.OABC[1], [128, 4, TT])
                continue
            pre, post = mk("st_gdn", "s_gdn", "sst_o")
            self.gdn(g, l, S32, S16,
                     hist=lambda kc: I["st_gconv"][l, :, kc * 128:(kc + 1) * 128, :].rearrange("s p j -> p s j"),
                     tails=lambda kc: ("sgc", O["s_gconv"][l, :, kc * 128:(kc + 1) * 128, :].rearrange("s p j -> p s j")),
                     pre=pre, post=post)
            if cfg.upto == "gdn":
                self.dbg_out(f"soc_{l}", self.OABC[2], [128, 4, TT])
                continue
            self.merge_out(g, l, XS)
            self.ffn(g, l, XS,
                     hist=lambda j: I["st_fconv"][l, :, j * 128:(j + 1) * 128, :].rearrange("s p j -> p s j"),
                     tails=lambda j: ("sfc", O["s_fconv"][l, :, j * 128:(j + 1) * 128, :].rearrange("s p j -> p s j")))
        if cfg.upto == "all":
            self.out_dma("ys", O["yT_s"].rearrange("(c p) t -> p c t", p=128), XS)

    def run(self):
        cfg, P, I = self.cfg, self.P, self.I
        with contextlib.ExitStack() as st:
            self.st = st
            self.setup()
            self.alloc_mixers()
            XR = self.sb("XR", [128, 8, TT])
            if cfg.samples:
                self.alloc_samples()
                self.run_samples()
            if cfg.prompt:
                for t in range(cfg.seq // TT):
                    g = Grp("p", TT, 1, 64, 8, tile=t)
                    P.dma("sp", "x", XR, I["xT_p"][:, t * TT:(t + 1) * TT].rearrange("(c p) t -> p c t", p=128))
                    for l in range(cfg.depth):
                        self.rmsnorm_fm(XR, self.LNM[:, l, :], self.H, g.n)
                        self.hgrn(g, l, lambda ch, h, l=l: self.HS32[:, l, h, :],
                                  lambda ch, h, l=l: self.HS16[:, l, h, :])
                        if cfg.upto == "hgrn":
                            self.dbg_out(f"oa_{t}_{l}", self.OABC[0], [128, 4, TT])
                            continue
                        self.attn_prompt(g, l)
                        if cfg.upto == "attn":
                            self.dbg_out(f"ob_{t}_{l}", self.OABC[1], [128, 4, TT])
                            continue
                        self.gdn(g, l, lambda ch, h, l=l: self.GS32[:, l, h, :],
                                 lambda ch, h, l=l: self.GS16[:, l, h, :])
                        if cfg.upto == "gdn":
                            self.dbg_out(f"oc_{t}_{l}", self.OABC[2], [128, 4, TT])
                            continue
                        self.merge_out(g, l, XR)
                        self.ffn(g, l, XR)
                    if cfg.upto == "all":
                        self.out_dma("yp", self.O["yT_p"][:, t * TT:(t + 1) * TT].rearrange("(c p) t -> p c t", p=128), XR)
                if cfg.upto == "all":
                    for l in range(cfg.depth):
                        self.out_dma("pst", self.O["p_hgrn"][l].rearrange("h p v -> p h v"), self.HS32[:, l])
                        self.out_dma("pst", self.O["p_gdn"][l].rearrange("h p v -> p h v"), self.GS32[:, l])
                        self.out_dma("pst", self.O["p_gconv"][l].rearrange("(k p) j -> p k j", p=128), self.TAILG[:, l])
                        self.out_dma("pst", self.O["p_fconv"][l].rearrange("(k p) j -> p k j", p=128), self.TAILF[:, l])
                if cfg.upto == "hgrn":
                    self.dbg_out("hs", self.HS32, [128, cfg.depth, 4, 128])
                if cfg.upto == "gdn":
                    self.dbg_out("gs", self.GS32, [128, cfg.depth, 4, 128])
                    self.dbg_out("tailg", self.TAILG, [128, DEPTH, 12, 3])
            P.emit(final_chans=self.out_chans)
        return self.nc


def _host_inputs(inp):
    c = np.ascontiguousarray
    f = np.float32
    shared = {k: c(np.asarray(inp[k], dtype=f)) for k in
              ("cache_k", "cache_v", "w_in", "w_branch_a", "w_branch_b", "w_branch_c", "w_out", "w_up", "w_down")}
    shared["params"] = _pack_params(inp)
    shared["consts"] = _const_table()
    shared["rope_p"] = _rope_table(np.arange(SEQ))
    shared["rope_s"] = _rope_table(np.tile(PAST + np.arange(SL), SPC))
    xp = np.asarray(inp["x_prompt"], dtype=f)
    xs = np.asarray(inp["x_sample"], dtype=f)
    maps = []
    for core in range(NCORES):
        b = core // 2
        s0 = core * SPC
        m = dict(shared)
        m["xT_p"] = c(xp[b].T)
        m["xT_s"] = c(xs[s0:s0 + SPC].reshape(ST, D).T)
        m["st_hgrn"] = c(np.asarray(inp["state_hgrn"], f)[:, s0:s0 + SPC])
        m["st_gdn"] = c(np.asarray(inp["state_gdn"], f)[:, s0:s0 + SPC])
        m["st_gconv"] = c(np.asarray(inp["state_gdn_conv"], f)[:, s0:s0 + SPC].transpose(0, 1, 3, 2))
        m["st_fconv"] = c(np.asarray(inp["state_ffn_conv"], f)[:, s0:s0 + SPC].transpose(0, 1, 3, 2))
        m["page_table"] = c(np.asarray(inp["page_table"], np.int32)[s0:s0 + SPC])
        maps.append(m)
    return maps


def _host_outputs(res):
    r = res
    f = np.float32
    y_p = np.stack([r[2 * b]["yT_p"].T for b in range(NB)]).astype(f)
    y_s = np.concatenate([r[ci]["yT_s"].T.reshape(SPC, SL, D) for ci in range(NCORES)]).astype(f)
    p_hgrn = np.stack([r[2 * b]["p_hgrn"] for b in range(NB)], axis=1).astype(f)
    p_k = np.stack([r[2 * b]["p_kT"].transpose(0, 3, 1, 2) for b in range(NB)], axis=1).astype(f)
    p_v = np.stack([r[2 * b]["p_v"].reshape(DEPTH, SEQ, NH, 128) for b in range(NB)], axis=1).astype(f)
    p_gdn = np.stack([r[2 * b]["p_gdn"] for b in range(NB)], axis=1).astype(f)
    p_gc = np.stack([r[2 * b]["p_gconv"].transpose(0, 2, 1) for b in range(NB)], axis=1).astype(f)
    p_fc = np.stack([r[2 * b]["p_fconv"].transpose(0, 2, 1) for b in range(NB)], axis=1).astype(f)
    s_hgrn = np.concatenate([r[ci]["s_hgrn"] for ci in range(NCORES)], axis=1).astype(f)
    s_k = np.concatenate([r[ci]["s_kT"].transpose(0, 3, 1, 2).reshape(DEPTH, SPC, SL, NH, 128)
                          for ci in range(NCORES)], axis=1).astype(f)
    s_v = np.concatenate([r[ci]["s_v"].reshape(DEPTH, SPC, SL, NH, 128) for ci in range(NCORES)], axis=1).astype(f)
    s_gdn = np.concatenate([r[ci]["s_gdn"] for ci in range(NCORES)], axis=1).astype(f)
    s_gc = np.concatenate([r[ci]["s_gconv"].transpose(0, 1, 3, 2) for ci in range(NCORES)], axis=1).astype(f)
    s_fc = np.concatenate([r[ci]["s_fconv"].transpose(0, 1, 3, 2) for ci in range(NCORES)], axis=1).astype(f)
    return (y_p, y_s, p_hgrn, p_k, p_v, p_gdn, p_gc, p_fc, s_hgrn, s_k, s_v, s_gdn, s_gc, s_fc)


def kernel(**inputs):
    nc = Builder(Cfg()).run()
    in_maps = _host_inputs(inputs)
    res = run_bass_kernel_spmd(nc, in_maps, core_ids=list(range(NCORES)))
    return _host_outputs(res.results)
```

```python
import contextlib
import os
import sys

import numpy as np
import concourse.bass as bass
import concourse.mybir as mybir
from concourse.bass_utils import run_bass_kernel_spmd

F32 = mybir.dt.float32
BF16 = mybir.dt.bfloat16
I32 = mybir.dt.int32
AF = mybir.ActivationFunctionType
ALU = mybir.AluOpType
AX = mybir.AxisListType

SEM_CAP = 2048
SEM_POOL_LIMIT = 96

ENGINES = ("pe", "act", "dve", "pool", "sp")
RSQRT_VIA_LN = os.environ.get("RSQRT_LN", "1") == "1"


class View:
    __slots__ = ("ap", "toks")

    def __init__(self, ap, toks):
        self.ap = ap
        self.toks = tuple(toks)

    def __getitem__(self, idx):
        return View(self.ap[idx], self.toks)

    def with_ap(self, ap):
        return View(ap, self.toks)

    def bc(self, shape):
        return View(self.ap.broadcast_to(list(shape)), self.toks)

    def r3(self, b):
        return View(self.ap.rearrange("p (a b) -> p a b", b=b), self.toks)

    def bitcast(self, dt):
        return View(self.ap.bitcast(dt), self.toks)

    def rs(self, pattern, **kw):
        return View(self.ap.rearrange(pattern, **kw), self.toks)


class _Op:
    __slots__ = ("eng", "fn", "deps", "chan", "ndma", "event", "needed", "idx", "where")

    def __init__(self, eng, fn, chan=None, ndma=0):
        self.eng = eng
        self.fn = fn
        self.deps = set()
        self.chan = chan
        self.ndma = ndma
        self.event = None
        self.needed = False
        self.idx = -1


class Prog:
    def __init__(self, nc):
        self.nc = nc
        self.ops = []
        self.streams = {e: [] for e in ENGINES}
        self.last_w = {}
        self.readers = {}
        self.chan_last = {}
        self.barrier_chans = set()
        self.ntok = 0

    def tok(self, name):
        self.ntok += 1
        return (name, self.ntok)

    def view(self, ap, name, n=1):
        return View(ap, [self.tok(name) for _ in range(n)])

    def op(self, eng, fn, reads=(), writes=(), chan=None, ndma=0):
        o = _Op(eng, fn, chan, ndma)
        for v in reads:
            if v is None or not isinstance(v, View):
                continue
            for t in v.toks:
                w = self.last_w.get(t)
                if w is not None:
                    o.deps.add(w)
        for v in writes:
            for t in v.toks:
                w = self.last_w.get(t)
                if w is not None:
                    o.deps.add(w)
                for r in self.readers.get(t, ()):
                    o.deps.add(r)
        if chan is not None:
            prev = self.chan_last.get(chan)
            if prev is not None and chan not in self.barrier_chans:
                o.deps.add(prev)
            self.chan_last[chan] = o
        o.deps.discard(o)
        if eng == "pe":
            o.deps = {d for d in o.deps if d.eng != "pe"}
        for v in reads:
            if v is None or not isinstance(v, View):
                continue
            for t in v.toks:
                self.readers.setdefault(t, []).append(o)
        for v in writes:
            for t in v.toks:
                self.last_w[t] = o
                self.readers[t] = []
        o.idx = len(self.ops)
        fr = sys._getframe(2)
        o.where = f"{fr.f_code.co_name}:{fr.f_lineno}"
        self.ops.append(o)
        self.streams[eng].append(o)
        return o

    def emit(self, final_chans=()):
        nc = self.nc
        for o in self.ops:
            for d in o.deps:
                d.needed = True
        tails = [self.chan_last[c] for c in final_chans if c in self.chan_last]
        for t in tails:
            t.needed = True

        semkeys = []
        cnt = {}
        gen = {}

        def bump(base, step):
            g = gen.get(base, 0)
            c = cnt.get(base, 0)
            if c + step > SEM_CAP:
                g += 1
                c = 0
            c += step
            gen[base] = g
            cnt[base] = c
            key = (base, g)
            if not semkeys or key not in seen:
                seen.add(key)
                semkeys.append(key)
            return key, c

        seen = set()
        for e in ENGINES:
            for o in self.streams[e]:
                if o.chan is not None:
                    key = None
                    val = 0
                    if cnt.get(("c", o.chan), 0) + 16 * o.ndma > SEM_CAP:
                        cnt[("c", o.chan)] = SEM_CAP
                    for _ in range(o.ndma):
                        key, val = bump(("c", o.chan), 16)
                    o.event = (key, val)
                elif o.needed:
                    o.event = bump(("e", e), 1)
        finals = {}
        for o in self.ops:
            if o.chan in self.barrier_chans:
                k = o.event[0]
                finals[k] = max(finals.get(k, 0), o.event[1])
        assert len(semkeys) <= SEM_POOL_LIMIT, f"semaphore pool exhausted: {len(semkeys)}"
        self.n_sems = len(semkeys)

        with contextlib.ExitStack() as st:
            sems = {}
            for i, k in enumerate(semkeys):
                sems[k] = st.enter_context(nc.semaphore(f"s{i}"))
            block = st.enter_context(nc.Block())
            handles = {"pe": block.tensor, "act": block.scalar, "dve": block.vector,
                       "pool": block.gpsimd, "sp": block.sync}

            def make(ename):
                stream = self.streams[ename]

                def body(eng):
                    known = {}
                    for o in stream:
                        need = {}
                        for d in o.deps:
                            k, v = d.event
                            if d.chan in self.barrier_chans:
                                v = finals[k]
                            if known.get(k, 0) < v:
                                need[k] = max(need.get(k, 0), v)
                        for k, v in need.items():
                            eng.wait_ge(sems[k], v)
                            known[k] = v
                        try:
                            if o.chan is not None:
                                o.fn(eng, sems[o.event[0]])
                            else:
                                ins = o.fn(eng)
                                if o.needed:
                                    ins.then_inc(sems[o.event[0]], 1)
                        except Exception as exc:
                            raise RuntimeError(f"op #{o.idx} on {ename} recorded at {o.where}: {exc}") from exc
                    if ename == "sp":
                        for t in tails:
                            k, v = t.event
                            if known.get(k, 0) < v:
                                eng.wait_ge(sems[k], v)
                                known[k] = v
                return body

            for ename in ENGINES:
                if self.streams[ename] or ename == "sp":
                    handles[ename](make(ename))

    @staticmethod
    def _a(x):
        return x.ap if isinstance(x, View) else x

    def mm(self, out, pairs, extra_reads=(), start=True, stop=True):
        aps = [(l.ap, r.ap) for l, r in pairs]
        n = len(aps)
        oap = out.ap

        def fn(eng):
            ins = None
            for i, (l, r) in enumerate(aps):
                ins = eng.matmul(oap, l, r, start=(start and i == 0), stop=(stop and i == n - 1))
            return ins
        return self.op("pe", fn, reads=[v for p in pairs for v in p] + list(extra_reads),
                       writes=[out])

    def transpose(self, out, in_, ident):
        def fn(eng):
            return eng.transpose(out.ap, in_.ap, ident.ap)
        return self.op("pe", fn, reads=[in_, ident], writes=[out])

    def act(self, out, in_, func, bias=0.0, scale=1.0, accum=None):
        b, s = self._a(bias), self._a(scale)
        kw = {}
        if accum is not None:
            kw["accum_out"] = accum.ap

        def fn(eng):
            return eng.activation(out.ap, in_.ap, func, bias=b, scale=s, **kw)
        return self.op("act", fn, reads=[in_, bias, scale],
                       writes=[out] + ([accum] if accum is not None else []))

    def tt(self, eng_name, out, in0, in1, op):
        def fn(eng):
            return eng.tensor_tensor(out.ap, in0.ap, in1.ap, op)
        return self.op(eng_name, fn, reads=[in0, in1], writes=[out])

    def ts(self, eng_name, out, in0, s1, op0, s2=None, op1=None, accum=None):
        a1, a2 = self._a(s1), self._a(s2)
        kw = {}
        if op1 is not None:
            kw["op1"] = op1
        if accum is not None:
            kw["accum_out"] = accum.ap

        def fn(eng):
            return eng.tensor_scalar(out.ap, in0.ap, a1, a2, op0, **kw)
        return self.op(eng_name, fn, reads=[in0, s1, s2],
                       writes=[out] + ([accum] if accum is not None else []))

    def stt(self, eng_name, out, in0, scalar, in1, op0, op1):
        sc = self._a(scalar)

        def fn(eng):
            return eng.scalar_tensor_tensor(out.ap, in0.ap, sc, in1.ap, op0, op1)
        return self.op(eng_name, fn, reads=[in0, scalar, in1], writes=[out])

    def copy(self, eng_name, out, in_):
        if eng_name == "act":
            def fn(eng):
                return eng.copy(out.ap, in_.ap)
        else:
            def fn(eng):
                return eng.tensor_copy(out.ap, in_.ap)
        return self.op(eng_name, fn, reads=[in_], writes=[out])

    def memset(self, eng_name, out, val):
        def fn(eng):
            return eng.memset(out.ap, val)
        return self.op(eng_name, fn, writes=[out])

    def reduce(self, eng_name, out, in_, op, axis=None):
        ax = AX.X if axis is None else axis

        def fn(eng):
            return eng.tensor_reduce(out.ap, in_.ap, ax, op)
        return self.op(eng_name, fn, reads=[in_], writes=[out])

    def dma(self, queue, chan, out, in_, **kw):
        oap, iap = self._a(out), self._a(in_)

        def fn(eng, sem):
            return eng.dma_start(out=oap, in_=iap, **kw).then_inc(sem, 16)
        return self.op(queue, fn, reads=[in_] if isinstance(in_, View) else [],
                       writes=[out] if isinstance(out, View) else [], chan=chan, ndma=1)

    def recip(self, out, in_):
        def fn(eng):
            return eng.reciprocal(out.ap, in_.ap)
        return self.op("dve", fn, reads=[in_], writes=[out])

    def recip_pos(self, out, in_):
        self.act(out, in_, AF.Ln)
        return self.act(out, out, AF.Exp, scale=-1.0)

    def rsqrt(self, out, in_):
        if RSQRT_VIA_LN:
            self.act(out, in_, AF.Ln)
            return self.act(out, out, AF.Exp, scale=-0.5)
        self.recip(out, in_)
        return self.act(out, out, AF.Sqrt)

    def mm_multi(self, out, groups, reads, start=True, stop=True):
        def fn(eng):
            ins = None
            for oap, pairs in groups:
                n = len(pairs)
                for i, (l, r) in enumerate(pairs):
                    ins = eng.matmul(oap, l, r, start=(start and i == 0), stop=(stop and i == n - 1))
            return ins
        return self.op("pe", fn, reads=list(reads), writes=[out])

    def transpose_multi(self, out, items, reads):
        def fn(eng):
            ins = None
            for oap, iap, idap in items:
                ins = eng.transpose(oap, iap, idap)
            return ins
        return self.op("pe", fn, reads=list(reads), writes=[out])

    def scan(self, out, data0, data1, initial=0.0, op0=None, op1=None):
        o0 = ALU.mult if op0 is None else op0
        o1 = ALU.add if op1 is None else op1

        def fn(eng):
            return eng.tensor_tensor_scan(out.ap, data0.ap, data1.ap, initial, o0, o1)
        return self.op("dve", fn, reads=[data0, data1], writes=[out])


D = 1024
SEQ = 4096
DEPTH = 2
NB = 4
SB = 32
SL = 8
PAST = 8192
PAGE = 128
NPAGES = PAST // PAGE
NPOOL = 2560
NH = 4
DFF = 2816
NIN = 8712
GCH = 1536
EPS = 1e-6
TT = 512
NCORES = 8
SPC = SB // NCORES
ST = SPC * SL
O_HQ, O_HF, O_HI, O_HOG = 0, 512, 1024, 1536
O_DQ, O_DK, O_DV = 2048, 2560, 3072
O_GQ, O_GK, O_GV, O_GZ = 3584, 4096, 4608, 5120
O_GB, O_GA = 5632, 5636
O_GA_, O_GTA, O_GTB, O_GTC = 5636, 5640, 6664, 7688


def _decl(nc, name, shape, dt, kind):
    return nc.dram_tensor(name, list(shape), dt, kind=kind).ap()


IN_SPECS = [
    ("xT_p", (D, SEQ)), ("xT_s", (D, ST)),
    ("st_hgrn", (DEPTH, SPC, NH, 128, 128)), ("st_gdn", (DEPTH, SPC, NH, 128, 128)),
    ("st_gconv", (DEPTH, SPC, GCH, 3)), ("st_fconv", (DEPTH, SPC, DFF, 2)),
    ("cache_k", (DEPTH, NPOOL, PAGE, NH, 128)), ("cache_v", (DEPTH, NPOOL, PAGE, NH, 128)),
    ("w_in", (DEPTH, D, NIN)),
    ("w_branch_a", (DEPTH, 512, D)), ("w_branch_b", (DEPTH, 512, D)), ("w_branch_c", (DEPTH, 512, D)),
    ("w_out", (DEPTH, D, D)), ("w_up", (DEPTH, D, 2 * DFF)), ("w_down", (DEPTH, DFF, D)),
]
OUT_SPECS = [
    ("yT_p", (D, SEQ)), ("yT_s", (D, ST)),
    ("p_hgrn", (DEPTH, NH, 128, 128)), ("p_kT", (DEPTH, NH, 128, SEQ)), ("p_v", (DEPTH, SEQ, 512)),
    ("p_gdn", (DEPTH, NH, 128, 128)), ("p_gconv", (DEPTH, GCH, 3)), ("p_fconv", (DEPTH, DFF, 2)),
    ("s_hgrn", (DEPTH, SPC, NH, 128, 128)), ("s_kT", (DEPTH, NH, 128, ST)), ("s_v", (DEPTH, ST, 512)),
    ("s_gdn", (DEPTH, SPC, NH, 128, 128)), ("s_gconv", (DEPTH, SPC, GCH, 3)),
    ("s_fconv", (DEPTH, SPC, DFF, 2)),
]


C_ID, C_ONES, C_BLK, C_ROT, C_M128 = 0, 128, 256, 384, 512
C_MI64, C_MS64, C_MI8, C_MS8, C_RST512, C_RST32, C_IOTA = 640, 1152, 1664, 1696, 1728, 2240, 2272
C_LS64, C_LS8, C_BD32 = 2273, 2785, 2817
NCONST = 2849


def _const_table():
    t = np.zeros((128, NCONST), np.float32)
    p = np.arange(128)
    t[:, C_ID:C_ID + 128] = np.eye(128)
    t[:, C_ONES:C_ONES + 128] = 1.0
    t[:, C_BLK:C_BLK + 128] = (p[:, None] // 64 == p[None, :] // 64)
    rot = np.zeros((128, 128), np.float32)
    for m in range(128):
        if m % 64 < 32:
            rot[m + 32, m] = -1.0
        else:
            rot[m - 32, m] = 1.0
    t[:, C_ROT:C_ROT + 128] = rot
    t[:, C_M128:C_M128 + 128] = (p[None, :] >= p[:, None])
    s64 = np.arange(64)
    mi = (s64[None, :] >= s64[:, None]).astype(np.float32)
    ms = (s64[None, :] > s64[:, None]).astype(np.float32)
    t[:64, C_MI64:C_MI64 + 512] = np.tile(mi, (1, 8))
    t[:64, C_MS64:C_MS64 + 512] = np.tile(ms, (1, 8))
    s8 = np.arange(8)
    t[:8, C_MI8:C_MI8 + 32] = np.tile((s8[None, :] >= s8[:, None]).astype(np.float32), (1, 4))
    t[:8, C_MS8:C_MS8 + 32] = np.tile((s8[None, :] > s8[:, None]).astype(np.float32), (1, 4))
    t[:64, C_LS64:C_LS64 + 512] = np.tile((s64[None, :] < s64[:, None]).astype(np.float32), (1, 8))
    t[:8, C_LS8:C_LS8 + 32] = np.tile((s8[None, :] < s8[:, None]).astype(np.float32), (1, 4))
    i32 = np.arange(32)
    t[:32, C_BD32:C_BD32 + 32] = ((i32[:, None] // 8 == i32[None, :] // 8) & (i32[:, None] % 8 <= i32[None, :] % 8))
    t[:, C_RST512:C_RST512 + 512] = (np.arange(512) % 64 != 0)
    t[:, C_RST32:C_RST32 + 32] = (np.arange(32) % 8 != 0)
    t[:, C_IOTA] = p
    return t


def _rope_table(pos):
    inv = 1.0 / (10000.0 ** (np.arange(0, 64, 2, dtype=np.float32) / 64.0))
    f = inv[(np.arange(128) % 64) % 32].astype(np.float32)
    ang = f[:, None] * np.asarray(pos, np.float32)[None, :]
    return np.stack([np.cos(ang), np.sin(ang)]).astype(np.float32)


P_LNM, P_LNF, P_LBR, P_HN, P_SUBLN, P_GN, P_QKG = 0, 16, 32, 40, 42, 44, 46
P_LAMR, P_GCW, P_FCW, P_GA, P_GDT, NPRM = 50, 562, 658, 790, 798, 806


def _pack_params(inp):
    f = np.float32
    t = np.zeros((128, NPRM), f)
    g = lambda k: np.asarray(inp[k], f)
    t[:, P_LNM:P_LNM + 16] = g("ln_mix").reshape(DEPTH, 8, 128).transpose(2, 0, 1).reshape(128, 16)
    t[:, P_LNF:P_LNF + 16] = g("ln_ffn").reshape(DEPTH, 8, 128).transpose(2, 0, 1).reshape(128, 16)
    t[:, P_LBR:P_LBR + 8] = g("hgrn_lb").reshape(DEPTH, 4, 128).transpose(2, 0, 1).reshape(128, 8)
    t[:, P_HN:P_HN + 2] = g("hgrn_norm").T
    t[:, P_SUBLN:P_SUBLN + 2] = g("diff_subln").T
    t[:, P_GN:P_GN + 2] = g("gdn_norm").T
    qk = g("diff_qk_norm").transpose(2, 0, 1).reshape(64, 4)
    t[:, P_QKG:P_QKG + 4] = np.concatenate([qk, qk], 0)
    t[:, P_LAMR:P_LAMR + 512] = g("diff_lambda").reshape(1, 512)
    t[:, P_GCW:P_GCW + 96] = g("gdn_conv").reshape(DEPTH, 4, 12, 128).transpose(3, 0, 2, 1).reshape(128, 96)
    t[:, P_FCW:P_FCW + 132] = g("ffn_conv").reshape(DEPTH, 3, 22, 128).transpose(3, 0, 2, 1).reshape(128, 132)
    t[:, P_GA:P_GA + 8] = g("gdn_a_log").reshape(1, 8)
    t[:, P_GDT:P_GDT + 8] = g("gdn_dt_bias").reshape(1, 8)
    return t


class Cfg:
    def __init__(self, seq=SEQ, npool=NPOOL, depth=DEPTH, prompt=True, samples=True, upto="all",
                 dbg=()):
        self.seq, self.npool, self.depth = seq, npool, depth
        self.prompt, self.samples, self.upto, self.dbg = prompt, samples, upto, tuple(dbg)


class Grp:
    def __init__(self, kind, n, nseq, c, nch, tile=0):
        self.kind, self.n, self.nseq, self.c, self.nch, self.tile = kind, n, nseq, c, nch, tile
        self.L = n // nseq


class Builder:
    def __init__(self, cfg):
        self.cfg = cfg
        nc = bass.Bass("TRN2", target_bir_lowering=False)
        nc.allow_low_precision("bf16 matmul operands with fp32 PSUM accumulation (problem statement)")
        self.nc = nc
        self.P = Prog(nc)
        self.P.barrier_chans.add("const")
        specs = dict(IN_SPECS)
        specs["xT_p"] = (D, cfg.seq)
        specs["cache_k"] = (DEPTH, cfg.npool, PAGE, NH, 128)
        specs["cache_v"] = (DEPTH, cfg.npool, PAGE, NH, 128)
        self.I = {n: _decl(nc, n, s, F32, "ExternalInput") for n, s in specs.items()}
        self.I["page_table"] = _decl(nc, "page_table", (SPC, NPAGES), I32, "ExternalInput")
        self.I["consts"] = _decl(nc, "consts", (128, NCONST), F32, "ExternalInput")
        self.I["params"] = _decl(nc, "params", (128, NPRM), F32, "ExternalInput")
        self.I["rope_p"] = _decl(nc, "rope_p", (2, 128, cfg.seq), F32, "ExternalInput")
        self.I["rope_s"] = _decl(nc, "rope_s", (2, 128, ST), F32, "ExternalInput")
        ospecs = dict(OUT_SPECS)
        ospecs["yT_p"] = (D, cfg.seq)
        ospecs["p_kT"] = (DEPTH, NH, 128, cfg.seq)
        ospecs["p_v"] = (DEPTH, cfg.seq, 512)
        self.O = {n: _decl(nc, n, s, F32, "ExternalOutput") for n, s in ospecs.items()}
        self.dbg = {}
        self.out_chans = []
        self.wslot = 0
        self.psi = 0

    def sb(self, name, shape, dt=F32):
        return self.P.view(self.st.enter_context(self.nc.sbuf_tensor(name, list(shape), dt))[:], name)

    def psum(self, name, shape, dt=F32):
        return self.P.view(self.st.enter_context(self.nc.psum_tensor(name, list(shape), dt))[:], name)

    def ps(self):
        v = self.PS[self.psi % 4]
        self.psi += 1
        return v

    def dbg_out(self, name, view, shape):
        if name not in self.cfg.dbg:
            return
        t = _decl(self.nc, "dbg_" + name, shape, view.ap.dtype, "ExternalOutput")
        self.P.dma("sp", "dbg_" + name, t, view)
        self.out_chans.append("dbg_" + name)

    def out_dma(self, chan, dst, src, queue="sp"):
        if chan not in self.out_chans:
            self.out_chans.append(chan)
        self.P.dma(queue, chan, dst, src)

    def setup(self):
        P, I = self.P, self.I
        dp = DEPTH
        self.CONST = self.sb("CONST", [128, NCONST])
        P.dma("sp", "const", self.CONST, I["consts"])
        C = self.CONST
        self.ID = C[:, C_ID:C_ID + 128]
        self.ONES = C[:, C_ONES:C_ONES + 128]
        self.BLK = C[:, C_BLK:C_BLK + 128]
        self.ROT = C[:, C_ROT:C_ROT + 128]
        self.M128 = C[:, C_M128:C_M128 + 128]
        self.IDB = self.sb("IDB", [128, 128], BF16)
        P.copy("dve", self.IDB, self.ID)
        self.ONESB = self.sb("ONESB", [128, 128], BF16)
        P.copy("dve", self.ONESB, self.ONES)
        self.PS = [self.psum(f"PS{i}", [128, 512]) for i in range(8)]
        PRM = self.sb("PRM", [128, NPRM])
        P.dma("sp", "const", PRM, I["params"])
        self.LNM = PRM[:, P_LNM:P_LNM + 16].rs("p (l c) -> p l c", l=dp)
        self.LNF = PRM[:, P_LNF:P_LNF + 16].rs("p (l c) -> p l c", l=dp)
        LBR = PRM[:, P_LBR:P_LBR + 8].rs("p (l h) -> p l h", l=dp)
        self.HN = PRM[:, P_HN:P_HN + 2]
        self.SUBLN = PRM[:, P_SUBLN:P_SUBLN + 2]
        self.GN = PRM[:, P_GN:P_GN + 2]
        self.QKG = PRM[:, P_QKG:P_QKG + 4].rs("p (l j) -> p l j", l=dp)
        LAMR = PRM[:, P_LAMR:P_LAMR + 512].rs("p (l j d) -> p l j d", l=dp, j=4)
        self.GCW = PRM[:, P_GCW:P_GCW + 96].rs("p (l c j) -> p l c j", l=dp, c=12)
        self.FCW = PRM[:, P_FCW:P_FCW + 132].rs("p (l c j) -> p l c j", l=dp, c=22)
        self.GDT = PRM[:, P_GDT:P_GDT + 8]
        self.GA = self.sb("GA", [128, dp * 4])
        self.LB = self.sb("LB", [128, dp, 4])
        self.LB1M = self.sb("LB1M", [128, dp, 4])
        E = self.sb("LBE", [128, dp, 4])
        P.act(E, LBR, AF.Exp)
        S = self.sb("LBS", [128, 4])
        P.tt("dve", S, E[:, 0, :], E[:, 1, :], ALU.add)
        P.recip(S, S)
        P.memset("dve", self.LB[:, 0, :], 0.0)
        P.tt("dve", self.LB[:, 1, :], E[:, 1, :], S, ALU.mult)
        P.ts("dve", self.LB1M, self.LB, -1.0, ALU.mult, 1.0, ALU.add)
        self.LAM = self.sb("LAM", [128, dp])
        self.NLAM = self.sb("NLAM", [128, dp])
        PR = self.sb("LAMP", [128, dp, 2, 64])
        SM = self.sb("LAMS", [128, dp, 2])
        for l in range(dp):
            P.tt("dve", PR[:, l, 0, :], LAMR[:, l, 0, :], LAMR[:, l, 1, :], ALU.mult)
            P.tt("dve", PR[:, l, 1, :], LAMR[:, l, 2, :], LAMR[:, l, 3, :], ALU.mult)
            for j in range(2):
                P.reduce("dve", SM[:, l, j:j + 1], PR[:, l, j, :], ALU.add)
        P.act(SM, SM, AF.Exp)
        for l in range(dp):
            lam_init = 0.8 - 0.6 * float(np.exp(-0.3 * l))
            P.tt("dve", self.LAM[:, l:l + 1], SM[:, l, 0:1], SM[:, l, 1:2], ALU.subtract)
            P.ts("dve", self.LAM[:, l:l + 1], self.LAM[:, l:l + 1], lam_init, ALU.add)
        P.ts("dve", self.NLAM, self.LAM, -1.0, ALU.mult)
        P.act(self.GA, PRM[:, P_GA:P_GA + 8], AF.Exp)
        P.ts("dve", self.GA, self.GA, -1.0, ALU.mult)
        self.NW = 4
        self.W16 = [self.sb(f"W16_{i}", [128, 4096], BF16) for i in range(self.NW)]
        self.H = self.sb("H", [128, 8, TT], BF16)
        self.FT = [self.sb(f"FT{i}", [128, TT]) for i in range(10)]
        self.BT = [self.sb(f"BT{i}", [128, TT], BF16) for i in range(6)]
        scr = self.st.enter_context(self.nc.sbuf_tensor("SCR", [128, 3 * 4 * TT], F32))[:]
        stoks = [P.tok(f"SCR{i}") for i in range(3)]
        self.F4 = [View(scr[:, i * 4 * TT:(i + 1) * 4 * TT].rearrange("p (a b) -> p a b", a=4), [stoks[i]])
                   for i in range(3)]
        self.A16 = View(scr.bitcast(BF16)[:, 0:22 * TT].rearrange("p (a b) -> p a b", a=22), stoks)
        self.HST = View(scr[:, 2 * 4 * TT:3 * 4 * TT], [stoks[2]])
        self.OABC = [self.sb(f"O{x}", [128, 4, TT], BF16) for x in "abc"]

    def precast_weights(self):
        P, nc = self.P, self.nc
        self.WB = {}
        stage32 = [self.F4[0].rs("p a b -> p (a b)"), self.F4[1].rs("p a b -> p (a b)")]
        engs = ("act", "dve", "pool")
        i = 0
        for name in ("w_in", "w_branch_a", "w_branch_b", "w_branch_c", "w_out", "w_up", "w_down"):
            self.WB[name] = []
            for l in range(self.cfg.depth):
                src = self.I[name][l]
                rows, cols = src.shape
                t = nc.dram_tensor(f"wb_{name}_{l}", [rows, cols], BF16, kind="Internal").ap()
                wv = View(t, [P.tok(f"wb_{name}_{l}")])
                self.WB[name].append(wv)
                for r0 in range(0, rows, 128):
                    for c0 in range(0, cols, 2048):
                        w = min(2048, cols - c0)
                        s32 = stage32[i % 2][:, 0:w]
                        s16 = self.W16[i % self.NW][:, 0:w]
                        P.dma("sp", f"pcl{i % 2}", s32, src[r0:r0 + 128, c0:c0 + w])
                        P.copy(engs[i % 3], s16, s32)
                        P.dma("pool", f"pcs{i % self.NW}", wv[r0:r0 + 128, c0:c0 + w], s16)
                        i += 1

    def wpanel(self, wv, r0, nk, c0, ncols):
        P = self.P
        slot = self.wslot
        self.wslot = (self.wslot + 1) % self.NW
        w16 = self.W16[slot]
        v16 = w16.with_ap(w16.ap[:, 0:nk * ncols].rearrange("p (k n) -> p k n", k=nk))
        P.dma("sp", f"w{slot}", v16, wv[r0:r0 + nk * 128, c0:c0 + ncols].rs("(k p) n -> p k n", p=128))
        return v16

    def gemm_fm(self, w16, nk, rhs, n, consume, nm=None):
        ncols = w16.ap.shape[2]
        for m in range(nm if nm is not None else ncols // 128):
            pv = self.ps()[:, 0:n]
            self.P.mm(pv, [(w16[:, k, m * 128:(m + 1) * 128], rhs(k)) for k in range(nk)])
            consume(m, pv)

    def gemm_tm(self, w16, nk, lhs, g, consume):
        ncols = w16.ap.shape[2]
        for ch in range(g.nch):
            pv = self.ps()[0:g.c, 0:ncols]
            self.P.mm(pv, [(lhs(k, ch), w16[:, k, :]) for k in range(nk)])
            consume(ch, pv)

    def rmsnorm_fm(self, X, gain, Hout, n, nk=8, dim=D):
        P = self.P
        pv = self.ps()[:, 0:n]
        for k in range(nk):
            sq = self.FT[8 + k % 2][:, 0:n]
            P.act(sq, X[:, k, 0:n], AF.Square)
            P.mm(pv, [(self.ONES, sq)], start=(k == 0), stop=(k == nk - 1))
        rs = self.FT[7][:, 0:n]
        P.ts("dve", rs, pv, 1.0 / dim, ALU.mult, EPS, ALU.add)
        P.rsqrt(rs, rs)
        for k in range(nk):
            P.stt("dve", Hout[:, k, 0:n], X[:, k, 0:n], gain[:, k:k + 1], rs, ALU.mult, ALU.mult)


    def alloc_mixers(self):
        self.VT16 = self.sb("VT16", [64, 8, 512], BF16)
        self.KHT = self.sb("KHT", [64, 8, 128], BF16)
        dp = self.cfg.depth
        self.HS32 = self.sb("HS32", [128, dp, 4, 128])
        self.HS16 = self.sb("HS16", [128, dp, 4, 128], BF16)
        self.GS32 = self.sb("GS32", [128, dp, 4, 128])
        self.GS16 = self.sb("GS16", [128, dp, 4, 128], BF16)
        for t in (self.HS32, self.HS16, self.GS32, self.GS16):
            self.P.memset("pool", t, 0.0)
        self.KH16 = self.sb("KH16", [128, SEQ], BF16)
        self.VH16 = self.sb("VH16", [128, SEQ // 128, 128], BF16)
        self.ROPE = self.sb("ROPE", [128, 2, TT])
        self.VST = [self.sb(f"VST{i}", [128, 512]) for i in range(2)]
        self.KST = [self.sb(f"KST{i}", [128, TT]) for i in range(2)]
        self.Q16 = self.sb("Q16", [128, TT], BF16)
        self.vsti = 0
        self.TAILG = self.sb("TAILG", [128, DEPTH, 12, 3])
        self.TAILF = self.sb("TAILF", [128, DEPTH, 22, 2])
        self.P.memset("pool", self.TAILG, 0.0)
        self.P.memset("pool", self.TAILF, 0.0)
        self.CB = self.sb("CB", [128, 3 + TT])
        self.BG = self.sb("BG", [64, 8, 8])
        self.BETA = self.sb("BETA", [64, 8, 4])
        self.GTT = self.sb("GTT", [64, 8, 4])
        self.GT = self.sb("GT", [64, 8, 4])
        self.EGT = self.sb("EGT", [64, 8, 4])
        self.NBEG = self.sb("NBEG", [64, 8, 4])
        self.EH = self.sb("EH", [64, 8])
        self.KT16 = self.sb("KT16", [64, 8, 128], BF16)
        self.KHAT = self.sb("KHAT", [64, 8, 128], BF16)
        self.BV = self.sb("BV", [64, 8, 128])
        self.QKM = self.sb("QKM", [64, TT], BF16)
        self.RHS = [self.sb(f"RHS{i}", [64, 128]) for i in range(2)]
        self.U16 = [self.sb(f"U16_{i}", [64, 128], BF16) for i in range(2)]
        self.D_PK = [[View(self.O["p_kT"][l, h], [self.P.tok("pk")]) for h in range(4)] for l in range(DEPTH)]
        self.D_PV = [View(self.O["p_v"][l], [self.P.tok("pv")]) for l in range(DEPTH)]

    def hgrn(self, g, l, S32, S16, pre=None, post=None):
        P = self.P
        n, c, nch = g.n, g.c, g.nch
        H, W = self.H, self.WB["w_in"][l]
        Q, SG, OG = self.F4
        VT, OA = self.VT16, self.OABC[0]
        C = self.CONST
        rst = C[:, C_RST512:C_RST512 + n] if g.kind == "p" else C[:, C_RST32:C_RST32 + n]
        mi0 = C_MI64 if g.kind == "p" else C_MI8
        maskT = C[0:c, mi0:mi0 + n]

        def Hn(k):
            return H[:, k, 0:n]

        def Hc(k, ch):
            return H[:, k, ch * c:(ch + 1) * c]

        w = self.wpanel(W, 0, 8, O_HI, 512)
        self.gemm_tm(w, 8, Hc, g, lambda ch, pv: P.copy("act", VT[0:c, ch, :], pv))
        for off, dst, fn in ((O_HQ, Q, AF.Silu), (O_HF, SG, AF.Sigmoid), (O_HOG, OG, AF.Silu)):
            w = self.wpanel(W, 0, 8, off, 512)
            self.gemm_fm(w, 8, Hn, n, lambda m, pv, dst=dst, fn=fn: P.act(dst[:, m, 0:n], pv, fn))
        cm = c // 2 - 1
        for h in range(4):
            fg, kk, lf, G, d1, EQ, e2 = [self.FT[i][:, 0:n] for i in range(7)]
            QG, QC, KC, KH = [self.BT[i][:, 0:n] for i in range(4)]
            AT = self.BT[4][0:c, 0:n]
            P.ts("dve", fg, SG[:, h, 0:n], self.LB1M[:, l, h:h + 1], ALU.mult, self.LB[:, l, h:h + 1], ALU.add)
            P.ts("dve", kk, fg, -1.0, ALU.mult, 1.0, ALU.add)
            P.act(lf, fg, AF.Ln)
            P.scan(G, rst, lf)
            G3 = G.r3(c)
            P.act(EQ, G, AF.Exp)
            P.tt("dve", QG, Q[:, h, 0:n], EQ, ALU.mult)
            P.tt("dve", d1.r3(c), G3, G3[:, :, cm:cm + 1].bc([128, nch, c]), ALU.subtract)
            P.act(e2, d1, AF.Exp)
            P.tt("dve", QC, Q[:, h, 0:n], e2, ALU.mult)
            P.act(e2, d1, AF.Exp, scale=-1.0)
            P.tt("dve", KC, kk, e2, ALU.mult)
            P.tt("dve", d1.r3(c), G3, G3[:, :, c - 1:c].bc([128, nch, c]), ALU.subtract)
            P.act(e2, d1, AF.Exp, scale=-1.0)
            P.tt("dve", KH, kk, e2, ALU.mult)
            PA = self.PS[4]
            P.mm_multi(PA, [(PA.ap[0:c, ch * c:(ch + 1) * c],
                             [(KC.ap[:, ch * c:(ch + 1) * c], QC.ap[:, ch * c:(ch + 1) * c])])
                            for ch in range(nch)], reads=[KC, QC])
            P.tt("dve", AT, PA[0:c, 0:n], maskT, ALU.mult)
            PTB = self.PS[7].bitcast(BF16)
            P.transpose_multi(PTB, [(PTB.ap[0:c, ch * 128:(ch + 1) * 128], KH.ap[:, ch * c:(ch + 1) * c],
                                     self.IDB.ap) for ch in range(nch)], reads=[KH, self.IDB])
            P.copy("act", self.KHT[0:c, 0:nch, :], PTB[0:c, 0:nch * 128].r3(128))
            PO = self.PS[5]
            if pre:
                pre(h)
            for ch in range(nch):
                cs = slice(ch * c, (ch + 1) * c)
                vch = VT[0:c, ch, h * 128:(h + 1) * 128]
                P.mm(PO[:, cs], [(S16(ch, h), QG[:, cs]), (vch, AT[:, cs])])
                su = self.PS[6][:, (ch % 4) * 128:(ch % 4 + 1) * 128]
                P.mm(su, [(self.KHT[0:c, ch, :], vch)])
                P.stt("dve", S32(ch, h), S32(ch, h), EQ[:, ch * c + c - 1:ch * c + c], su, ALU.mult, ALU.add)
                P.copy("act", S16(ch, h), S32(ch, h))
            if post:
                post(h)
            sq, rs, t1 = self.FT[8][:, 0:n], self.FT[7][:, 0:n], self.FT[9][:, 0:n]
            P.act(sq, PO[:, 0:n], AF.Square)
            pss = self.ps()[:, 0:n]
            P.mm(pss, [(self.ONES, sq)])
            P.ts("dve", rs, pss, 1.0 / 128, ALU.mult, EPS, ALU.add)
            P.rsqrt(rs, rs)
            P.stt("dve", t1, PO[:, 0:n], self.HN[:, l:l + 1], rs, ALU.mult, ALU.mult)
            P.tt("dve", OA[:, h, 0:n], t1, OG[:, h, 0:n], ALU.mult)

    def qknorm_rope(self, x, gain, out):
        P = self.P
        n = x.ap.shape[1]
        sq, rs, xn, t1 = [self.FT[i][:, 0:n] for i in (8, 7, 6, 5)]
        P.act(sq, x, AF.Square)
        pss = self.ps()[:, 0:n]
        P.mm(pss, [(self.BLK, sq)])
        P.ts("dve", rs, pss, 1.0 / 64, ALU.mult, EPS, ALU.add)
        P.rsqrt(rs, rs)
        P.stt("dve", xn, x, gain, rs, ALU.mult, ALU.mult)
        pr = self.ps()[:, 0:n]
        P.mm(pr, [(self.ROT, xn)])
        P.tt("dve", t1, xn, self.ROPE[:, 0, 0:n], ALU.mult)
        P.tt("dve", xn, pr, self.ROPE[:, 1, 0:n], ALU.mult)
        P.tt("dve", out, t1, xn, ALU.add)

    def attn_prompt(self, g, l):
        P = self.P
        n, t = g.n, g.tile
        H, W = self.H, self.WB["w_in"][l]
        DQ, DK = self.F4[0], self.F4[1]
        OB = self.OABC[1]
        tok0 = t * TT

        def Hn(k):
            return H[:, k, 0:n]

        P.dma("pool", "rope", self.ROPE[:, :, 0:n], self.I["rope_p"][:, :, tok0:tok0 + n].rearrange("a p t -> p a t"))
        for off, dst in ((O_DQ, DQ), (O_DK, DK)):
            w = self.wpanel(W, 0, 8, off, 512)
            self.gemm_fm(w, 8, Hn, n, lambda m, pv, dst=dst: P.copy("act", dst[:, m, 0:n], pv))
        w = self.wpanel(W, 0, 8, O_DV, 512)
        for b in range(n // 128):
            pv = self.ps()
            P.mm(pv, [(H[:, k, b * 128:(b + 1) * 128], w[:, k, :]) for k in range(8)])
            vs = self.VST[self.vsti % 2]
            self.vsti += 1
            P.copy("act", vs, pv)
            self.out_dma("pv", self.D_PV[l][tok0 + b * 128:tok0 + (b + 1) * 128, :], vs, queue="pool")
        nkb = (tok0 + n) // 128
        for h in range(4):
            ks = self.KST[h % 2][:, 0:n]
            self.qknorm_rope(DK[:, h, 0:n], self.QKG[:, l, 1:2], ks)
            self.out_dma(f"pk{h}", self.D_PK[l][h][:, tok0:tok0 + n], ks, queue="pool")
            self.qknorm_rope(DQ[:, h, 0:n], self.QKG[:, l, 0:1], self.Q16[:, 0:n])
            for p0 in range(0, tok0 + n, 2048):
                p1 = min(tok0 + n, p0 + 2048)
                st = self.HST[:, 0:p1 - p0]
                P.dma("pool", "hist", st, self.D_PK[l][h][:, p0:p1])
                P.copy("pool", self.KH16[:, p0:p1], st)
                nb = (p1 - p0) // 128
                st3 = self.HST[:, 0:nb * 128].rs("p (b d) -> p b d", d=128)
                P.dma("pool", "hist", st3, self.D_PV[l][p0:p1, h * 128:(h + 1) * 128].rs("(b p) d -> p b d", p=128))
                P.copy("pool", self.VH16[:, p0 // 128:p0 // 128 + nb, :], st3)
            acc = [self.PS[4], self.PS[5], self.PS[6], self.PS[7]]
            ei = 0
            for half in range(2):
                hs = slice(half * 64, half * 64 + 64)
                O_, Z_ = acc[2 * half], acc[2 * half + 1]
                for kb in range(nkb):
                    r = kb - tok0 // 128
                    qlo = max(0, r * 128)
                    sp = self.ps()[:, 0:n - qlo]
                    P.mm(sp, [(self.KH16[hs, kb * 128:(kb + 1) * 128], self.Q16[hs, qlo:n])])
                    e = self.BT[ei % 6][:, 0:n - qlo]
                    ei += 1
                    P.act(e, sp, AF.Exp, scale=0.125)
                    if r >= 0:
                        P.tt("dve", e[:, 0:128], e[:, 0:128], self.M128, ALU.mult)
                    P.mm(O_[:, qlo:n], [(self.VH16[:, kb, :], e)], start=(kb == 0), stop=(kb == nkb - 1))
                    P.mm(Z_[:, qlo:n], [(self.ONESB, e)], start=(kb == 0), stop=(kb == nkb - 1))
            r1, r2, o1, o2 = [self.FT[i][:, 0:n] for i in range(4)]
            P.recip(r1, acc[1][:, 0:n])
            P.recip(r2, acc[3][:, 0:n])
            P.tt("dve", o1, acc[0][:, 0:n], r1, ALU.mult)
            P.tt("dve", o2, acc[2][:, 0:n], r2, ALU.mult)
            P.stt("dve", o1, o2, self.NLAM[:, l:l + 1], o1, ALU.mult, ALU.add)
            self.subln(o1, l, OB[:, h, 0:n])

    def subln(self, od, l, out):
        P = self.P
        n = od.ap.shape[1]
        sq, rs, t1 = self.FT[8][:, 0:n], self.FT[7][:, 0:n], self.FT[9][:, 0:n]
        P.act(sq, od, AF.Square)
        pss = self.ps()[:, 0:n]
        P.mm(pss, [(self.ONES, sq)])
        P.ts("dve", rs, pss, 1.0 / 128, ALU.mult, EPS, ALU.add)
        P.rsqrt(rs, rs)
        P.stt("dve", t1, od, self.SUBLN[:, l:l + 1], rs, ALU.mult, ALU.mult)
        lam_init = 0.8 - 0.6 * float(np.exp(-0.3 * l))
        P.ts("dve", out, t1, 1.0 - lam_init, ALU.mult)

    def conv_fm(self, g, pv, wcol, tail, ntap, dst, hist_src=None, tail_dst=None):
        P = self.P
        L, ns, hl = g.L, g.nseq, ntap - 1
        cb = self.CB[:, 0:ns * (hl + L)].rs("p (s j) -> p s j", s=ns)
        if g.kind == "p":
            P.copy("act", cb[:, 0, 0:hl], tail)
        else:
            P.dma("pool", "chist", cb[:, :, 0:hl], hist_src)
        P.copy("act", cb[:, :, hl:hl + L], pv.rs("p (s j) -> p s j", s=ns))
        d3 = dst.rs("p (s j) -> p s j", s=ns)
        P.ts("dve", d3, cb[:, :, 0:L], wcol(0), ALU.mult)
        for j in range(1, ntap):
            P.stt("dve", d3, cb[:, :, j:j + L], wcol(j), d3, ALU.mult, ALU.add)
        if g.kind == "p":
            P.copy("act", tail, cb[:, 0, L:L + hl])
        else:
            self.out_dma(tail_dst[0], tail_dst[1], cb[:, :, L:L + hl], queue="pool")

    def gdn(self, g, l, S32, S16, hist=None, tails=None, pre=None, post=None):
        P = self.P
        n, c, nch = g.n, g.c, g.nch
        H, W = self.H, self.WB["w_in"][l]
        C = self.CONST
        isp = g.kind == "p"
        MI = C[0:c, (C_MI64 if isp else C_MI8):][:, 0:n]
        MS = C[0:c, (C_MS64 if isp else C_MS8):][:, 0:n]
        LS = C[0:c, (C_LS64 if isp else C_LS8):][:, 0:n]
        IDb = C[0:c, C_ID:C_ID + c].rs("p (a t) -> p a t", a=1).bc([c, nch, c])
        OC = self.OABC[2]
        QF, KF, VF = self.F4

        def Hn(k):
            return H[:, k, 0:n]

        def Hc(k, ch):
            return H[:, k, ch * c:(ch + 1) * c]

        def cols(v, ch):
            return v.ap[:, ch * c:(ch + 1) * c]

        for j3, (off, dst) in enumerate(((O_GQ, QF), (O_GK, KF), (O_GV, VF))):
            w = self.wpanel(W, 0, 8, off, 512)

            def cons(m, pv, j3=j3, dst=dst):
                kc = j3 * 4 + m
                pre = self.FT[9][:, 0:n]
                self.conv_fm(g, pv, lambda j: self.GCW[:, l, kc, j:j + 1], self.TAILG[:, l, kc, :], 4, pre,
                             hist_src=None if isp else hist(kc), tail_dst=None if isp else tails(kc))
                P.act(dst[:, m, 0:n], pre, AF.Silu)
            self.gemm_fm(w, 8, Hn, n, cons)
        w = self.wpanel(W, 0, 8, O_GZ, 512)
        self.gemm_fm(w, 8, Hn, n, lambda m, pv: P.act(OC[:, m, 0:n], pv, AF.Silu))
        w = self.wpanel(W, 0, 8, O_GB, 8)
        BG, BETA, GTT, GT, EGT, NBEG = [x[0:c, 0:nch, :] for x in
                                        (self.BG, self.BETA, self.GTT, self.GT, self.EGT, self.NBEG)]
        self.gemm_tm(w, 8, Hc, g, lambda ch, pv: P.copy("act", BG[:, ch, :], pv))
        P.act(BETA, BG[:, :, 0:4], AF.Sigmoid)
        P.tt("dve", GTT, BG[:, :, 4:8], self.GDT[0:c, l * 4:l * 4 + 4].rs("p (a h) -> p a h", a=1).bc([c, nch, 4]), ALU.add)
        P.act(GTT, GTT, AF.Exp)
        P.act(GTT, GTT, AF.Ln, bias=1.0)
        P.tt("dve", GTT, GTT, self.GA[0:c, l * 4:l * 4 + 4].rs("p (a h) -> p a h", a=1).bc([c, nch, 4]), ALU.mult)
        pg = self.ps()[0:c, 0:nch * 4]
        P.mm(pg, [(MI[:, 0:c], GTT.rs("p a h -> p (a h)"))])
        P.copy("act", GT.rs("p a h -> p (a h)"), pg)
        P.act(EGT, GT, AF.Exp)
        P.stt("dve", NBEG, BETA, -1.0, EGT, ALU.mult, ALU.mult)
        nlev = int(round(np.log2(c)))
        EH2 = self.EH[0:c, 0:nch]
        EH3 = EH2.rs("p (a b) -> p a b", b=1)
        for h in range(4):
            qn, kn = self.BT[0][:, 0:n], self.BT[1][:, 0:n]
            for src, dst, sc in ((QF, qn, 128 ** -0.5), (KF, kn, 1.0)):
                sq, rs = self.FT[8][:, 0:n], self.FT[7][:, 0:n]
                P.act(sq, src[:, h, 0:n], AF.Square)
                pss = self.ps()[:, 0:n]
                P.mm(pss, [(self.ONES, sq)])
                P.ts("dve", rs, pss, EPS, ALU.add)
                P.rsqrt(rs, rs)
                P.stt("dve", dst, src[:, h, 0:n], sc, rs, ALU.mult, ALU.mult)
            v16 = self.BT[2][:, 0:n]
            P.copy("act", v16, VF[:, h, 0:n])
            PTB = self.ps().bitcast(BF16)
            P.transpose_multi(PTB, [(PTB.ap[0:c, ch * 128:(ch + 1) * 128], cols(kn, ch), self.IDB.ap)
                                    for ch in range(nch)], reads=[kn, self.IDB])
            KT = self.KT16[0:c, 0:nch, :]
            P.copy("act", KT, PTB[0:c, 0:nch * 128].r3(128))
            PTB = self.ps().bitcast(BF16)
            P.transpose_multi(PTB, [(PTB.ap[0:c, ch * 128:(ch + 1) * 128], cols(v16, ch), self.IDB.ap)
                                    for ch in range(nch)], reads=[v16, self.IDB])
            BV = self.BV[0:c, 0:nch, :]
            P.tt("dve", BV, PTB[0:c, 0:nch * 128].r3(128), BETA[:, :, h:h + 1].bc([c, nch, 128]), ALU.mult)
            GU, raw, dT, dL, Y, Yt = [self.FT[i][0:c, 0:n] for i in (1, 2, 3, 4, 5, 6)]
            tmp, Wm = GU, dL
            P.tt("dve", GU.r3(c), GTT[:, :, h:h + 1].bc([c, nch, c]), MI.r3(c), ALU.mult)
            R = self.PS[4][:, 0:n]
            P.mm(R, [(self.ONES[0:c, :], GU)])
            EG = self.FT[0][:, 0:n]
            P.act(EG, R, AF.Exp)
            qg = self.BT[3][:, 0:n]
            P.tt("dve", qg, qn, EG, ALU.mult)
            Rc = R[0:c, :].r3(c)
            P.tt("dve", raw.r3(c), Rc, GT[:, :, h:h + 1].bc([c, nch, c]), ALU.subtract)
            P.tt("dve", EH3, Rc[:, :, c - 1:c], GT[:, :, h:h + 1], ALU.subtract)
            P.act(EH2, EH2, AF.Exp)
            P.tt("dve", self.KHAT[0:c, 0:nch, :], KT, EH3.bc([c, nch, 128]), ALU.mult)
            P.ts("dve", dT, raw, 0.0, ALU.min)
            P.act(dT, dT, AF.Exp)
            P.tt("dve", dT, dT, MI, ALU.mult)
            P.ts("dve", dL, raw, -1.0, ALU.mult, 0.0, ALU.min)
            P.act(dL, dL, AF.Exp)
            P.tt("dve", dL, dL, LS, ALU.mult)
            PKK, PQK = self.PS[5], self.PS[6]
            P.mm_multi(PKK, [(PKK.ap[0:c, ch * c:(ch + 1) * c], [(cols(kn, ch), cols(kn, ch))]) for ch in range(nch)], reads=[kn])
            P.mm_multi(PQK, [(PQK.ap[0:c, ch * c:(ch + 1) * c], [(cols(kn, ch), cols(qn, ch))]) for ch in range(nch)], reads=[kn, qn])
            QKM = self.QKM[0:c, 0:n]
            P.tt("dve", QKM, PQK[0:c, 0:n], dT, ALU.mult)
            P.tt("dve", tmp.r3(c), BETA[:, :, h:h + 1].bc([c, nch, c]), IDb, ALU.mult)
            PB = self.PS[7][0:c, 0:n]
            P.mm(PB, [(self.ONES[0:c, 0:c], tmp)])
            P.tt("dve", Y, PKK[0:c, 0:n], dT, ALU.mult)
            P.tt("dve", Y, Y, MS, ALU.mult)
            P.stt("dve", Y, PB, -1.0, Y, ALU.mult, ALU.mult)
            P.tt("dve", Yt, PKK[0:c, 0:n], dL, ALU.mult)
            P.stt("dve", Yt.r3(c), BETA[:, :, h:h + 1].bc([c, nch, c]), -1.0, Yt.r3(c), ALU.mult, ALU.mult)
            P.tt("dve", Wm.r3(c), Y.r3(c), IDb, ALU.add)
            for lev in range(1, nlev):
                last = lev == nlev - 1
                pb = self.ps()
                P.mm_multi(pb, [(pb.ap[0:c, ch * c:(ch + 1) * c], [(cols(Y, ch), cols(Yt, ch))]) for ch in range(nch)], reads=[Y, Yt])
                if not last:
                    pa = self.ps()
                    P.mm_multi(pa, [(pa.ap[0:c, ch * c:(ch + 1) * c], [(cols(Yt, ch), cols(Y, ch))]) for ch in range(nch)], reads=[Y, Yt])
                    P.copy("act", Y, pa[0:c, 0:n])
                P.copy("act", Yt, pb[0:c, 0:n])
                pw = self.ps()
                P.mm_multi(pw, [(pw.ap[0:c, ch * c:(ch + 1) * c], [(cols(Yt, ch), cols(Wm, ch))]) for ch in range(nch)], reads=[Yt, Wm])
                P.tt("dve", Wm, Wm, pw[0:c, 0:n], ALU.add)
            PO = self.PS[4]
            if pre:
                pre(h)
            for ch in range(nch):
                cs = slice(ch * c, (ch + 1) * c)
                pk = self.ps()[0:c, 0:128]
                P.mm(pk, [(kn[:, cs], S16(ch, h))])
                rh = self.RHS[ch % 2][0:c, :]
                P.stt("dve", rh, pk, NBEG[:, ch, h:h + 1], BV[:, ch, :], ALU.mult, ALU.add)
                pu = self.ps()[0:c, 0:128]
                P.mm(pu, [(Wm[:, cs], rh)])
                u16 = self.U16[ch % 2][0:c, :]
                P.copy("act", u16, pu)
                P.mm(PO[:, cs], [(S16(ch, h), qg[:, cs]), (u16, QKM[:, cs])])
                su = self.ps()[:, 0:128]
                P.mm(su, [(self.KHAT[0:c, ch, :], u16)])
                P.stt("dve", S32(ch, h), S32(ch, h), EG[:, ch * c + c - 1:ch * c + c], su, ALU.mult, ALU.add)
                P.copy("act", S16(ch, h), S32(ch, h))
            if post:
                post(h)
            sq, rs, t1 = self.FT[8][:, 0:n], self.FT[7][:, 0:n], self.FT[9][:, 0:n]
            P.act(sq, PO[:, 0:n], AF.Square)
            pss = self.ps()[:, 0:n]
            P.mm(pss, [(self.ONES, sq)])
            P.ts("dve", rs, pss, 1.0 / 128, ALU.mult, EPS, ALU.add)
            P.rsqrt(rs, rs)
            P.stt("dve", t1, PO[:, 0:n], self.GN[:, l:l + 1], rs, ALU.mult, ALU.mult)
            P.tt("dve", OC[:, h, 0:n], t1, OC[:, h, 0:n], ALU.mult)

    def merge_out(self, g, l, XR):
        P = self.P
        n = g.n
        H, I = self.H, self.I
        MIX = [self.F4[0], self.F4[1]]

        def Hn(k):
            return H[:, k, 0:n]

        for c0 in (0, 512):
            for bi, (wname, goff) in enumerate((("w_branch_a", O_GTA), ("w_branch_b", O_GTB), ("w_branch_c", O_GTC))):
                O_x = self.OABC[bi]
                wg = self.wpanel(self.WB["w_in"][l], 0, 8, goff + c0, 512)
                wb = self.wpanel(self.WB[wname][l], 0, 4, c0, 512)
                for m in range(4):
                    pg = self.ps()[:, 0:n]
                    P.mm(pg, [(wg[:, k, m * 128:(m + 1) * 128], Hn(k)) for k in range(8)])
                    sg = self.FT[m % 2][:, 0:n]
                    P.act(sg, pg, AF.Sigmoid)
                    pb = self.ps()[:, 0:n]
                    P.mm(pb, [(wb[:, k, m * 128:(m + 1) * 128], O_x[:, k, 0:n]) for k in range(4)])
                    dst = MIX[c0 // 512][:, m, 0:n]
                    if bi == 0:
                        P.tt("dve", dst, pb, sg, ALU.mult)
                    else:
                        tmp = self.FT[2 + m % 2][:, 0:n]
                        P.tt("dve", tmp, pb, sg, ALU.mult)
                        P.tt("pool", dst, dst, tmp, ALU.add)
        for k in range(8):
            P.copy("act", H[:, k, 0:n], MIX[k // 4][:, k % 4, 0:n])
        for c0 in (0, 512):
            w = self.wpanel(self.WB["w_out"][l], 0, 8, c0, 512)
            self.gemm_fm(w, 8, Hn, n, lambda m, pv, c0=c0: P.tt("dve", XR[:, c0 // 128 + m, 0:n], XR[:, c0 // 128 + m, 0:n], pv, ALU.add))

    def ffn(self, g, l, XR, hist=None, tails=None):
        P = self.P
        n = g.n
        H, I = self.H, self.I
        isp = g.kind == "p"
        A16 = self.A16

        def Hn(k):
            return H[:, k, 0:n]

        self.rmsnorm_fm(XR, self.LNF[:, l, :], H, n)
        for c0 in range(0, DFF, 512):
            wd = min(512, DFF - c0)
            wg = self.wpanel(self.WB["w_up"][l], 0, 8, c0, wd)
            wu = self.wpanel(self.WB["w_up"][l], 0, 8, DFF + c0, wd)
            for m in range(wd // 128):
                j = c0 // 128 + m
                pg = self.ps()[:, 0:n]
                P.mm(pg, [(wg[:, k, m * 128:(m + 1) * 128], Hn(k)) for k in range(8)])
                pre = self.FT[9][:, 0:n]
                self.conv_fm(g, pg, lambda t, j=j: self.FCW[:, l, j, t:t + 1], self.TAILF[:, l, j, :], 3, pre,
                             hist_src=None if isp else hist(j), tail_dst=None if isp else tails(j))
                sg = self.FT[m % 2][:, 0:n]
                P.act(sg, pre, AF.Silu)
                pu = self.ps()[:, 0:n]
                P.mm(pu, [(wu[:, k, m * 128:(m + 1) * 128], Hn(k)) for k in range(8)])
                P.tt("dve", A16[:, j, 0:n], pu, sg, ALU.mult)
        for m in range(8):
            w = self.wpanel(self.WB["w_down"][l], 0, 22, m * 128, 128)
            pv = self.ps()[:, 0:n]
            P.mm(pv, [(w[:, k, :], A16[:, k, 0:n]) for k in range(22)])
            P.tt("dve", XR[:, m, 0:n], XR[:, m, 0:n], pv, ALU.add)

    def alloc_samples(self):
        P = self.P
        self.XS = self.sb("XS", [128, 8, ST])
        self.SST32 = self.sb("SST32", [128, SPC, 128])
        self.SST16 = self.sb("SST16", [128, SPC, 128], BF16)
        self.QS16 = self.sb("QS16", [128, 4, ST], BF16)
        self.ZB = self.sb("ZB", [128, 256], BF16)
        P.memset("dve", self.ZB, 0.0)
        self.KZ16 = self.sb("KZ16", [128, 2, 4, ST], BF16)
        P.memset("dve", self.KZ16, 0.0)
        self.VA16 = self.sb("VA16", [ST, 512], BF16)
        npt = SPC * NPAGES
        self.IDX = self.sb("IDX", [128, DEPTH, npt], I32)
        pti = self.FT[2][:, 0:npt].bitcast(I32)
        ptf, idf = self.FT[0][:, 0:npt], self.FT[1][:, 0:npt]
        P.dma("sp", "pt", pti, self.I["page_table"].rearrange("s j -> (s j)").partition_broadcast(128))
        P.copy("dve", ptf, pti)
        P.ts("dve", idf, ptf, float(PAGE), ALU.mult, self.CONST[:, C_IOTA:C_IOTA + 1], ALU.add)
        for l in range(DEPTH):
            if l:
                P.ts("dve", idf, idf, float(self.cfg.npool * PAGE), ALU.add)
            P.copy("dve", self.IDX[:, l, :], idf)
        self.dbg_out("idx", self.IDX, [128, DEPTH, npt])

    def gather(self, chan, dst, rows, idx_col, nrows):
        iap, dap = idx_col.ap, dst.ap

        def fn(eng, sem):
            return eng.indirect_dma_start(
                out=dap, out_offset=None, in_=rows,
                in_offset=bass.IndirectOffsetOnAxis(ap=iap, axis=0)).then_inc(sem, 16)
        return self.P.op("pool", fn, reads=[idx_col], writes=[dst], chan=chan, ndma=1)

    def attn_sample(self, g, l):
        P = self.P
        n = g.n
        H, W = self.H, self.WB["w_in"][l]
        DQ, DK = self.F4[0], self.F4[1]
        OB = self.OABC[1]
        cfg = self.cfg

        def Hn(k):
            return H[:, k, 0:n]

        P.dma("pool", "rope", self.ROPE[:, :, 0:n], self.I["rope_s"].rearrange("a p t -> p a t"))
        for off, dst in ((O_DQ, DQ), (O_DK, DK)):
            w = self.wpanel(W, 0, 8, off, 512)
            self.gemm_fm(w, 8, Hn, n, lambda m, pv, dst=dst: P.copy("act", dst[:, m, 0:n], pv))
        w = self.wpanel(W, 0, 8, O_DV, 512)

        pv = self.ps()[0:n, :]
        P.mm(pv, [(H[:, k, 0:n], w[:, k, :]) for k in range(8)])
        vs = self.VST[0][0:n, :]
        P.copy("act", vs, pv)
        P.copy("act", self.VA16, pv)
        self.out_dma("sv", self.O["s_v"][l], vs, queue="pool")
        for h in range(4):
            ks = self.KST[h % 2][:, 0:n]
            self.qknorm_rope(DK[:, h, 0:n], self.QKG[:, l, 1:2], ks)
            self.out_dma(f"sk{h % 2}", self.O["s_kT"][l, h], ks, queue="pool")
            P.copy("act", self.KZ16[0:64, 0, h, :], ks[0:64, :])
            P.copy("act", self.KZ16[64:128, 1, h, :], ks[64:128, :])
            self.qknorm_rope(DQ[:, h, 0:n], self.QKG[:, l, 0:1], self.QS16[:, h, :])
        step = int(os.environ.get("ATT_STEP", 6))
        if step <= 1:
            return
        krows = self.I["cache_k"].rearrange("l a s h d -> (l a s) (h d)")
        vrows = self.I["cache_v"].rearrange("l a s h d -> (l a s) (h d)")
        nrows = DEPTH * cfg.npool * PAGE
        PO, PZ = self.PS[4], self.PS[5]

        def col(s_, h_, half):
            return ((h_ * 2 + half) * SPC + s_) * SL

        for acc in (PO, PZ):
            P.mm(acc[:, 0:256], [(self.ZB[:, 0:128], self.ZB)], start=True, stop=False)
        ei = 0
        dbg_ns = int(os.environ.get("ATT_SEQS", SPC))
        dbg_np = int(os.environ.get("ATT_PAGES", NPAGES))
        for s_ in range(dbg_ns):
            qs = slice(s_ * SL, (s_ + 1) * SL)
            for j in range(dbg_np):
                kp, vp = self.KST[j % 2], self.VST[j % 2]
                ic = self.IDX[:, l, s_ * NPAGES + j:s_ * NPAGES + j + 1]
                self.gather(f"gk{j % 2}", kp, krows, ic, nrows)
                self.gather(f"gv{j % 2}", vp, vrows, ic, nrows)
                pt = self.ps()
                P.transpose_multi(pt, [(pt.ap[:, hh * 128:(hh + 1) * 128], kp.ap[:, hh * 128:(hh + 1) * 128], self.ID.ap)
                                       for hh in range(4)], reads=[kp, self.ID])
                ktp = self.KH16[:, (j % 2) * 512:(j % 2) * 512 + 512]
                P.copy("act", ktp, pt)
                vp16 = self.VH16[:, (j % 2) * 4:(j % 2) * 4 + 4, :].rs("p a d -> p (a d)")
                P.copy("dve", vp16, vp)
                if step <= 2:
                    continue
                px = self.ps()[:, 0:64]
                P.mm_multi(px, [(px.ap[:, (hh * 2 + hf) * SL:(hh * 2 + hf + 1) * SL],
                                 [(ktp.ap[hf * 64:(hf + 1) * 64, hh * 128:(hh + 1) * 128],
                                   self.QS16.ap[hf * 64:(hf + 1) * 64, hh, qs])])
                                for hh in range(4) for hf in range(2)], reads=[ktp, self.QS16])
                e = self.BT[ei % 6][:, 0:64]
                ei += 1
                P.act(e, px, AF.Exp, scale=0.125)
                if step <= 3:
                    continue
                P.mm_multi(PO, [(PO.ap[:, col(s_, hh, hf):col(s_, hh, hf) + SL],
                                 [(vp16.ap[:, hh * 128:(hh + 1) * 128], e.ap[:, (hh * 2 + hf) * SL:(hh * 2 + hf + 1) * SL])])
                                for hh in range(4) for hf in range(2)], reads=[vp16, e], start=False, stop=False)
                P.mm_multi(PZ, [(PZ.ap[:, col(s_, hh, hf):col(s_, hh, hf) + SL],
                                 [(self.ONESB.ap, e.ap[:, (hh * 2 + hf) * SL:(hh * 2 + hf + 1) * SL])])
                                for hh in range(4) for hf in range(2)], reads=[self.ONESB, e], start=False, stop=False)
        if step >= 5:
            px = self.ps()[0:n, 0:256]
            P.mm_multi(px, [(px.ap[:, (hh * 2 + hf) * n:(hh * 2 + hf + 1) * n],
                             [(self.KZ16.ap[:, hf, hh, :], self.QS16.ap[:, hh, :])])
                            for hh in range(4) for hf in range(2)], reads=[self.KZ16, self.QS16])
            e = self.BT[ei % 6][0:n, 0:256]
            ei += 1
            P.act(e, px, AF.Exp, scale=0.125)
            bd = self.CONST[0:n, C_BD32:C_BD32 + n].rs("p (a t) -> p a t", a=1).bc([n, 8, n])
            P.tt("dve", e.r3(n), e.r3(n), bd, ALU.mult)
            if os.environ.get("ATT_SUB") != "a":
                P.mm_multi(PO, [(PO.ap[:, col(0, hh, hf):col(0, hh, hf) + n],
                                 [(self.VA16.ap[:, hh * 128:(hh + 1) * 128], e.ap[:, (hh * 2 + hf) * n:(hh * 2 + hf + 1) * n])])
                                for hh in range(4) for hf in range(2)], reads=[self.VA16, e], start=False, stop=False)
                P.mm_multi(PZ, [(PZ.ap[:, col(0, hh, hf):col(0, hh, hf) + n],
                                 [(self.ONESB.ap[0:n, :], e.ap[:, (hh * 2 + hf) * n:(hh * 2 + hf + 1) * n])])
                                for hh in range(4) for hf in range(2)], reads=[self.ONESB, e], start=False, stop=False)
        for acc in (PO, PZ):
            P.mm(acc[:, 0:256], [(self.ZB[:, 0:128], self.ZB)], start=False, stop=True)
        if step <= 5:
            return
        for h in range(4):
            r1, r2, o1, o2 = [self.FT[i][:, 0:n] for i in range(4)]
            c0, c1 = col(0, h, 0), col(0, h, 1)
            P.recip(r1, PZ[:, c0:c0 + n])
            P.recip(r2, PZ[:, c1:c1 + n])
            P.tt("dve", o1, PO[:, c0:c0 + n], r1, ALU.mult)
            P.tt("dve", o2, PO[:, c1:c1 + n], r2, ALU.mult)
            P.stt("dve", o1, o2, self.NLAM[:, l:l + 1], o1, ALU.mult, ALU.add)
            self.subln(o1, l, OB[:, h, 0:n])

    def run_samples(self):
        cfg, P, I, O = self.cfg, self.P, self.I, self.O
        g = Grp("s", ST, SPC, SL, SPC)
        XS = self.XS
        P.dma("sp", "xs", XS, I["xT_s"].rearrange("(c p) t -> p c t", p=128))

        def S32(ch, h):
            return self.SST32[:, ch, :]

        def S16(ch, h):
            return self.SST16[:, ch, :]

        for l in range(cfg.depth):
            def mk(src, dst, chan):
                def pre(h):
                    P.dma("pool", "sst", self.SST32, I[src][l, :, h].rearrange("s p v -> p s v"))
                    P.copy("act", self.SST16, self.SST32)

                def post(h):
                    self.out_dma(chan, O[dst][l, :, h].rearrange("s p v -> p s v"), self.SST32, queue="pool")
                return pre, post
            self.rmsnorm_fm(XS, self.LNM[:, l, :], self.H, g.n)
            pre, post = mk("st_hgrn", "s_hgrn", "sst_o")
            self.hgrn(g, l, S32, S16, pre, post)
            if cfg.upto == "hgrn":
                self.dbg_out(f"soa_{l}", self.OABC[0], [128, 4, TT])
                continue
            self.attn_sample(g, l)
            if cfg.upto == "attn":
                self.dbg_out(f"sob_{l}", self.OABC[1], [128, 4, TT])
                continue
            pre, post = mk("st_gdn", "s_gdn", "sst_o")
            self.gdn(g, l, S32, S16,
                     hist=lambda kc: I["st_gconv"][l, :, kc * 128:(kc + 1) * 128, :].rearrange("s p j -> p s j"),
                     tails=lambda kc: ("sgc", O["s_gconv"][l, :, kc * 128:(kc + 1) * 128, :].rearrange("s p j -> p s j")),
                     pre=pre, post=post)
            if cfg.upto == "gdn":
                self.dbg_out(f"soc_{l}", self.OABC[2], [128, 4, TT])
                continue
            self.merge_out(g, l, XS)
            self.ffn(g, l, XS,
                     hist=lambda j: I["st_fconv"][l, :, j * 128:(j + 1) * 128, :].rearrange("s p j -> p s j"),
                     tails=lambda j: ("sfc", O["s_fconv"][l, :, j * 128:(j + 1) * 128, :].rearrange("s p j -> p s j")))
        if cfg.upto == "all":
            self.out_dma("ys", O["yT_s"].rearrange("(c p) t -> p c t", p=128), XS)

    def run(self):
        cfg, P, I = self.cfg, self.P, self.I
        with contextlib.ExitStack() as st:
            self.st = st
            self.setup()
            self.alloc_mixers()
            self.precast_weights()
            XR = self.sb("XR", [128, 8, TT])
            if cfg.samples:
                self.alloc_samples()
                self.run_samples()
            if cfg.prompt:
                for t in range(cfg.seq // TT):
                    g = Grp("p", TT, 1, 64, 8, tile=t)
                    P.dma("sp", "x", XR, I["xT_p"][:, t * TT:(t + 1) * TT].rearrange("(c p) t -> p c t", p=128))
                    for l in range(cfg.depth):
                        self.rmsnorm_fm(XR, self.LNM[:, l, :], self.H, g.n)
                        self.hgrn(g, l, lambda ch, h, l=l: self.HS32[:, l, h, :],
                                  lambda ch, h, l=l: self.HS16[:, l, h, :])
                        if cfg.upto == "hgrn":
                            self.dbg_out(f"oa_{t}_{l}", self.OABC[0], [128, 4, TT])
                            continue
                        self.attn_prompt(g, l)
                        if cfg.upto == "attn":
                            self.dbg_out(f"ob_{t}_{l}", self.OABC[1], [128, 4, TT])
                            continue
                        self.gdn(g, l, lambda ch, h, l=l: self.GS32[:, l, h, :],
                                 lambda ch, h, l=l: self.GS16[:, l, h, :])
                        if cfg.upto == "gdn":
                            self.dbg_out(f"oc_{t}_{l}", self.OABC[2], [128, 4, TT])
                            continue
                        self.merge_out(g, l, XR)
                        self.ffn(g, l, XR)
                    if cfg.upto == "all":
                        self.out_dma("yp", self.O["yT_p"][:, t * TT:(t + 1) * TT].rearrange("(c p) t -> p c t", p=128), XR, queue="pool")
                if cfg.upto == "all":
                    for l in range(cfg.depth):
                        self.out_dma("pst", self.O["p_hgrn"][l].rearrange("h p v -> p h v"), self.HS32[:, l])
                        self.out_dma("pst", self.O["p_gdn"][l].rearrange("h p v -> p h v"), self.GS32[:, l])
                        self.out_dma("pst", self.O["p_gconv"][l].rearrange("(k p) j -> p k j", p=128), self.TAILG[:, l])
                        self.out_dma("pst", self.O["p_fconv"][l].rearrange("(k p) j -> p k j", p=128), self.TAILF[:, l])
                if cfg.upto == "hgrn":
                    self.dbg_out("hs", self.HS32, [128, cfg.depth, 4, 128])
                if cfg.upto == "gdn":
                    self.dbg_out("gs", self.GS32, [128, cfg.depth, 4, 128])
                    self.dbg_out("tailg", self.TAILG, [128, DEPTH, 12, 3])
            P.emit(final_chans=self.out_chans)
        return self.nc


def _host_inputs(inp):
    c = np.ascontiguousarray
    f = np.float32
    shared = {k: c(np.asarray(inp[k], dtype=f)) for k in
              ("cache_k", "cache_v", "w_in", "w_branch_a", "w_branch_b", "w_branch_c", "w_out", "w_up", "w_down")}
    shared["params"] = _pack_params(inp)
    shared["consts"] = _const_table()
    shared["rope_p"] = _rope_table(np.arange(SEQ))
    shared["rope_s"] = _rope_table(np.tile(PAST + np.arange(SL), SPC))
    xp = np.asarray(inp["x_prompt"], dtype=f)
    xs = np.asarray(inp["x_sample"], dtype=f)
    maps = []
    for core in range(NCORES):
        b = core // 2
        s0 = core * SPC
        m = dict(shared)
        m["xT_p"] = c(xp[b].T)
        m["xT_s"] = c(xs[s0:s0 + SPC].reshape(ST, D).T)
        m["st_hgrn"] = c(np.asarray(inp["state_hgrn"], f)[:, s0:s0 + SPC])
        m["st_gdn"] = c(np.asarray(inp["state_gdn"], f)[:, s0:s0 + SPC])
        m["st_gconv"] = c(np.asarray(inp["state_gdn_conv"], f)[:, s0:s0 + SPC].transpose(0, 1, 3, 2))
        m["st_fconv"] = c(np.asarray(inp["state_ffn_conv"], f)[:, s0:s0 + SPC].transpose(0, 1, 3, 2))
        m["page_table"] = c(np.asarray(inp["page_table"], np.int32)[s0:s0 + SPC])
        maps.append(m)
    return maps


def _host_outputs(res):
    r = res
    f = np.float32
    y_p = np.stack([r[2 * b]["yT_p"].T for b in range(NB)]).astype(f)
    y_s = np.concatenate([r[ci]["yT_s"].T.reshape(SPC, SL, D) for ci in range(NCORES)]).astype(f)
    p_hgrn = np.stack([r[2 * b]["p_hgrn"] for b in range(NB)], axis=1).astype(f)
    p_k = np.stack([r[2 * b]["p_kT"].transpose(0, 3, 1, 2) for b in range(NB)], axis=1).astype(f)
    p_v = np.stack([r[2 * b]["p_v"].reshape(DEPTH, SEQ, NH, 128) for b in range(NB)], axis=1).astype(f)
    p_gdn = np.stack([r[2 * b]["p_gdn"] for b in range(NB)], axis=1).astype(f)
    p_gc = np.stack([r[2 * b]["p_gconv"].transpose(0, 2, 1) for b in range(NB)], axis=1).astype(f)
    p_fc = np.stack([r[2 * b]["p_fconv"].transpose(0, 2, 1) for b in range(NB)], axis=1).astype(f)
    s_hgrn = np.concatenate([r[ci]["s_hgrn"] for ci in range(NCORES)], axis=1).astype(f)
    s_k = np.concatenate([r[ci]["s_kT"].transpose(0, 3, 1, 2).reshape(DEPTH, SPC, SL, NH, 128)
                          for ci in range(NCORES)], axis=1).astype(f)
    s_v = np.concatenate([r[ci]["s_v"].reshape(DEPTH, SPC, SL, NH, 128) for ci in range(NCORES)], axis=1).astype(f)
    s_gdn = np.concatenate([r[ci]["s_gdn"] for ci in range(NCORES)], axis=1).astype(f)
    s_gc = np.concatenate([r[ci]["s_gconv"].transpose(0, 1, 3, 2) for ci in range(NCORES)], axis=1).astype(f)
    s_fc = np.concatenate([r[ci]["s_fconv"].transpose(0, 1, 3, 2) for ci in range(NCORES)], axis=1).astype(f)
    return (y_p, y_s, p_hgrn, p_k, p_v, p_gdn, p_gc, p_fc, s_hgrn, s_k, s_v, s_gdn, s_gc, s_fc)


def kernel(**inputs):
    nc = Builder(Cfg()).run()
    in_maps = _host_inputs(inputs)
    res = run_bass_kernel_spmd(nc, in_maps, core_ids=list(range(NCORES)))
    return _host_outputs(res.results)
```

```python
import contextlib
import os
import sys

import numpy as np
import concourse.bass as bass
import concourse.mybir as mybir
from concourse.bass_utils import run_bass_kernel_spmd

F32 = mybir.dt.float32
BF16 = mybir.dt.bfloat16
I32 = mybir.dt.int32
AF = mybir.ActivationFunctionType
ALU = mybir.AluOpType
AX = mybir.AxisListType

SEM_CAP = 2048
SEM_POOL_LIMIT = 96

ENGINES = ("pe", "act", "dve", "pool", "sp")
RSQRT_VIA_LN = os.environ.get("RSQRT_LN", "1") == "1"


class View:
    __slots__ = ("ap", "toks")

    def __init__(self, ap, toks):
        self.ap = ap
        self.toks = tuple(toks)

    def __getitem__(self, idx):
        return View(self.ap[idx], self.toks)

    def with_ap(self, ap):
        return View(ap, self.toks)

    def bc(self, shape):
        return View(self.ap.broadcast_to(list(shape)), self.toks)

    def r3(self, b):
        return View(self.ap.rearrange("p (a b) -> p a b", b=b), self.toks)

    def bitcast(self, dt):
        return View(self.ap.bitcast(dt), self.toks)

    def rs(self, pattern, **kw):
        return View(self.ap.rearrange(pattern, **kw), self.toks)


class _Op:
    __slots__ = ("eng", "fn", "deps", "chan", "ndma", "event", "needed", "idx", "where")

    def __init__(self, eng, fn, chan=None, ndma=0):
        self.eng = eng
        self.fn = fn
        self.deps = set()
        self.chan = chan
        self.ndma = ndma
        self.event = None
        self.needed = False
        self.idx = -1


class Prog:
    def __init__(self, nc):
        self.nc = nc
        self.ops = []
        self.streams = {e: [] for e in ENGINES}
        self.last_w = {}
        self.readers = {}
        self.chan_last = {}
        self.barrier_chans = set()
        self.ntok = 0

    def tok(self, name):
        self.ntok += 1
        return (name, self.ntok)

    def view(self, ap, name, n=1):
        return View(ap, [self.tok(name) for _ in range(n)])

    def op(self, eng, fn, reads=(), writes=(), chan=None, ndma=0):
        o = _Op(eng, fn, chan, ndma)
        for v in reads:
            if v is None or not isinstance(v, View):
                continue
            for t in v.toks:
                w = self.last_w.get(t)
                if w is not None:
                    o.deps.add(w)
        for v in writes:
            for t in v.toks:
                w = self.last_w.get(t)
                if w is not None:
                    o.deps.add(w)
                for r in self.readers.get(t, ()):
                    o.deps.add(r)
        if chan is not None:
            prev = self.chan_last.get(chan)
            if prev is not None and chan not in self.barrier_chans:
                o.deps.add(prev)
            self.chan_last[chan] = o
        o.deps.discard(o)
        if eng == "pe":
            o.deps = {d for d in o.deps if d.eng != "pe"}
        for v in reads:
            if v is None or not isinstance(v, View):
                continue
            for t in v.toks:
                self.readers.setdefault(t, []).append(o)
        for v in writes:
            for t in v.toks:
                self.last_w[t] = o
                self.readers[t] = []
        o.idx = len(self.ops)
        fr = sys._getframe(2)
        o.where = f"{fr.f_code.co_name}:{fr.f_lineno}"
        self.ops.append(o)
        self.streams[eng].append(o)
        return o

    def emit(self, final_chans=()):
        nc = self.nc
        for o in self.ops:
            for d in o.deps:
                d.needed = True
        tails = [self.chan_last[c] for c in final_chans if c in self.chan_last]
        for t in tails:
            t.needed = True

        semkeys = []
        cnt = {}
        gen = {}

        def bump(base, step):
            g = gen.get(base, 0)
            c = cnt.get(base, 0)
            if c + step > SEM_CAP:
                g += 1
                c = 0
            c += step
            gen[base] = g
            cnt[base] = c
            key = (base, g)
            if not semkeys or key not in seen:
                seen.add(key)
                semkeys.append(key)
            return key, c

        seen = set()
        for e in ENGINES:
            for o in self.streams[e]:
                if o.chan is not None:
                    key = None
                    val = 0
                    if cnt.get(("c", o.chan), 0) + 16 * o.ndma > SEM_CAP:
                        cnt[("c", o.chan)] = SEM_CAP
                    for _ in range(o.ndma):
                        key, val = bump(("c", o.chan), 16)
                    o.event = (key, val)
                elif o.needed:
                    o.event = bump(("e", e), 1)
        finals = {}
        for o in self.ops:
            if o.chan in self.barrier_chans:
                k = o.event[0]
                finals[k] = max(finals.get(k, 0), o.event[1])
        assert len(semkeys) <= SEM_POOL_LIMIT, f"semaphore pool exhausted: {len(semkeys)}"
        self.n_sems = len(semkeys)

        with contextlib.ExitStack() as st:
            sems = {}
            for i, k in enumerate(semkeys):
                sems[k] = st.enter_context(nc.semaphore(f"s{i}"))
            block = st.enter_context(nc.Block())
            handles = {"pe": block.tensor, "act": block.scalar, "dve": block.vector,
                       "pool": block.gpsimd, "sp": block.sync}

            def make(ename):
                stream = self.streams[ename]

                def body(eng):
                    known = {}
                    for o in stream:
                        need = {}
                        for d in o.deps:
                            k, v = d.event
                            if d.chan in self.barrier_chans:
                                v = finals[k]
                            if known.get(k, 0) < v:
                                need[k] = max(need.get(k, 0), v)
                        for k, v in need.items():
                            eng.wait_ge(sems[k], v)
                            known[k] = v
                        try:
                            if o.chan is not None:
                                o.fn(eng, sems[o.event[0]])
                            else:
                                ins = o.fn(eng)
                                if o.needed:
                                    ins.then_inc(sems[o.event[0]], 1)
                        except Exception as exc:
                            raise RuntimeError(f"op #{o.idx} on {ename} recorded at {o.where}: {exc}") from exc
                    if ename == "sp":
                        for t in tails:
                            k, v = t.event
                            if known.get(k, 0) < v:
                                eng.wait_ge(sems[k], v)
                                known[k] = v
                return body

            for ename in ENGINES:
                if self.streams[ename] or ename == "sp":
                    handles[ename](make(ename))

    @staticmethod
    def _a(x):
        return x.ap if isinstance(x, View) else x

    def mm(self, out, pairs, extra_reads=(), start=True, stop=True):
        aps = [(l.ap, r.ap) for l, r in pairs]
        n = len(aps)
        oap = out.ap

        def fn(eng):
            ins = None
            for i, (l, r) in enumerate(aps):
                ins = eng.matmul(oap, l, r, start=(start and i == 0), stop=(stop and i == n - 1))
            return ins
        return self.op("pe", fn, reads=[v for p in pairs for v in p] + list(extra_reads),
                       writes=[out])

    def transpose(self, out, in_, ident):
        def fn(eng):
            return eng.transpose(out.ap, in_.ap, ident.ap)
        return self.op("pe", fn, reads=[in_, ident], writes=[out])

    def act(self, out, in_, func, bias=0.0, scale=1.0, accum=None):
        b, s = self._a(bias), self._a(scale)
        kw = {}
        if accum is not None:
            kw["accum_out"] = accum.ap

        def fn(eng):
            return eng.activation(out.ap, in_.ap, func, bias=b, scale=s, **kw)
        return self.op("act", fn, reads=[in_, bias, scale],
                       writes=[out] + ([accum] if accum is not None else []))

    def tt(self, eng_name, out, in0, in1, op):
        def fn(eng):
            return eng.tensor_tensor(out.ap, in0.ap, in1.ap, op)
        return self.op(eng_name, fn, reads=[in0, in1], writes=[out])

    def ts(self, eng_name, out, in0, s1, op0, s2=None, op1=None, accum=None):
        a1, a2 = self._a(s1), self._a(s2)
        kw = {}
        if op1 is not None:
            kw["op1"] = op1
        if accum is not None:
            kw["accum_out"] = accum.ap

        def fn(eng):
            return eng.tensor_scalar(out.ap, in0.ap, a1, a2, op0, **kw)
        return self.op(eng_name, fn, reads=[in0, s1, s2],
                       writes=[out] + ([accum] if accum is not None else []))

    def stt(self, eng_name, out, in0, scalar, in1, op0, op1):
        sc = self._a(scalar)

        def fn(eng):
            return eng.scalar_tensor_tensor(out.ap, in0.ap, sc, in1.ap, op0, op1)
        return self.op(eng_name, fn, reads=[in0, scalar, in1], writes=[out])

    def copy(self, eng_name, out, in_):
        if eng_name == "act":
            def fn(eng):
                return eng.copy(out.ap, in_.ap)
        else:
            def fn(eng):
                return eng.tensor_copy(out.ap, in_.ap)
        return self.op(eng_name, fn, reads=[in_], writes=[out])

    def memset(self, eng_name, out, val):
        def fn(eng):
            return eng.memset(out.ap, val)
        return self.op(eng_name, fn, writes=[out])

    def reduce(self, eng_name, out, in_, op, axis=None):
        ax = AX.X if axis is None else axis

        def fn(eng):
            return eng.tensor_reduce(out.ap, in_.ap, ax, op)
        return self.op(eng_name, fn, reads=[in_], writes=[out])

    def dma(self, queue, chan, out, in_, **kw):
        oap, iap = self._a(out), self._a(in_)

        def fn(eng, sem):
            return eng.dma_start(out=oap, in_=iap, **kw).then_inc(sem, 16)
        return self.op(queue, fn, reads=[in_] if isinstance(in_, View) else [],
                       writes=[out] if isinstance(out, View) else [], chan=chan, ndma=1)

    def recip(self, out, in_):
        def fn(eng):
            return eng.reciprocal(out.ap, in_.ap)
        return self.op("dve", fn, reads=[in_], writes=[out])

    def recip_pos(self, out, in_):
        self.act(out, in_, AF.Ln)
        return self.act(out, out, AF.Exp, scale=-1.0)

    def rsqrt(self, out, in_):
        if RSQRT_VIA_LN:
            self.act(out, in_, AF.Ln)
            return self.act(out, out, AF.Exp, scale=-0.5)
        self.recip(out, in_)
        return self.act(out, out, AF.Sqrt)

    def mm_multi(self, out, groups, reads, start=True, stop=True):
        def fn(eng):
            ins = None
            for oap, pairs in groups:
                n = len(pairs)
                for i, (l, r) in enumerate(pairs):
                    ins = eng.matmul(oap, l, r, start=(start and i == 0), stop=(stop and i == n - 1))
            return ins
        return self.op("pe", fn, reads=list(reads), writes=[out])

    def transpose_multi(self, out, items, reads):
        def fn(eng):
            ins = None
            for oap, iap, idap in items:
                ins = eng.transpose(oap, iap, idap)
            return ins
        return self.op("pe", fn, reads=list(reads), writes=[out])

    def scan(self, out, data0, data1, initial=0.0, op0=None, op1=None):
        o0 = ALU.mult if op0 is None else op0
        o1 = ALU.add if op1 is None else op1

        def fn(eng):
            return eng.tensor_tensor_scan(out.ap, data0.ap, data1.ap, initial, o0, o1)
        return self.op("dve", fn, reads=[data0, data1], writes=[out])


D = 1024
SEQ = 4096
DEPTH = 2
NB = 4
SB = 32
SL = 8
PAST = 8192
PAGE = 128
NPAGES = PAST // PAGE
NPOOL = 2560
NH = 4
DFF = 2816
NIN = 8712
GCH = 1536
EPS = 1e-6
TT = 512
NCORES = 8
SPC = SB // NCORES
ST = SPC * SL
O_HQ, O_HF, O_HI, O_HOG = 0, 512, 1024, 1536
O_DQ, O_DK, O_DV = 2048, 2560, 3072
O_GQ, O_GK, O_GV, O_GZ = 3584, 4096, 4608, 5120
O_GB, O_GA = 5632, 5636
O_GA_, O_GTA, O_GTB, O_GTC = 5636, 5640, 6664, 7688


def _decl(nc, name, shape, dt, kind):
    return nc.dram_tensor(name, list(shape), dt, kind=kind).ap()


IN_SPECS = [
    ("xT_p", (D, SEQ)), ("xT_s", (D, ST)),
    ("st_hgrn", (DEPTH, SPC, NH, 128, 128)), ("st_gdn", (DEPTH, SPC, NH, 128, 128)),
    ("st_gconv", (DEPTH, SPC, GCH, 3)), ("st_fconv", (DEPTH, SPC, DFF, 2)),
    ("cache_k", (DEPTH, NPOOL, PAGE, NH, 128)), ("cache_v", (DEPTH, NPOOL, PAGE, NH, 128)),
    ("w_in", (DEPTH, D, NIN)),
    ("w_branch_a", (DEPTH, 512, D)), ("w_branch_b", (DEPTH, 512, D)), ("w_branch_c", (DEPTH, 512, D)),
    ("w_out", (DEPTH, D, D)), ("w_up", (DEPTH, D, 2 * DFF)), ("w_down", (DEPTH, DFF, D)),
]
OUT_SPECS = [
    ("yT_p", (D, SEQ)), ("yT_s", (D, ST)),
    ("p_hgrn", (DEPTH, NH, 128, 128)), ("p_kT", (DEPTH, NH, 128, SEQ)), ("p_v", (DEPTH, SEQ, 512)),
    ("p_gdn", (DEPTH, NH, 128, 128)), ("p_gconv", (DEPTH, GCH, 3)), ("p_fconv", (DEPTH, DFF, 2)),
    ("s_hgrn", (DEPTH, SPC, NH, 128, 128)), ("s_kT", (DEPTH, NH, 128, ST)), ("s_v", (DEPTH, ST, 512)),
    ("s_gdn", (DEPTH, SPC, NH, 128, 128)), ("s_gconv", (DEPTH, SPC, GCH, 3)),
    ("s_fconv", (DEPTH, SPC, DFF, 2)),
]


C_ID, C_ONES, C_BLK, C_ROT, C_M128 = 0, 128, 256, 384, 512
C_MI64, C_MS64, C_MI8, C_MS8, C_RST512, C_RST32, C_IOTA = 640, 1152, 1664, 1696, 1728, 2240, 2272
C_LS64, C_LS8, C_BD32 = 2273, 2785, 2817
NCONST = 2849


def _const_table():
    t = np.zeros((128, NCONST), np.float32)
    p = np.arange(128)
    t[:, C_ID:C_ID + 128] = np.eye(128)
    t[:, C_ONES:C_ONES + 128] = 1.0
    t[:, C_BLK:C_BLK + 128] = (p[:, None] // 64 == p[None, :] // 64)
    rot = np.zeros((128, 128), np.float32)
    for m in range(128):
        if m % 64 < 32:
            rot[m + 32, m] = -1.0
        else:
            rot[m - 32, m] = 1.0
    t[:, C_ROT:C_ROT + 128] = rot
    t[:, C_M128:C_M128 + 128] = (p[None, :] >= p[:, None])
    s64 = np.arange(64)
    mi = (s64[None, :] >= s64[:, None]).astype(np.float32)
    ms = (s64[None, :] > s64[:, None]).astype(np.float32)
    t[:64, C_MI64:C_MI64 + 512] = np.tile(mi, (1, 8))
    t[:64, C_MS64:C_MS64 + 512] = np.tile(ms, (1, 8))
    s8 = np.arange(8)
    t[:8, C_MI8:C_MI8 + 32] = np.tile((s8[None, :] >= s8[:, None]).astype(np.float32), (1, 4))
    t[:8, C_MS8:C_MS8 + 32] = np.tile((s8[None, :] > s8[:, None]).astype(np.float32), (1, 4))
    t[:64, C_LS64:C_LS64 + 512] = np.tile((s64[None, :] < s64[:, None]).astype(np.float32), (1, 8))
    t[:8, C_LS8:C_LS8 + 32] = np.tile((s8[None, :] < s8[:, None]).astype(np.float32), (1, 4))
    i32 = np.arange(32)
    t[:32, C_BD32:C_BD32 + 32] = ((i32[:, None] // 8 == i32[None, :] // 8) & (i32[:, None] % 8 <= i32[None, :] % 8))
    t[:, C_RST512:C_RST512 + 512] = (np.arange(512) % 64 != 0)
    t[:, C_RST32:C_RST32 + 32] = (np.arange(32) % 8 != 0)
    t[:, C_IOTA] = p
    return t


def _rope_table(pos):
    inv = 1.0 / (10000.0 ** (np.arange(0, 64, 2, dtype=np.float32) / 64.0))
    f = inv[(np.arange(128) % 64) % 32].astype(np.float32)
    ang = f[:, None] * np.asarray(pos, np.float32)[None, :]
    return np.stack([np.cos(ang), np.sin(ang)]).astype(np.float32)


P_LNM, P_LNF, P_LBR, P_HN, P_SUBLN, P_GN, P_QKG = 0, 16, 32, 40, 42, 44, 46
P_LAMR, P_GCW, P_FCW, P_GA, P_GDT, NPRM = 50, 562, 658, 790, 798, 806


def _pack_params(inp):
    f = np.float32
    t = np.zeros((128, NPRM), f)
    g = lambda k: np.asarray(inp[k], f)
    t[:, P_LNM:P_LNM + 16] = g("ln_mix").reshape(DEPTH, 8, 128).transpose(2, 0, 1).reshape(128, 16)
    t[:, P_LNF:P_LNF + 16] = g("ln_ffn").reshape(DEPTH, 8, 128).transpose(2, 0, 1).reshape(128, 16)
    t[:, P_LBR:P_LBR + 8] = g("hgrn_lb").reshape(DEPTH, 4, 128).transpose(2, 0, 1).reshape(128, 8)
    t[:, P_HN:P_HN + 2] = g("hgrn_norm").T
    t[:, P_SUBLN:P_SUBLN + 2] = g("diff_subln").T
    t[:, P_GN:P_GN + 2] = g("gdn_norm").T
    qk = g("diff_qk_norm").transpose(2, 0, 1).reshape(64, 4)
    t[:, P_QKG:P_QKG + 4] = np.concatenate([qk, qk], 0)
    t[:, P_LAMR:P_LAMR + 512] = g("diff_lambda").reshape(1, 512)
    t[:, P_GCW:P_GCW + 96] = g("gdn_conv").reshape(DEPTH, 4, 12, 128).transpose(3, 0, 2, 1).reshape(128, 96)
    t[:, P_FCW:P_FCW + 132] = g("ffn_conv").reshape(DEPTH, 3, 22, 128).transpose(3, 0, 2, 1).reshape(128, 132)
    t[:, P_GA:P_GA + 8] = g("gdn_a_log").reshape(1, 8)
    t[:, P_GDT:P_GDT + 8] = g("gdn_dt_bias").reshape(1, 8)
    return t


class Cfg:
    def __init__(self, seq=SEQ, npool=NPOOL, depth=DEPTH, prompt=True, samples=True, upto="all",
                 dbg=()):
        self.seq, self.npool, self.depth = seq, npool, depth
        self.prompt, self.samples, self.upto, self.dbg = prompt, samples, upto, tuple(dbg)


class Grp:
    def __init__(self, kind, n, nseq, c, nch, tile=0):
        self.kind, self.n, self.nseq, self.c, self.nch, self.tile = kind, n, nseq, c, nch, tile
        self.L = n // nseq


class Builder:
    def __init__(self, cfg):
        self.cfg = cfg
        nc = bass.Bass("TRN2", target_bir_lowering=False)
        nc.allow_low_precision("bf16 matmul operands with fp32 PSUM accumulation (problem statement)")
        self.nc = nc
        self.P = Prog(nc)
        self.P.barrier_chans.add("const")
        specs = dict(IN_SPECS)
        specs["xT_p"] = (D, cfg.seq)
        specs["cache_k"] = (DEPTH, cfg.npool, PAGE, NH, 128)
        specs["cache_v"] = (DEPTH, cfg.npool, PAGE, NH, 128)
        self.I = {n: _decl(nc, n, s, F32, "ExternalInput") for n, s in specs.items()}
        self.I["page_table"] = _decl(nc, "page_table", (SPC, NPAGES), I32, "ExternalInput")
        self.I["consts"] = _decl(nc, "consts", (128, NCONST), F32, "ExternalInput")
        self.I["params"] = _decl(nc, "params", (128, NPRM), F32, "ExternalInput")
        self.I["rope_p"] = _decl(nc, "rope_p", (2, 128, cfg.seq), F32, "ExternalInput")
        self.I["rope_s"] = _decl(nc, "rope_s", (2, 128, ST), F32, "ExternalInput")
        ospecs = dict(OUT_SPECS)
        ospecs["yT_p"] = (D, cfg.seq)
        ospecs["p_kT"] = (DEPTH, NH, 128, cfg.seq)
        ospecs["p_v"] = (DEPTH, cfg.seq, 512)
        self.O = {n: _decl(nc, n, s, F32, "ExternalOutput") for n, s in ospecs.items()}
        self.dbg = {}
        self.out_chans = []
        self.wslot = 0
        self.psi = 0

    def sb(self, name, shape, dt=F32):
        return self.P.view(self.st.enter_context(self.nc.sbuf_tensor(name, list(shape), dt))[:], name)

    def psum(self, name, shape, dt=F32):
        return self.P.view(self.st.enter_context(self.nc.psum_tensor(name, list(shape), dt))[:], name)

    def ps(self):
        v = self.PS[self.psi % 4]
        self.psi += 1
        return v

    def dbg_out(self, name, view, shape):
        if name not in self.cfg.dbg:
            return
        t = _decl(self.nc, "dbg_" + name, shape, view.ap.dtype, "ExternalOutput")
        self.P.dma("sp", "dbg_" + name, t, view)
        self.out_chans.append("dbg_" + name)

    def out_dma(self, chan, dst, src, queue="sp"):
        if chan not in self.out_chans:
            self.out_chans.append(chan)
        self.P.dma(queue, chan, dst, src)

    def setup(self):
        P, I = self.P, self.I
        dp = DEPTH
        self.CONST = self.sb("CONST", [128, NCONST])
        P.dma("sp", "const", self.CONST, I["consts"])
        C = self.CONST
        self.ID = C[:, C_ID:C_ID + 128]
        self.ONES = C[:, C_ONES:C_ONES + 128]
        self.BLK = C[:, C_BLK:C_BLK + 128]
        self.ROT = C[:, C_ROT:C_ROT + 128]
        self.M128 = C[:, C_M128:C_M128 + 128]
        self.IDB = self.sb("IDB", [128, 128], BF16)
        P.copy("dve", self.IDB, self.ID)
        self.ONESB = self.sb("ONESB", [128, 128], BF16)
        P.copy("dve", self.ONESB, self.ONES)
        self.PS = [self.psum(f"PS{i}", [128, 512]) for i in range(8)]
        PRM = self.sb("PRM", [128, NPRM])
        P.dma("sp", "const", PRM, I["params"])
        self.LNM = PRM[:, P_LNM:P_LNM + 16].rs("p (l c) -> p l c", l=dp)
        self.LNF = PRM[:, P_LNF:P_LNF + 16].rs("p (l c) -> p l c", l=dp)
        LBR = PRM[:, P_LBR:P_LBR + 8].rs("p (l h) -> p l h", l=dp)
        self.HN = PRM[:, P_HN:P_HN + 2]
        self.SUBLN = PRM[:, P_SUBLN:P_SUBLN + 2]
        self.GN = PRM[:, P_GN:P_GN + 2]
        self.QKG = PRM[:, P_QKG:P_QKG + 4].rs("p (l j) -> p l j", l=dp)
        LAMR = PRM[:, P_LAMR:P_LAMR + 512].rs("p (l j d) -> p l j d", l=dp, j=4)
        self.GCW = PRM[:, P_GCW:P_GCW + 96].rs("p (l c j) -> p l c j", l=dp, c=12)
        self.FCW = PRM[:, P_FCW:P_FCW + 132].rs("p (l c j) -> p l c j", l=dp, c=22)
        self.GDT = PRM[:, P_GDT:P_GDT + 8]
        self.GA = self.sb("GA", [128, dp * 4])
        self.LB = self.sb("LB", [128, dp, 4])
        self.LB1M = self.sb("LB1M", [128, dp, 4])
        E = self.sb("LBE", [128, dp, 4])
        P.act(E, LBR, AF.Exp)
        S = self.sb("LBS", [128, 4])
        P.tt("dve", S, E[:, 0, :], E[:, 1, :], ALU.add)
        P.recip(S, S)
        P.memset("dve", self.LB[:, 0, :], 0.0)
        P.tt("dve", self.LB[:, 1, :], E[:, 1, :], S, ALU.mult)
        P.ts("dve", self.LB1M, self.LB, -1.0, ALU.mult, 1.0, ALU.add)
        self.LAM = self.sb("LAM", [128, dp])
        self.NLAM = self.sb("NLAM", [128, dp])
        PR = self.sb("LAMP", [128, dp, 2, 64])
        SM = self.sb("LAMS", [128, dp, 2])
        for l in range(dp):
            P.tt("dve", PR[:, l, 0, :], LAMR[:, l, 0, :], LAMR[:, l, 1, :], ALU.mult)
            P.tt("dve", PR[:, l, 1, :], LAMR[:, l, 2, :], LAMR[:, l, 3, :], ALU.mult)
            for j in range(2):
                P.reduce("dve", SM[:, l, j:j + 1], PR[:, l, j, :], ALU.add)
        P.act(SM, SM, AF.Exp)
        for l in range(dp):
            lam_init = 0.8 - 0.6 * float(np.exp(-0.3 * l))
            P.tt("dve", self.LAM[:, l:l + 1], SM[:, l, 0:1], SM[:, l, 1:2], ALU.subtract)
            P.ts("dve", self.LAM[:, l:l + 1], self.LAM[:, l:l + 1], lam_init, ALU.add)
        P.ts("dve", self.NLAM, self.LAM, -1.0, ALU.mult)
        P.act(self.GA, PRM[:, P_GA:P_GA + 8], AF.Exp)
        P.ts("dve", self.GA, self.GA, -1.0, ALU.mult)
        self.NW = 4
        self.W16 = [self.sb(f"W16_{i}", [128, 4096], BF16) for i in range(self.NW)]
        self.H = self.sb("H", [128, 8, TT], BF16)
        self.FT = [self.sb(f"FT{i}", [128, TT]) for i in range(10)]
        self.BT = [self.sb(f"BT{i}", [128, TT], BF16) for i in range(6)]
        scr = self.st.enter_context(self.nc.sbuf_tensor("SCR", [128, 3 * 4 * TT], F32))[:]
        stoks = [P.tok(f"SCR{i}") for i in range(3)]
        self.F4 = [View(scr[:, i * 4 * TT:(i + 1) * 4 * TT].rearrange("p (a b) -> p a b", a=4), [stoks[i]])
                   for i in range(3)]
        self.A16 = View(scr.bitcast(BF16)[:, 0:22 * TT].rearrange("p (a b) -> p a b", a=22), stoks)
        self.HST = View(scr[:, 2 * 4 * TT:3 * 4 * TT], [stoks[2]])
        self.OABC = [self.sb(f"O{x}", [128, 4, TT], BF16) for x in "abc"]

    def precast_weights(self):
        P, nc = self.P, self.nc
        self.WB = {}
        stage32 = [self.F4[0].rs("p a b -> p (a b)"), self.F4[1].rs("p a b -> p (a b)")]
        engs = ("act", "dve", "pool")
        i = 0
        for name in ("w_in", "w_branch_a", "w_branch_b", "w_branch_c", "w_out", "w_up", "w_down"):
            self.WB[name] = []
            for l in range(self.cfg.depth):
                src = self.I[name][l]
                rows, cols = src.shape
                t = nc.dram_tensor(f"wb_{name}_{l}", [rows, cols], BF16, kind="Internal").ap()
                rtoks = {r: P.tok(f"wb_{name}_{l}") for r in range(0, rows, 128)}
                wv = View(t, list(rtoks.values()))
                self.WB[name].append(wv)
                for r0 in range(0, rows, 128):
                    for c0 in range(0, cols, 2048):
                        w = min(2048, cols - c0)
                        s32 = stage32[i % 2][:, 0:w]
                        s16 = self.W16[i % self.NW][:, 0:w]
                        P.dma("sp", f"pcl{i % 2}", s32, src[r0:r0 + 128, c0:c0 + w])
                        P.copy(engs[i % 3], s16, s32)
                        P.dma("pool", f"pcs{i % self.NW}", View(t[r0:r0 + 128, c0:c0 + w], [rtoks[r0]]), s16)
                        i += 1

    def wpanel(self, wv, r0, nk, c0, ncols):
        P = self.P
        slot = self.wslot
        self.wslot = (self.wslot + 1) % self.NW
        w16 = self.W16[slot]
        v16 = w16.with_ap(w16.ap[:, 0:nk * ncols].rearrange("p (k n) -> p k n", k=nk))
        P.dma("sp", f"w{slot}", v16, wv[r0:r0 + nk * 128, c0:c0 + ncols].rs("(k p) n -> p k n", p=128))
        return v16

    def gemm_fm(self, w16, nk, rhs, n, consume, nm=None):
        ncols = w16.ap.shape[2]
        for m in range(nm if nm is not None else ncols // 128):
            pv = self.ps()[:, 0:n]
            self.P.mm(pv, [(w16[:, k, m * 128:(m + 1) * 128], rhs(k)) for k in range(nk)])
            consume(m, pv)

    def gemm_tm(self, w16, nk, lhs, g, consume):
        ncols = w16.ap.shape[2]
        for ch in range(g.nch):
            pv = self.ps()[0:g.c, 0:ncols]
            self.P.mm(pv, [(lhs(k, ch), w16[:, k, :]) for k in range(nk)])
            consume(ch, pv)

    def rmsnorm_fm(self, X, gain, Hout, n, nk=8, dim=D):
        P = self.P
        pv = self.ps()[:, 0:n]
        for k in range(nk):
            sq = self.FT[8 + k % 2][:, 0:n]
            P.act(sq, X[:, k, 0:n], AF.Square)
            P.mm(pv, [(self.ONES, sq)], start=(k == 0), stop=(k == nk - 1))
        rs = self.FT[7][:, 0:n]
        P.ts("dve", rs, pv, 1.0 / dim, ALU.mult, EPS, ALU.add)
        P.rsqrt(rs, rs)
        for k in range(nk):
            P.stt("dve", Hout[:, k, 0:n], X[:, k, 0:n], gain[:, k:k + 1], rs, ALU.mult, ALU.mult)


    def alloc_mixers(self):
        self.VT16 = self.sb("VT16", [64, 8, 512], BF16)
        self.KHT = self.sb("KHT", [64, 8, 128], BF16)
        dp = self.cfg.depth
        self.HS32 = self.sb("HS32", [128, dp, 4, 128])
        self.HS16 = self.sb("HS16", [128, dp, 4, 128], BF16)
        self.GS32 = self.sb("GS32", [128, dp, 4, 128])
        self.GS16 = self.sb("GS16", [128, dp, 4, 128], BF16)
        for t in (self.HS32, self.HS16, self.GS32, self.GS16):
            self.P.memset("pool", t, 0.0)
        self.KH16 = self.sb("KH16", [128, SEQ], BF16)
        self.VH16 = self.sb("VH16", [128, SEQ // 128, 128], BF16)
        self.ROPE = self.sb("ROPE", [128, 2, TT])
        self.VST = [self.sb(f"VST{i}", [128, 512]) for i in range(2)]
        self.KST = [self.sb(f"KST{i}", [128, TT]) for i in range(2)]
        self.Q16 = self.sb("Q16", [128, TT], BF16)
        self.vsti = 0
        self.TAILG = self.sb("TAILG", [128, DEPTH, 12, 3])
        self.TAILF = self.sb("TAILF", [128, DEPTH, 22, 2])
        self.P.memset("pool", self.TAILG, 0.0)
        self.P.memset("pool", self.TAILF, 0.0)
        self.CB = self.sb("CB", [128, 3 + TT])
        self.BG = self.sb("BG", [64, 8, 8])
        self.BETA = self.sb("BETA", [64, 8, 4])
        self.GTT = self.sb("GTT", [64, 8, 4])
        self.GT = self.sb("GT", [64, 8, 4])
        self.EGT = self.sb("EGT", [64, 8, 4])
        self.NBEG = self.sb("NBEG", [64, 8, 4])
        self.EH = self.sb("EH", [64, 8])
        self.KT16 = self.sb("KT16", [64, 8, 128], BF16)
        self.KHAT = self.sb("KHAT", [64, 8, 128], BF16)
        self.BV = self.sb("BV", [64, 8, 128])
        self.QKM = self.sb("QKM", [64, TT], BF16)
        self.RHS = [self.sb(f"RHS{i}", [64, 128]) for i in range(2)]
        self.U16 = [self.sb(f"U16_{i}", [64, 128], BF16) for i in range(2)]
        nt = self.cfg.seq // TT
        self.T_PK = [[[self.P.tok("pk") for _ in range(nt)] for h in range(4)] for l in range(DEPTH)]
        self.T_PV = [[self.P.tok("pv") for _ in range(nt)] for l in range(DEPTH)]

    def hgrn(self, g, l, S32, S16, pre=None, post=None):
        P = self.P
        n, c, nch = g.n, g.c, g.nch
        H, W = self.H, self.WB["w_in"][l]
        Q, SG, OG = self.F4
        VT, OA = self.VT16, self.OABC[0]
        C = self.CONST
        rst = C[:, C_RST512:C_RST512 + n] if g.kind == "p" else C[:, C_RST32:C_RST32 + n]
        mi0 = C_MI64 if g.kind == "p" else C_MI8
        maskT = C[0:c, mi0:mi0 + n]

        def Hn(k):
            return H[:, k, 0:n]

        def Hc(k, ch):
            return H[:, k, ch * c:(ch + 1) * c]

        w = self.wpanel(W, 0, 8, O_HI, 512)
        self.gemm_tm(w, 8, Hc, g, lambda ch, pv: P.copy("act", VT[0:c, ch, :], pv))
        for off, dst, fn in ((O_HQ, Q, AF.Silu), (O_HF, SG, AF.Sigmoid), (O_HOG, OG, AF.Silu)):
            w = self.wpanel(W, 0, 8, off, 512)
            self.gemm_fm(w, 8, Hn, n, lambda m, pv, dst=dst, fn=fn: P.act(dst[:, m, 0:n], pv, fn))
        cm = c // 2 - 1
        for h in range(4):
            fg, kk, lf, G, d1, EQ, e2 = [self.FT[i][:, 0:n] for i in range(7)]
            QG, QC, KC, KH = [self.BT[i][:, 0:n] for i in range(4)]
            AT = self.BT[4][0:c, 0:n]
            P.ts("dve", fg, SG[:, h, 0:n], self.LB1M[:, l, h:h + 1], ALU.mult, self.LB[:, l, h:h + 1], ALU.add)
            P.ts("dve", kk, fg, -1.0, ALU.mult, 1.0, ALU.add)
            P.act(lf, fg, AF.Ln)
            P.scan(G, rst, lf)
            G3 = G.r3(c)
            P.act(EQ, G, AF.Exp)
            P.tt("dve", QG, Q[:, h, 0:n], EQ, ALU.mult)
            P.tt("dve", d1.r3(c), G3, G3[:, :, cm:cm + 1].bc([128, nch, c]), ALU.subtract)
            P.act(e2, d1, AF.Exp)
            P.tt("dve", QC, Q[:, h, 0:n], e2, ALU.mult)
            P.act(e2, d1, AF.Exp, scale=-1.0)
            P.tt("dve", KC, kk, e2, ALU.mult)
            P.tt("dve", d1.r3(c), G3, G3[:, :, c - 1:c].bc([128, nch, c]), ALU.subtract)
            P.act(e2, d1, AF.Exp, scale=-1.0)
            P.tt("dve", KH, kk, e2, ALU.mult)
            PA = self.PS[4]
            P.mm_multi(PA, [(PA.ap[0:c, ch * c:(ch + 1) * c],
                             [(KC.ap[:, ch * c:(ch + 1) * c], QC.ap[:, ch * c:(ch + 1) * c])])
                            for ch in range(nch)], reads=[KC, QC])
            P.tt("dve", AT, PA[0:c, 0:n], maskT, ALU.mult)
            PTB = self.PS[7].bitcast(BF16)
            P.transpose_multi(PTB, [(PTB.ap[0:c, ch * 128:(ch + 1) * 128], KH.ap[:, ch * c:(ch + 1) * c],
                                     self.IDB.ap) for ch in range(nch)], reads=[KH, self.IDB])
            P.copy("act", self.KHT[0:c, 0:nch, :], PTB[0:c, 0:nch * 128].r3(128))
            PO = self.PS[5]
            if pre:
                pre(h)
            for ch in range(nch):
                cs = slice(ch * c, (ch + 1) * c)
                vch = VT[0:c, ch, h * 128:(h + 1) * 128]
                P.mm(PO[:, cs], [(S16(ch, h), QG[:, cs]), (vch, AT[:, cs])])
                su = self.PS[6][:, (ch % 4) * 128:(ch % 4 + 1) * 128]
                P.mm(su, [(self.KHT[0:c, ch, :], vch)])
                P.stt("dve", S32(ch, h), S32(ch, h), EQ[:, ch * c + c - 1:ch * c + c], su, ALU.mult, ALU.add)
                P.copy("act", S16(ch, h), S32(ch, h))
            if post:
                post(h)
            sq, rs, t1 = self.FT[8][:, 0:n], self.FT[7][:, 0:n], self.FT[9][:, 0:n]
            P.act(sq, PO[:, 0:n], AF.Square)
            pss = self.ps()[:, 0:n]
            P.mm(pss, [(self.ONES, sq)])
            P.ts("dve", rs, pss, 1.0 / 128, ALU.mult, EPS, ALU.add)
            P.rsqrt(rs, rs)
            P.stt("dve", t1, PO[:, 0:n], self.HN[:, l:l + 1], rs, ALU.mult, ALU.mult)
            P.tt("dve", OA[:, h, 0:n], t1, OG[:, h, 0:n], ALU.mult)

    def qknorm_rope(self, x, gain, out):
        P = self.P
        n = x.ap.shape[1]
        sq, rs, xn, t1 = [self.FT[i][:, 0:n] for i in (8, 7, 6, 5)]
        P.act(sq, x, AF.Square)
        pss = self.ps()[:, 0:n]
        P.mm(pss, [(self.BLK, sq)])
        P.ts("dve", rs, pss, 1.0 / 64, ALU.mult, EPS, ALU.add)
        P.rsqrt(rs, rs)
        P.stt("dve", xn, x, gain, rs, ALU.mult, ALU.mult)
        pr = self.ps()[:, 0:n]
        P.mm(pr, [(self.ROT, xn)])
        P.tt("dve", t1, xn, self.ROPE[:, 0, 0:n], ALU.mult)
        P.tt("dve", xn, pr, self.ROPE[:, 1, 0:n], ALU.mult)
        P.tt("dve", out, t1, xn, ALU.add)

    def attn_prompt(self, g, l):
        P = self.P
        n, t = g.n, g.tile
        H, W = self.H, self.WB["w_in"][l]
        DQ, DK = self.F4[0], self.F4[1]
        OB = self.OABC[1]
        tok0 = t * TT

        def Hn(k):
            return H[:, k, 0:n]

        P.dma("pool", "rope", self.ROPE[:, :, 0:n], self.I["rope_p"][:, :, tok0:tok0 + n].rearrange("a p t -> p a t"))
        for off, dst in ((O_DQ, DQ), (O_DK, DK)):
            w = self.wpanel(W, 0, 8, off, 512)
            self.gemm_fm(w, 8, Hn, n, lambda m, pv, dst=dst: P.copy("act", dst[:, m, 0:n], pv))
        w = self.wpanel(W, 0, 8, O_DV, 512)
        for b in range(n // 128):
            pv = self.ps()
            P.mm(pv, [(H[:, k, b * 128:(b + 1) * 128], w[:, k, :]) for k in range(8)])
            vs = self.VST[self.vsti % 2]
            self.vsti += 1
            P.copy("act", vs, pv)
            self.out_dma("pv", View(self.O["p_v"][l][tok0 + b * 128:tok0 + (b + 1) * 128, :], [self.T_PV[l][t]]), vs, queue="pool")
        nkb = (tok0 + n) // 128
        for h in range(4):
            ks = self.KST[h % 2][:, 0:n]
            self.qknorm_rope(DK[:, h, 0:n], self.QKG[:, l, 1:2], ks)
            self.out_dma(f"pk{h}", View(self.O["p_kT"][l, h][:, tok0:tok0 + n], [self.T_PK[l][h][t]]), ks, queue="pool")
            self.qknorm_rope(DQ[:, h, 0:n], self.QKG[:, l, 0:1], self.Q16[:, 0:n])
            P.copy("act", self.KH16[:, tok0:tok0 + n], ks)
            for p0 in range(0, tok0, 2048):
                p1 = min(tok0, p0 + 2048)
                st = self.HST[:, 0:p1 - p0]
                toks = [self.T_PK[l][h][j] for j in range(p0 // TT, (p1 - 1) // TT + 1)]
                P.dma("pool", "hist", st, View(self.O["p_kT"][l, h][:, p0:p1], toks))
                P.copy("pool", self.KH16[:, p0:p1], st)
            for p0 in range(0, tok0 + n, 2048):
                p1 = min(tok0 + n, p0 + 2048)
                nb = (p1 - p0) // 128
                st3 = self.HST[:, 0:nb * 128].rs("p (b d) -> p b d", d=128)
                toks = [self.T_PV[l][j] for j in range(p0 // TT, (p1 - 1) // TT + 1)]
                P.dma("pool", "hist", st3, View(self.O["p_v"][l][p0:p1, h * 128:(h + 1) * 128].rearrange("(b p) d -> p b d", p=128), toks))
                P.copy("pool", self.VH16[:, p0 // 128:p0 // 128 + nb, :], st3)
            acc = [self.PS[4], self.PS[5], self.PS[6], self.PS[7]]
            ei = 0
            for half in range(2):
                hs = slice(half * 64, half * 64 + 64)
                O_, Z_ = acc[2 * half], acc[2 * half + 1]
                for kb in range(nkb):
                    r = kb - tok0 // 128
                    qlo = max(0, r * 128)
                    sp = self.ps()[:, 0:n - qlo]
                    P.mm(sp, [(self.KH16[hs, kb * 128:(kb + 1) * 128], self.Q16[hs, qlo:n])])
                    e = self.BT[ei % 6][:, 0:n - qlo]
                    ei += 1
                    P.act(e, sp, AF.Exp, scale=0.125)
                    if r >= 0:
                        P.tt("dve", e[:, 0:128], e[:, 0:128], self.M128, ALU.mult)
                    P.mm(O_[:, qlo:n], [(self.VH16[:, kb, :], e)], start=(kb == 0), stop=(kb == nkb - 1))
                    P.mm(Z_[:, qlo:n], [(self.ONESB, e)], start=(kb == 0), stop=(kb == nkb - 1))
            r1, r2, o1, o2 = [self.FT[i][:, 0:n] for i in range(4)]
            P.recip(r1, acc[1][:, 0:n])
            P.recip(r2, acc[3][:, 0:n])
            P.tt("dve", o1, acc[0][:, 0:n], r1, ALU.mult)
            P.tt("dve", o2, acc[2][:, 0:n], r2, ALU.mult)
            P.stt("dve", o1, o2, self.NLAM[:, l:l + 1], o1, ALU.mult, ALU.add)
            self.subln(o1, l, OB[:, h, 0:n])

    def subln(self, od, l, out):
        P = self.P
        n = od.ap.shape[1]
        sq, rs, t1 = self.FT[8][:, 0:n], self.FT[7][:, 0:n], self.FT[9][:, 0:n]
        P.act(sq, od, AF.Square)
        pss = self.ps()[:, 0:n]
        P.mm(pss, [(self.ONES, sq)])
        P.ts("dve", rs, pss, 1.0 / 128, ALU.mult, EPS, ALU.add)
        P.rsqrt(rs, rs)
        P.stt("dve", t1, od, self.SUBLN[:, l:l + 1], rs, ALU.mult, ALU.mult)
        lam_init = 0.8 - 0.6 * float(np.exp(-0.3 * l))
        P.ts("dve", out, t1, 1.0 - lam_init, ALU.mult)

    def conv_fm(self, g, pv, wcol, tail, ntap, dst, hist_src=None, tail_dst=None):
        P = self.P
        L, ns, hl = g.L, g.nseq, ntap - 1
        cb = self.CB[:, 0:ns * (hl + L)].rs("p (s j) -> p s j", s=ns)
        if g.kind == "p":
            P.copy("act", cb[:, 0, 0:hl], tail)
        else:
            P.dma("pool", "chist", cb[:, :, 0:hl], hist_src)
        P.copy("act", cb[:, :, hl:hl + L], pv.rs("p (s j) -> p s j", s=ns))
        d3 = dst.rs("p (s j) -> p s j", s=ns)
        P.ts("dve", d3, cb[:, :, 0:L], wcol(0), ALU.mult)
        for j in range(1, ntap):
            P.stt("dve", d3, cb[:, :, j:j + L], wcol(j), d3, ALU.mult, ALU.add)
        if g.kind == "p":
            P.copy("act", tail, cb[:, 0, L:L + hl])
        else:
            self.out_dma(tail_dst[0], tail_dst[1], cb[:, :, L:L + hl], queue="pool")

    def gdn(self, g, l, S32, S16, hist=None, tails=None, pre=None, post=None):
        P = self.P
        n, c, nch = g.n, g.c, g.nch
        H, W = self.H, self.WB["w_in"][l]
        C = self.CONST
        isp = g.kind == "p"
        MI = C[0:c, (C_MI64 if isp else C_MI8):][:, 0:n]
        MS = C[0:c, (C_MS64 if isp else C_MS8):][:, 0:n]
        LS = C[0:c, (C_LS64 if isp else C_LS8):][:, 0:n]
        IDb = C[0:c, C_ID:C_ID + c].rs("p (a t) -> p a t", a=1).bc([c, nch, c])
        OC = self.OABC[2]
        QF, KF, VF = self.F4

        def Hn(k):
            return H[:, k, 0:n]

        def Hc(k, ch):
            return H[:, k, ch * c:(ch + 1) * c]

        def cols(v, ch):
            return v.ap[:, ch * c:(ch + 1) * c]

        for j3, (off, dst) in enumerate(((O_GQ, QF), (O_GK, KF), (O_GV, VF))):
            w = self.wpanel(W, 0, 8, off, 512)

            def cons(m, pv, j3=j3, dst=dst):
                kc = j3 * 4 + m
                pre = self.FT[9][:, 0:n]
                self.conv_fm(g, pv, lambda j: self.GCW[:, l, kc, j:j + 1], self.TAILG[:, l, kc, :], 4, pre,
                             hist_src=None if isp else hist(kc), tail_dst=None if isp else tails(kc))
                P.act(dst[:, m, 0:n], pre, AF.Silu)
            self.gemm_fm(w, 8, Hn, n, cons)
        w = self.wpanel(W, 0, 8, O_GZ, 512)
        self.gemm_fm(w, 8, Hn, n, lambda m, pv: P.act(OC[:, m, 0:n], pv, AF.Silu))
        w = self.wpanel(W, 0, 8, O_GB, 8)
        BG, BETA, GTT, GT, EGT, NBEG = [x[0:c, 0:nch, :] for x in
                                        (self.BG, self.BETA, self.GTT, self.GT, self.EGT, self.NBEG)]
        self.gemm_tm(w, 8, Hc, g, lambda ch, pv: P.copy("act", BG[:, ch, :], pv))
        P.act(BETA, BG[:, :, 0:4], AF.Sigmoid)
        P.tt("dve", GTT, BG[:, :, 4:8], self.GDT[0:c, l * 4:l * 4 + 4].rs("p (a h) -> p a h", a=1).bc([c, nch, 4]), ALU.add)
        P.act(GTT, GTT, AF.Exp)
        P.act(GTT, GTT, AF.Ln, bias=1.0)
        P.tt("dve", GTT, GTT, self.GA[0:c, l * 4:l * 4 + 4].rs("p (a h) -> p a h", a=1).bc([c, nch, 4]), ALU.mult)
        pg = self.ps()[0:c, 0:nch * 4]
        P.mm(pg, [(MI[:, 0:c], GTT.rs("p a h -> p (a h)"))])
        P.copy("act", GT.rs("p a h -> p (a h)"), pg)
        P.act(EGT, GT, AF.Exp)
        P.stt("dve", NBEG, BETA, -1.0, EGT, ALU.mult, ALU.mult)
        nlev = int(round(np.log2(c)))
        EH2 = self.EH[0:c, 0:nch]
        EH3 = EH2.rs("p (a b) -> p a b", b=1)
        for h in range(4):
            qn, kn = self.BT[0][:, 0:n], self.BT[1][:, 0:n]
            for src, dst, sc in ((QF, qn, 128 ** -0.5), (KF, kn, 1.0)):
                sq, rs = self.FT[8][:, 0:n], self.FT[7][:, 0:n]
                P.act(sq, src[:, h, 0:n], AF.Square)
                pss = self.ps()[:, 0:n]
                P.mm(pss, [(self.ONES, sq)])
                P.ts("dve", rs, pss, EPS, ALU.add)
                P.rsqrt(rs, rs)
                P.stt("dve", dst, src[:, h, 0:n], sc, rs, ALU.mult, ALU.mult)
            v16 = self.BT[2][:, 0:n]
            P.copy("act", v16, VF[:, h, 0:n])
            PTB = self.ps().bitcast(BF16)
            P.transpose_multi(PTB, [(PTB.ap[0:c, ch * 128:(ch + 1) * 128], cols(kn, ch), self.IDB.ap)
                                    for ch in range(nch)], reads=[kn, self.IDB])
            KT = self.KT16[0:c, 0:nch, :]
            P.copy("act", KT, PTB[0:c, 0:nch * 128].r3(128))
            PTB = self.ps().bitcast(BF16)
            P.transpose_multi(PTB, [(PTB.ap[0:c, ch * 128:(ch + 1) * 128], cols(v16, ch), self.IDB.ap)
                                    for ch in range(nch)], reads=[v16, self.IDB])
            BV = self.BV[0:c, 0:nch, :]
            P.tt("dve", BV, PTB[0:c, 0:nch * 128].r3(128), BETA[:, :, h:h + 1].bc([c, nch, 128]), ALU.mult)
            GU, raw, dT, dL, Y, Yt = [self.FT[i][0:c, 0:n] for i in (1, 2, 3, 4, 5, 6)]
            tmp, Wm = GU, dL
            P.tt("dve", GU.r3(c), GTT[:, :, h:h + 1].bc([c, nch, c]), MI.r3(c), ALU.mult)
            R = self.PS[4][:, 0:n]
            P.mm(R, [(self.ONES[0:c, :], GU)])
            EG = self.FT[0][:, 0:n]
            P.act(EG, R, AF.Exp)
            qg = self.BT[3][:, 0:n]
            P.tt("dve", qg, qn, EG, ALU.mult)
            Rc = R[0:c, :].r3(c)
            P.tt("dve", raw.r3(c), Rc, GT[:, :, h:h + 1].bc([c, nch, c]), ALU.subtract)
            P.tt("dve", EH3, Rc[:, :, c - 1:c], GT[:, :, h:h + 1], ALU.subtract)
            P.act(EH2, EH2, AF.Exp)
            P.tt("dve", self.KHAT[0:c, 0:nch, :], KT, EH3.bc([c, nch, 128]), ALU.mult)
            P.ts("dve", dT, raw, 0.0, ALU.min)
            P.act(dT, dT, AF.Exp)
            P.tt("dve", dT, dT, MI, ALU.mult)
            P.ts("dve", dL, raw, -1.0, ALU.mult, 0.0, ALU.min)
            P.act(dL, dL, AF.Exp)
            P.tt("dve", dL, dL, LS, ALU.mult)
            PKK, PQK = self.PS[5], self.PS[6]
            P.mm_multi(PKK, [(PKK.ap[0:c, ch * c:(ch + 1) * c], [(cols(kn, ch), cols(kn, ch))]) for ch in range(nch)], reads=[kn])
            P.mm_multi(PQK, [(PQK.ap[0:c, ch * c:(ch + 1) * c], [(cols(kn, ch), cols(qn, ch))]) for ch in range(nch)], reads=[kn, qn])
            QKM = self.QKM[0:c, 0:n]
            P.tt("dve", QKM, PQK[0:c, 0:n], dT, ALU.mult)
            P.tt("dve", tmp.r3(c), BETA[:, :, h:h + 1].bc([c, nch, c]), IDb, ALU.mult)
            PB = self.PS[7][0:c, 0:n]
            P.mm(PB, [(self.ONES[0:c, 0:c], tmp)])
            P.tt("dve", Y, PKK[0:c, 0:n], dT, ALU.mult)
            P.tt("dve", Y, Y, MS, ALU.mult)
            P.stt("dve", Y, PB, -1.0, Y, ALU.mult, ALU.mult)
            P.tt("dve", Yt, PKK[0:c, 0:n], dL, ALU.mult)
            P.stt("dve", Yt.r3(c), BETA[:, :, h:h + 1].bc([c, nch, c]), -1.0, Yt.r3(c), ALU.mult, ALU.mult)
            P.tt("dve", Wm.r3(c), Y.r3(c), IDb, ALU.add)
            for lev in range(1, nlev):
                last = lev == nlev - 1
                pb = self.ps()
                P.mm_multi(pb, [(pb.ap[0:c, ch * c:(ch + 1) * c], [(cols(Y, ch), cols(Yt, ch))]) for ch in range(nch)], reads=[Y, Yt])
                if not last:
                    pa = self.ps()
                    P.mm_multi(pa, [(pa.ap[0:c, ch * c:(ch + 1) * c], [(cols(Yt, ch), cols(Y, ch))]) for ch in range(nch)], reads=[Y, Yt])
                    P.copy("act", Y, pa[0:c, 0:n])
                P.copy("act", Yt, pb[0:c, 0:n])
                pw = self.ps()
                P.mm_multi(pw, [(pw.ap[0:c, ch * c:(ch + 1) * c], [(cols(Yt, ch), cols(Wm, ch))]) for ch in range(nch)], reads=[Yt, Wm])
                P.tt("dve", Wm, Wm, pw[0:c, 0:n], ALU.add)
            PO = self.PS[4]
            if pre:
                pre(h)
            for ch in range(nch):
                cs = slice(ch * c, (ch + 1) * c)
                pk = self.ps()[0:c, 0:128]
                P.mm(pk, [(kn[:, cs], S16(ch, h))])
                rh = self.RHS[ch % 2][0:c, :]
                P.stt("dve", rh, pk, NBEG[:, ch, h:h + 1], BV[:, ch, :], ALU.mult, ALU.add)
                pu = self.ps()[0:c, 0:128]
                P.mm(pu, [(Wm[:, cs], rh)])
                u16 = self.U16[ch % 2][0:c, :]
                P.copy("act", u16, pu)
                P.mm(PO[:, cs], [(S16(ch, h), qg[:, cs]), (u16, QKM[:, cs])])
                su = self.ps()[:, 0:128]
                P.mm(su, [(self.KHAT[0:c, ch, :], u16)])
                P.stt("dve", S32(ch, h), S32(ch, h), EG[:, ch * c + c - 1:ch * c + c], su, ALU.mult, ALU.add)
                P.copy("act", S16(ch, h), S32(ch, h))
            if post:
                post(h)
            sq, rs, t1 = self.FT[8][:, 0:n], self.FT[7][:, 0:n], self.FT[9][:, 0:n]
            P.act(sq, PO[:, 0:n], AF.Square)
            pss = self.ps()[:, 0:n]
            P.mm(pss, [(self.ONES, sq)])
            P.ts("dve", rs, pss, 1.0 / 128, ALU.mult, EPS, ALU.add)
            P.rsqrt(rs, rs)
            P.stt("dve", t1, PO[:, 0:n], self.GN[:, l:l + 1], rs, ALU.mult, ALU.mult)
            P.tt("dve", OC[:, h, 0:n], t1, OC[:, h, 0:n], ALU.mult)

    def merge_out(self, g, l, XR):
        P = self.P
        n = g.n
        H, I = self.H, self.I
        MIX = [self.F4[0], self.F4[1]]

        def Hn(k):
            return H[:, k, 0:n]

        for c0 in (0, 512):
            for bi, (wname, goff) in enumerate((("w_branch_a", O_GTA), ("w_branch_b", O_GTB), ("w_branch_c", O_GTC))):
                O_x = self.OABC[bi]
                wg = self.wpanel(self.WB["w_in"][l], 0, 8, goff + c0, 512)
                wb = self.wpanel(self.WB[wname][l], 0, 4, c0, 512)
                for m in range(4):
                    pg = self.ps()[:, 0:n]
                    P.mm(pg, [(wg[:, k, m * 128:(m + 1) * 128], Hn(k)) for k in range(8)])
                    sg = self.FT[m % 2][:, 0:n]
                    P.act(sg, pg, AF.Sigmoid)
                    pb = self.ps()[:, 0:n]
                    P.mm(pb, [(wb[:, k, m * 128:(m + 1) * 128], O_x[:, k, 0:n]) for k in range(4)])
                    dst = MIX[c0 // 512][:, m, 0:n]
                    if bi == 0:
                        P.tt("dve", dst, pb, sg, ALU.mult)
                    else:
                        tmp = self.FT[2 + m % 2][:, 0:n]
                        P.tt("dve", tmp, pb, sg, ALU.mult)
                        P.tt("pool", dst, dst, tmp, ALU.add)
        for k in range(8):
            P.copy("act", H[:, k, 0:n], MIX[k // 4][:, k % 4, 0:n])
        for c0 in (0, 512):
            w = self.wpanel(self.WB["w_out"][l], 0, 8, c0, 512)
            self.gemm_fm(w, 8, Hn, n, lambda m, pv, c0=c0: P.tt("dve", XR[:, c0 // 128 + m, 0:n], XR[:, c0 // 128 + m, 0:n], pv, ALU.add))

    def ffn(self, g, l, XR, hist=None, tails=None):
        P = self.P
        n = g.n
        H, I = self.H, self.I
        isp = g.kind == "p"
        A16 = self.A16

        def Hn(k):
            return H[:, k, 0:n]

        self.rmsnorm_fm(XR, self.LNF[:, l, :], H, n)
        for c0 in range(0, DFF, 512):
            wd = min(512, DFF - c0)
            wg = self.wpanel(self.WB["w_up"][l], 0, 8, c0, wd)
            wu = self.wpanel(self.WB["w_up"][l], 0, 8, DFF + c0, wd)
            for m in range(wd // 128):
                j = c0 // 128 + m
                pg = self.ps()[:, 0:n]
                P.mm(pg, [(wg[:, k, m * 128:(m + 1) * 128], Hn(k)) for k in range(8)])
                pre = self.FT[9][:, 0:n]
                self.conv_fm(g, pg, lambda t, j=j: self.FCW[:, l, j, t:t + 1], self.TAILF[:, l, j, :], 3, pre,
                             hist_src=None if isp else hist(j), tail_dst=None if isp else tails(j))
                sg = self.FT[m % 2][:, 0:n]
                P.act(sg, pre, AF.Silu)
                pu = self.ps()[:, 0:n]
                P.mm(pu, [(wu[:, k, m * 128:(m + 1) * 128], Hn(k)) for k in range(8)])
                P.tt("dve", A16[:, j, 0:n], pu, sg, ALU.mult)
        for m in range(8):
            w = self.wpanel(self.WB["w_down"][l], 0, 22, m * 128, 128)
            pv = self.ps()[:, 0:n]
            P.mm(pv, [(w[:, k, :], A16[:, k, 0:n]) for k in range(22)])
            P.tt("dve", XR[:, m, 0:n], XR[:, m, 0:n], pv, ALU.add)

    def alloc_samples(self):
        P = self.P
        self.XS = self.sb("XS", [128, 8, ST])
        self.SST32 = self.sb("SST32", [128, SPC, 128])
        self.SST16 = self.sb("SST16", [128, SPC, 128], BF16)
        self.QS16 = self.sb("QS16", [128, 4, ST], BF16)
        self.ZB = self.sb("ZB", [128, 256], BF16)
        P.memset("dve", self.ZB, 0.0)
        self.KZ16 = self.sb("KZ16", [128, 2, 4, ST], BF16)
        P.memset("dve", self.KZ16, 0.0)
        self.VA16 = self.sb("VA16", [ST, 512], BF16)
        npt = SPC * NPAGES
        self.IDX = self.sb("IDX", [128, DEPTH, npt], I32)
        pti = self.FT[2][:, 0:npt].bitcast(I32)
        ptf, idf = self.FT[0][:, 0:npt], self.FT[1][:, 0:npt]
        P.dma("sp", "pt", pti, self.I["page_table"].rearrange("s j -> (s j)").partition_broadcast(128))
        P.copy("dve", ptf, pti)
        P.ts("dve", idf, ptf, float(PAGE), ALU.mult, self.CONST[:, C_IOTA:C_IOTA + 1], ALU.add)
        for l in range(DEPTH):
            if l:
                P.ts("dve", idf, idf, float(self.cfg.npool * PAGE), ALU.add)
            P.copy("dve", self.IDX[:, l, :], idf)
        self.dbg_out("idx", self.IDX, [128, DEPTH, npt])

    def gather(self, chan, dst, rows, idx_col, nrows):
        iap, dap = idx_col.ap, dst.ap

        def fn(eng, sem):
            return eng.indirect_dma_start(
                out=dap, out_offset=None, in_=rows,
                in_offset=bass.IndirectOffsetOnAxis(ap=iap, axis=0)).then_inc(sem, 16)
        return self.P.op("pool", fn, reads=[idx_col], writes=[dst], chan=chan, ndma=1)

    def attn_sample(self, g, l):
        P = self.P
        n = g.n
        H, W = self.H, self.WB["w_in"][l]
        DQ, DK = self.F4[0], self.F4[1]
        OB = self.OABC[1]
        cfg = self.cfg

        def Hn(k):
            return H[:, k, 0:n]

        P.dma("pool", "rope", self.ROPE[:, :, 0:n], self.I["rope_s"].rearrange("a p t -> p a t"))
        for off, dst in ((O_DQ, DQ), (O_DK, DK)):
            w = self.wpanel(W, 0, 8, off, 512)
            self.gemm_fm(w, 8, Hn, n, lambda m, pv, dst=dst: P.copy("act", dst[:, m, 0:n], pv))
        w = self.wpanel(W, 0, 8, O_DV, 512)

        pv = self.ps()[0:n, :]
        P.mm(pv, [(H[:, k, 0:n], w[:, k, :]) for k in range(8)])
        vs = self.VST[0][0:n, :]
        P.copy("act", vs, pv)
        P.copy("act", self.VA16, pv)
        self.out_dma("sv", self.O["s_v"][l], vs, queue="pool")
        for h in range(4):
            ks = self.KST[h % 2][:, 0:n]
            self.qknorm_rope(DK[:, h, 0:n], self.QKG[:, l, 1:2], ks)
            self.out_dma(f"sk{h % 2}", self.O["s_kT"][l, h], ks, queue="pool")
            P.copy("act", self.KZ16[0:64, 0, h, :], ks[0:64, :])
            P.copy("act", self.KZ16[64:128, 1, h, :], ks[64:128, :])
            self.qknorm_rope(DQ[:, h, 0:n], self.QKG[:, l, 0:1], self.QS16[:, h, :])
        step = int(os.environ.get("ATT_STEP", 6))
        if step <= 1:
            return
        krows = self.I["cache_k"].rearrange("l a s h d -> (l a s) (h d)")
        vrows = self.I["cache_v"].rearrange("l a s h d -> (l a s) (h d)")
        nrows = DEPTH * cfg.npool * PAGE
        PO, PZ = self.PS[4], self.PS[5]

        def col(s_, h_, half):
            return ((h_ * 2 + half) * SPC + s_) * SL

        for acc in (PO, PZ):
            P.mm(acc[:, 0:256], [(self.ZB[:, 0:128], self.ZB)], start=True, stop=False)
        ei = 0
        dbg_ns = int(os.environ.get("ATT_SEQS", SPC))
        dbg_np = int(os.environ.get("ATT_PAGES", NPAGES))
        for s_ in range(dbg_ns):
            qs = slice(s_ * SL, (s_ + 1) * SL)
            for j in range(dbg_np):
                kp, vp = self.KST[j % 2], self.VST[j % 2]
                ic = self.IDX[:, l, s_ * NPAGES + j:s_ * NPAGES + j + 1]
                self.gather(f"gk{j % 2}", kp, krows, ic, nrows)
                self.gather(f"gv{j % 2}", vp, vrows, ic, nrows)
                pt = self.ps()
                P.transpose_multi(pt, [(pt.ap[:, hh * 128:(hh + 1) * 128], kp.ap[:, hh * 128:(hh + 1) * 128], self.ID.ap)
                                       for hh in range(4)], reads=[kp, self.ID])
                ktp = self.KH16[:, (j % 2) * 512:(j % 2) * 512 + 512]
                P.copy("act", ktp, pt)
                vp16 = self.VH16[:, (j % 2) * 4:(j % 2) * 4 + 4, :].rs("p a d -> p (a d)")
                P.copy("dve", vp16, vp)
                if step <= 2:
                    continue
                px = self.ps()[:, 0:64]
                P.mm_multi(px, [(px.ap[:, (hh * 2 + hf) * SL:(hh * 2 + hf + 1) * SL],
                                 [(ktp.ap[hf * 64:(hf + 1) * 64, hh * 128:(hh + 1) * 128],
                                   self.QS16.ap[hf * 64:(hf + 1) * 64, hh, qs])])
                                for hh in range(4) for hf in range(2)], reads=[ktp, self.QS16])
                e = self.BT[ei % 6][:, 0:64]
                ei += 1
                P.act(e, px, AF.Exp, scale=0.125)
                if step <= 3:
                    continue
                P.mm_multi(PO, [(PO.ap[:, col(s_, hh, hf):col(s_, hh, hf) + SL],
                                 [(vp16.ap[:, hh * 128:(hh + 1) * 128], e.ap[:, (hh * 2 + hf) * SL:(hh * 2 + hf + 1) * SL])])
                                for hh in range(4) for hf in range(2)], reads=[vp16, e], start=False, stop=False)
                P.mm_multi(PZ, [(PZ.ap[:, col(s_, hh, hf):col(s_, hh, hf) + SL],
                                 [(self.ONESB.ap, e.ap[:, (hh * 2 + hf) * SL:(hh * 2 + hf + 1) * SL])])
                                for hh in range(4) for hf in range(2)], reads=[self.ONESB, e], start=False, stop=False)
        if step >= 5:
            px = self.ps()[0:n, 0:256]
            P.mm_multi(px, [(px.ap[:, (hh * 2 + hf) * n:(hh * 2 + hf + 1) * n],
                             [(self.KZ16.ap[:, hf, hh, :], self.QS16.ap[:, hh, :])])
                            for hh in range(4) for hf in range(2)], reads=[self.KZ16, self.QS16])
            e = self.BT[ei % 6][0:n, 0:256]
            ei += 1
            P.act(e, px, AF.Exp, scale=0.125)
            bd = self.CONST[0:n, C_BD32:C_BD32 + n].rs("p (a t) -> p a t", a=1).bc([n, 8, n])
            P.tt("dve", e.r3(n), e.r3(n), bd, ALU.mult)
            if os.environ.get("ATT_SUB") != "a":
                P.mm_multi(PO, [(PO.ap[:, col(0, hh, hf):col(0, hh, hf) + n],
                                 [(self.VA16.ap[:, hh * 128:(hh + 1) * 128], e.ap[:, (hh * 2 + hf) * n:(hh * 2 + hf + 1) * n])])
                                for hh in range(4) for hf in range(2)], reads=[self.VA16, e], start=False, stop=False)
                P.mm_multi(PZ, [(PZ.ap[:, col(0, hh, hf):col(0, hh, hf) + n],
                                 [(self.ONESB.ap[0:n, :], e.ap[:, (hh * 2 + hf) * n:(hh * 2 + hf + 1) * n])])
                                for hh in range(4) for hf in range(2)], reads=[self.ONESB, e], start=False, stop=False)
        for acc in (PO, PZ):
            P.mm(acc[:, 0:256], [(self.ZB[:, 0:128], self.ZB)], start=False, stop=True)
        if step <= 5:
            return
        for h in range(4):
            r1, r2, o1, o2 = [self.FT[i][:, 0:n] for i in range(4)]
            c0, c1 = col(0, h, 0), col(0, h, 1)
            P.recip(r1, PZ[:, c0:c0 + n])
            P.recip(r2, PZ[:, c1:c1 + n])
            P.tt("dve", o1, PO[:, c0:c0 + n], r1, ALU.mult)
            P.tt("dve", o2, PO[:, c1:c1 + n], r2, ALU.mult)
            P.stt("dve", o1, o2, self.NLAM[:, l:l + 1], o1, ALU.mult, ALU.add)
            self.subln(o1, l, OB[:, h, 0:n])

    def run_samples(self):
        cfg, P, I, O = self.cfg, self.P, self.I, self.O
        g = Grp("s", ST, SPC, SL, SPC)
        XS = self.XS
        P.dma("sp", "xs", XS, I["xT_s"].rearrange("(c p) t -> p c t", p=128))

        def S32(ch, h):
            return self.SST32[:, ch, :]

        def S16(ch, h):
            return self.SST16[:, ch, :]

        for l in range(cfg.depth):
            def mk(src, dst, chan):
                def pre(h):
                    P.dma("pool", "sst", self.SST32, I[src][l, :, h].rearrange("s p v -> p s v"))
                    P.copy("act", self.SST16, self.SST32)

                def post(h):
                    self.out_dma(chan, O[dst][l, :, h].rearrange("s p v -> p s v"), self.SST32, queue="pool")
                return pre, post
            self.rmsnorm_fm(XS, self.LNM[:, l, :], self.H, g.n)
            pre, post = mk("st_hgrn", "s_hgrn", "sst_o")
            self.hgrn(g, l, S32, S16, pre, post)
            if cfg.upto == "hgrn":
                self.dbg_out(f"soa_{l}", self.OABC[0], [128, 4, TT])
                continue
            self.attn_sample(g, l)
            if cfg.upto == "attn":
                self.dbg_out(f"sob_{l}", self.OABC[1], [128, 4, TT])
                continue
            pre, post = mk("st_gdn", "s_gdn", "sst_o")
            self.gdn(g, l, S32, S16,
                     hist=lambda kc: I["st_gconv"][l, :, kc * 128:(kc + 1) * 128, :].rearrange("s p j -> p s j"),
                     tails=lambda kc: ("sgc", O["s_gconv"][l, :, kc * 128:(kc + 1) * 128, :].rearrange("s p j -> p s j")),
                     pre=pre, post=post)
            if cfg.upto == "gdn":
                self.dbg_out(f"soc_{l}", self.OABC[2], [128, 4, TT])
                continue
            self.merge_out(g, l, XS)
            self.ffn(g, l, XS,
                     hist=lambda j: I["st_fconv"][l, :, j * 128:(j + 1) * 128, :].rearrange("s p j -> p s j"),
                     tails=lambda j: ("sfc", O["s_fconv"][l, :, j * 128:(j + 1) * 128, :].rearrange("s p j -> p s j")))
        if cfg.upto == "all":
            self.out_dma("ys", O["yT_s"].rearrange("(c p) t -> p c t", p=128), XS)

    def run(self):
        cfg, P, I = self.cfg, self.P, self.I
        with contextlib.ExitStack() as st:
            self.st = st
            self.setup()
            self.alloc_mixers()
            self.precast_weights()
            XR = self.sb("XR", [128, 8, TT])
            if cfg.samples:
                self.alloc_samples()
                self.run_samples()
            if cfg.prompt:
                for t in range(cfg.seq // TT):
                    g = Grp("p", TT, 1, 64, 8, tile=t)
                    P.dma("sp", "x", XR, I["xT_p"][:, t * TT:(t + 1) * TT].rearrange("(c p) t -> p c t", p=128))
                    for l in range(cfg.depth):
                        self.rmsnorm_fm(XR, self.LNM[:, l, :], self.H, g.n)
                        self.hgrn(g, l, lambda ch, h, l=l: self.HS32[:, l, h, :],
                                  lambda ch, h, l=l: self.HS16[:, l, h, :])
                        if cfg.upto == "hgrn":
                            self.dbg_out(f"oa_{t}_{l}", self.OABC[0], [128, 4, TT])
                            continue
                        self.attn_prompt(g, l)
                        if cfg.upto == "attn":
                            self.dbg_out(f"ob_{t}_{l}", self.OABC[1], [128, 4, TT])
                            continue
                        self.gdn(g, l, lambda ch, h, l=l: self.GS32[:, l, h, :],
                                 lambda ch, h, l=l: self.GS16[:, l, h, :])
                        if cfg.upto == "gdn":
                            self.dbg_out(f"oc_{t}_{l}", self.OABC[2], [128, 4, TT])
                            continue
                        self.merge_out(g, l, XR)
                        self.ffn(g, l, XR)
                    if cfg.upto == "all":
                        self.out_dma("yp", self.O["yT_p"][:, t * TT:(t + 1) * TT].rearrange("(c p) t -> p c t", p=128), XR, queue="pool")
                if cfg.upto == "all":
                    for l in range(cfg.depth):
                        self.out_dma("pst", self.O["p_hgrn"][l].rearrange("h p v -> p h v"), self.HS32[:, l])
                        self.out_dma("pst", self.O["p_gdn"][l].rearrange("h p v -> p h v"), self.GS32[:, l])
                        self.out_dma("pst", self.O["p_gconv"][l].rearrange("(k p) j -> p k j", p=128), self.TAILG[:, l])
                        self.out_dma("pst", self.O["p_fconv"][l].rearrange("(k p) j -> p k j", p=128), self.TAILF[:, l])
                if cfg.upto == "hgrn":
                    self.dbg_out("hs", self.HS32, [128, cfg.depth, 4, 128])
                if cfg.upto == "gdn":
                    self.dbg_out("gs", self.GS32, [128, cfg.depth, 4, 128])
                    self.dbg_out("tailg", self.TAILG, [128, DEPTH, 12, 3])
            P.emit(final_chans=self.out_chans)
        return self.nc


def _host_inputs(inp):
    c = np.ascontiguousarray
    f = np.float32
    shared = {k: c(np.asarray(inp[k], dtype=f)) for k in
              ("cache_k", "cache_v", "w_in", "w_branch_a", "w_branch_b", "w_branch_c", "w_out", "w_up", "w_down")}
    shared["params"] = _pack_params(inp)
    shared["consts"] = _const_table()
    shared["rope_p"] = _rope_table(np.arange(SEQ))
    shared["rope_s"] = _rope_table(np.tile(PAST + np.arange(SL), SPC))
    xp = np.asarray(inp["x_prompt"], dtype=f)
    xs = np.asarray(inp["x_sample"], dtype=f)
    maps = []
    for core in range(NCORES):
        b = core // 2
        s0 = core * SPC
        m = dict(shared)
        m["xT_p"] = c(xp[b].T)
        m["xT_s"] = c(xs[s0:s0 + SPC].reshape(ST, D).T)
        m["st_hgrn"] = c(np.asarray(inp["state_hgrn"], f)[:, s0:s0 + SPC])
        m["st_gdn"] = c(np.asarray(inp["state_gdn"], f)[:, s0:s0 + SPC])
        m["st_gconv"] = c(np.asarray(inp["state_gdn_conv"], f)[:, s0:s0 + SPC].transpose(0, 1, 3, 2))
        m["st_fconv"] = c(np.asarray(inp["state_ffn_conv"], f)[:, s0:s0 + SPC].transpose(0, 1, 3, 2))
        m["page_table"] = c(np.asarray(inp["page_table"], np.int32)[s0:s0 + SPC])
        maps.append(m)
    return maps


def _host_outputs(res):
    r = res
    f = np.float32
    y_p = np.stack([r[2 * b]["yT_p"].T for b in range(NB)]).astype(f)
    y_s = np.concatenate([r[ci]["yT_s"].T.reshape(SPC, SL, D) for ci in range(NCORES)]).astype(f)
    p_hgrn = np.stack([r[2 * b]["p_hgrn"] for b in range(NB)], axis=1).astype(f)
    p_k = np.stack([r[2 * b]["p_kT"].transpose(0, 3, 1, 2) for b in range(NB)], axis=1).astype(f)
    p_v = np.stack([r[2 * b]["p_v"].reshape(DEPTH, SEQ, NH, 128) for b in range(NB)], axis=1).astype(f)
    p_gdn = np.stack([r[2 * b]["p_gdn"] for b in range(NB)], axis=1).astype(f)
    p_gc = np.stack([r[2 * b]["p_gconv"].transpose(0, 2, 1) for b in range(NB)], axis=1).astype(f)
    p_fc = np.stack([r[2 * b]["p_fconv"].transpose(0, 2, 1) for b in range(NB)], axis=1).astype(f)
    s_hgrn = np.concatenate([r[ci]["s_hgrn"] for ci in range(NCORES)], axis=1).astype(f)
    s_k = np.concatenate([r[ci]["s_kT"].transpose(0, 3, 1, 2).reshape(DEPTH, SPC, SL, NH, 128)
                          for ci in range(NCORES)], axis=1).astype(f)
    s_v = np.concatenate([r[ci]["s_v"].reshape(DEPTH, SPC, SL, NH, 128) for ci in range(NCORES)], axis=1).astype(f)
    s_gdn = np.concatenate([r[ci]["s_gdn"] for ci in range(NCORES)], axis=1).astype(f)
    s_gc = np.concatenate([r[ci]["s_gconv"].transpose(0, 1, 3, 2) for ci in range(NCORES)], axis=1).astype(f)
    s_fc = np.concatenate([r[ci]["s_fconv"].transpose(0, 1, 3, 2) for ci in range(NCORES)], axis=1).astype(f)
    return (y_p, y_s, p_hgrn, p_k, p_v, p_gdn, p_gc, p_fc, s_hgrn, s_k, s_v, s_gdn, s_gc, s_fc)


def kernel(**inputs):
    nc = Builder(Cfg()).run()
    in_maps = _host_inputs(inputs)
    res = run_bass_kernel_spmd(nc, in_maps, core_ids=list(range(NCORES)))
    return _host_outputs(res.results)
```

```python
import contextlib
import os
import sys

import numpy as np
import concourse.bass as bass
import concourse.mybir as mybir
from concourse.bass_utils import run_bass_kernel_spmd

F32 = mybir.dt.float32
BF16 = mybir.dt.bfloat16
I32 = mybir.dt.int32
AF = mybir.ActivationFunctionType
ALU = mybir.AluOpType
AX = mybir.AxisListType

SEM_CAP = 2048
SEM_POOL_LIMIT = 96

ENGINES = ("pe", "act", "dve", "pool", "sp")
RSQRT_VIA_LN = os.environ.get("RSQRT_LN", "1") == "1"


class View:
    __slots__ = ("ap", "toks")

    def __init__(self, ap, toks):
        self.ap = ap
        self.toks = tuple(toks)

    def __getitem__(self, idx):
        return View(self.ap[idx], self.toks)

    def with_ap(self, ap):
        return View(ap, self.toks)

    def bc(self, shape):
        return View(self.ap.broadcast_to(list(shape)), self.toks)

    def r3(self, b):
        return View(self.ap.rearrange("p (a b) -> p a b", b=b), self.toks)

    def bitcast(self, dt):
        return View(self.ap.bitcast(dt), self.toks)

    def rs(self, pattern, **kw):
        return View(self.ap.rearrange(pattern, **kw), self.toks)


class _Op:
    __slots__ = ("eng", "fn", "deps", "chan", "ndma", "event", "needed", "idx", "where")

    def __init__(self, eng, fn, chan=None, ndma=0):
        self.eng = eng
        self.fn = fn
        self.deps = set()
        self.chan = chan
        self.ndma = ndma
        self.event = None
        self.needed = False
        self.idx = -1


class Prog:
    def __init__(self, nc):
        self.nc = nc
        self.ops = []
        self.streams = {e: [] for e in ENGINES}
        self.last_w = {}
        self.readers = {}
        self.chan_last = {}
        self.barrier_chans = set()
        self.ntok = 0

    def tok(self, name):
        self.ntok += 1
        return (name, self.ntok)

    def view(self, ap, name, n=1):
        return View(ap, [self.tok(name) for _ in range(n)])

    def op(self, eng, fn, reads=(), writes=(), chan=None, ndma=0):
        o = _Op(eng, fn, chan, ndma)
        for v in reads:
            if v is None or not isinstance(v, View):
                continue
            for t in v.toks:
                w = self.last_w.get(t)
                if w is not None:
                    o.deps.add(w)
        for v in writes:
            for t in v.toks:
                w = self.last_w.get(t)
                if w is not None:
                    o.deps.add(w)
                for r in self.readers.get(t, ()):
                    o.deps.add(r)
        if chan is not None:
            prev = self.chan_last.get(chan)
            if prev is not None and chan not in self.barrier_chans:
                o.deps.add(prev)
            self.chan_last[chan] = o
        o.deps.discard(o)
        if eng == "pe":
            o.deps = {d for d in o.deps if d.eng != "pe"}
        for v in reads:
            if v is None or not isinstance(v, View):
                continue
            for t in v.toks:
                self.readers.setdefault(t, []).append(o)
        for v in writes:
            for t in v.toks:
                self.last_w[t] = o
                self.readers[t] = []
        o.idx = len(self.ops)
        fr = sys._getframe(2)
        o.where = f"{fr.f_code.co_name}:{fr.f_lineno}"
        self.ops.append(o)
        self.streams[eng].append(o)
        return o

    def emit(self, final_chans=()):
        nc = self.nc
        for o in self.ops:
            for d in o.deps:
                d.needed = True
        tails = [self.chan_last[c] for c in final_chans if c in self.chan_last]
        for t in tails:
            t.needed = True

        semkeys = []
        cnt = {}
        gen = {}

        def bump(base, step):
            g = gen.get(base, 0)
            c = cnt.get(base, 0)
            if c + step > SEM_CAP:
                g += 1
                c = 0
            c += step
            gen[base] = g
            cnt[base] = c
            key = (base, g)
            if not semkeys or key not in seen:
                seen.add(key)
                semkeys.append(key)
            return key, c

        seen = set()
        for e in ENGINES:
            for o in self.streams[e]:
                if o.chan is not None:
                    key = None
                    val = 0
                    if cnt.get(("c", o.chan), 0) + 16 * o.ndma > SEM_CAP:
                        cnt[("c", o.chan)] = SEM_CAP
                    for _ in range(o.ndma):
                        key, val = bump(("c", o.chan), 16)
                    o.event = (key, val)
                elif o.needed:
                    o.event = bump(("e", e), 1)
        finals = {}
        for o in self.ops:
            if o.chan in self.barrier_chans:
                k = o.event[0]
                finals[k] = max(finals.get(k, 0), o.event[1])
        assert len(semkeys) <= SEM_POOL_LIMIT, f"semaphore pool exhausted: {len(semkeys)}"
        self.n_sems = len(semkeys)

        with contextlib.ExitStack() as st:
            sems = {}
            for i, k in enumerate(semkeys):
                sems[k] = st.enter_context(nc.semaphore(f"s{i}"))
            block = st.enter_context(nc.Block())
            handles = {"pe": block.tensor, "act": block.scalar, "dve": block.vector,
                       "pool": block.gpsimd, "sp": block.sync}

            def make(ename):
                stream = self.streams[ename]

                def body(eng):
                    known = {}
                    for o in stream:
                        need = {}
                        for d in o.deps:
                            k, v = d.event
                            if d.chan in self.barrier_chans:
                                v = finals[k]
                            if known.get(k, 0) < v:
                                need[k] = max(need.get(k, 0), v)
                        for k, v in need.items():
                            eng.wait_ge(sems[k], v)
                            known[k] = v
                        try:
                            if o.chan is not None:
                                o.fn(eng, sems[o.event[0]])
                            else:
                                ins = o.fn(eng)
                                if o.needed:
                                    ins.then_inc(sems[o.event[0]], 1)
                        except Exception as exc:
                            raise RuntimeError(f"op #{o.idx} on {ename} recorded at {o.where}: {exc}") from exc
                    if ename == "sp":
                        for t in tails:
                            k, v = t.event
                            if known.get(k, 0) < v:
                                eng.wait_ge(sems[k], v)
                                known[k] = v
                return body

            for ename in ENGINES:
                if self.streams[ename] or ename == "sp":
                    handles[ename](make(ename))

    @staticmethod
    def _a(x):
        return x.ap if isinstance(x, View) else x

    def mm(self, out, pairs, extra_reads=(), start=True, stop=True):
        aps = [(l.ap, r.ap) for l, r in pairs]
        n = len(aps)
        oap = out.ap

        def fn(eng):
            ins = None
            for i, (l, r) in enumerate(aps):
                ins = eng.matmul(oap, l, r, start=(start and i == 0), stop=(stop and i == n - 1))
            return ins
        return self.op("pe", fn, reads=[v for p in pairs for v in p] + list(extra_reads),
                       writes=[out])

    def transpose(self, out, in_, ident):
        def fn(eng):
            return eng.transpose(out.ap, in_.ap, ident.ap)
        return self.op("pe", fn, reads=[in_, ident], writes=[out])

    def act(self, out, in_, func, bias=0.0, scale=1.0, accum=None):
        b, s = self._a(bias), self._a(scale)
        kw = {}
        if accum is not None:
            kw["accum_out"] = accum.ap

        def fn(eng):
            return eng.activation(out.ap, in_.ap, func, bias=b, scale=s, **kw)
        return self.op("act", fn, reads=[in_, bias, scale],
                       writes=[out] + ([accum] if accum is not None else []))

    def tt(self, eng_name, out, in0, in1, op):
        def fn(eng):
            return eng.tensor_tensor(out.ap, in0.ap, in1.ap, op)
        return self.op(eng_name, fn, reads=[in0, in1], writes=[out])

    def ts(self, eng_name, out, in0, s1, op0, s2=None, op1=None, accum=None):
        a1, a2 = self._a(s1), self._a(s2)
        kw = {}
        if op1 is not None:
            kw["op1"] = op1
        if accum is not None:
            kw["accum_out"] = accum.ap

        def fn(eng):
            return eng.tensor_scalar(out.ap, in0.ap, a1, a2, op0, **kw)
        return self.op(eng_name, fn, reads=[in0, s1, s2],
                       writes=[out] + ([accum] if accum is not None else []))

    def stt(self, eng_name, out, in0, scalar, in1, op0, op1):
        sc = self._a(scalar)

        def fn(eng):
            return eng.scalar_tensor_tensor(out.ap, in0.ap, sc, in1.ap, op0, op1)
        return self.op(eng_name, fn, reads=[in0, scalar, in1], writes=[out])

    def copy(self, eng_name, out, in_):
        if eng_name == "act":
            def fn(eng):
                return eng.copy(out.ap, in_.ap)
        else:
            def fn(eng):
                return eng.tensor_copy(out.ap, in_.ap)
        return self.op(eng_name, fn, reads=[in_], writes=[out])

    def memset(self, eng_name, out, val):
        def fn(eng):
            return eng.memset(out.ap, val)
        return self.op(eng_name, fn, writes=[out])

    def reduce(self, eng_name, out, in_, op, axis=None):
        ax = AX.X if axis is None else axis

        def fn(eng):
            return eng.tensor_reduce(out.ap, in_.ap, ax, op)
        return self.op(eng_name, fn, reads=[in_], writes=[out])

    def dma(self, queue, chan, out, in_, **kw):
        oap, iap = self._a(out), self._a(in_)

        def fn(eng, sem):
            return eng.dma_start(out=oap, in_=iap, **kw).then_inc(sem, 16)
        return self.op(queue, fn, reads=[in_] if isinstance(in_, View) else [],
                       writes=[out] if isinstance(out, View) else [], chan=chan, ndma=1)

    def recip(self, out, in_):
        def fn(eng):
            return eng.reciprocal(out.ap, in_.ap)
        return self.op("dve", fn, reads=[in_], writes=[out])

    def recip_pos(self, out, in_):
        self.act(out, in_, AF.Ln)
        return self.act(out, out, AF.Exp, scale=-1.0)

    def rsqrt(self, out, in_):
        if RSQRT_VIA_LN:
            self.act(out, in_, AF.Ln)
            return self.act(out, out, AF.Exp, scale=-0.5)
        self.recip(out, in_)
        return self.act(out, out, AF.Sqrt)

    def mm_multi(self, out, groups, reads, start=True, stop=True):
        def fn(eng):
            ins = None
            for oap, pairs in groups:
                n = len(pairs)
                for i, (l, r) in enumerate(pairs):
                    ins = eng.matmul(oap, l, r, start=(start and i == 0), stop=(stop and i == n - 1))
            return ins
        return self.op("pe", fn, reads=list(reads), writes=[out])

    def transpose_multi(self, out, items, reads):
        def fn(eng):
            ins = None
            for oap, iap, idap in items:
                ins = eng.transpose(oap, iap, idap)
            return ins
        return self.op("pe", fn, reads=list(reads), writes=[out])

    def scan(self, out, data0, data1, initial=0.0, op0=None, op1=None):
        o0 = ALU.mult if op0 is None else op0
        o1 = ALU.add if op1 is None else op1

        def fn(eng):
            return eng.tensor_tensor_scan(out.ap, data0.ap, data1.ap, initial, o0, o1)
        return self.op("dve", fn, reads=[data0, data1], writes=[out])


D = 1024
SEQ = 4096
DEPTH = 2
NB = 4
SB = 32
SL = 8
PAST = 8192
PAGE = 128
NPAGES = PAST // PAGE
NPOOL = 2560
NH = 4
DFF = 2816
NIN = 8712
GCH = 1536
EPS = 1e-6
TT = 512
NCORES = 8
SPC = SB // NCORES
ST = SPC * SL
O_HQ, O_HF, O_HI, O_HOG = 0, 512, 1024, 1536
O_DQ, O_DK, O_DV = 2048, 2560, 3072
O_GQ, O_GK, O_GV, O_GZ = 3584, 4096, 4608, 5120
O_GB, O_GA = 5632, 5636
O_GA_, O_GTA, O_GTB, O_GTC = 5636, 5640, 6664, 7688


def _decl(nc, name, shape, dt, kind):
    return nc.dram_tensor(name, list(shape), dt, kind=kind).ap()


IN_SPECS = [
    ("xT_p", (D, SEQ)), ("xT_s", (D, ST)),
    ("st_hgrn", (DEPTH, SPC, NH, 128, 128)), ("st_gdn", (DEPTH, SPC, NH, 128, 128)),
    ("st_gconv", (DEPTH, SPC, GCH, 3)), ("st_fconv", (DEPTH, SPC, DFF, 2)),
    ("cache_k", (DEPTH, NPOOL, PAGE, NH, 128)), ("cache_v", (DEPTH, NPOOL, PAGE, NH, 128)),
    ("w_in", (DEPTH, D, NIN)),
    ("w_branch_a", (DEPTH, 512, D)), ("w_branch_b", (DEPTH, 512, D)), ("w_branch_c", (DEPTH, 512, D)),
    ("w_out", (DEPTH, D, D)), ("w_up", (DEPTH, D, 2 * DFF)), ("w_down", (DEPTH, DFF, D)),
]
OUT_SPECS = [
    ("yT_p", (D, SEQ)), ("yT_s", (D, ST)),
    ("p_hgrn", (DEPTH, NH, 128, 128)), ("p_kT", (DEPTH, NH, 128, SEQ)), ("p_v", (DEPTH, SEQ, 512)),
    ("p_gdn", (DEPTH, NH, 128, 128)), ("p_gconv", (DEPTH, GCH, 3)), ("p_fconv", (DEPTH, DFF, 2)),
    ("s_hgrn", (DEPTH, SPC, NH, 128, 128)), ("s_kT", (DEPTH, NH, 128, ST)), ("s_v", (DEPTH, ST, 512)),
    ("s_gdn", (DEPTH, SPC, NH, 128, 128)), ("s_gconv", (DEPTH, SPC, GCH, 3)),
    ("s_fconv", (DEPTH, SPC, DFF, 2)),
]


C_ID, C_ONES, C_BLK, C_ROT, C_M128 = 0, 128, 256, 384, 512
C_MI64, C_MS64, C_MI8, C_MS8, C_RST512, C_RST32, C_IOTA = 640, 1152, 1664, 1696, 1728, 2240, 2272
C_LS64, C_LS8, C_BD32 = 2273, 2785, 2817
NCONST = 2849


def _const_table():
    t = np.zeros((128, NCONST), np.float32)
    p = np.arange(128)
    t[:, C_ID:C_ID + 128] = np.eye(128)
    t[:, C_ONES:C_ONES + 128] = 1.0
    t[:, C_BLK:C_BLK + 128] = (p[:, None] // 64 == p[None, :] // 64)
    rot = np.zeros((128, 128), np.float32)
    for m in range(128):
        if m % 64 < 32:
            rot[m + 32, m] = -1.0
        else:
            rot[m - 32, m] = 1.0
    t[:, C_ROT:C_ROT + 128] = rot
    t[:, C_M128:C_M128 + 128] = (p[None, :] >= p[:, None])
    s64 = np.arange(64)
    mi = (s64[None, :] >= s64[:, None]).astype(np.float32)
    ms = (s64[None, :] > s64[:, None]).astype(np.float32)
    t[:64, C_MI64:C_MI64 + 512] = np.tile(mi, (1, 8))
    t[:64, C_MS64:C_MS64 + 512] = np.tile(ms, (1, 8))
    s8 = np.arange(8)
    t[:8, C_MI8:C_MI8 + 32] = np.tile((s8[None, :] >= s8[:, None]).astype(np.float32), (1, 4))
    t[:8, C_MS8:C_MS8 + 32] = np.tile((s8[None, :] > s8[:, None]).astype(np.float32), (1, 4))
    t[:64, C_LS64:C_LS64 + 512] = np.tile((s64[None, :] < s64[:, None]).astype(np.float32), (1, 8))
    t[:8, C_LS8:C_LS8 + 32] = np.tile((s8[None, :] < s8[:, None]).astype(np.float32), (1, 4))
    i32 = np.arange(32)
    t[:32, C_BD32:C_BD32 + 32] = ((i32[:, None] // 8 == i32[None, :] // 8) & (i32[:, None] % 8 <= i32[None, :] % 8))
    t[:, C_RST512:C_RST512 + 512] = (np.arange(512) % 64 != 0)
    t[:, C_RST32:C_RST32 + 32] = (np.arange(32) % 8 != 0)
    t[:, C_IOTA] = p
    return t


def _rope_table(pos):
    inv = 1.0 / (10000.0 ** (np.arange(0, 64, 2, dtype=np.float32) / 64.0))
    f = inv[(np.arange(128) % 64) % 32].astype(np.float32)
    ang = f[:, None] * np.asarray(pos, np.float32)[None, :]
    return np.stack([np.cos(ang), np.sin(ang)]).astype(np.float32)


P_LNM, P_LNF, P_LBR, P_HN, P_SUBLN, P_GN, P_QKG = 0, 16, 32, 40, 42, 44, 46
P_LAMR, P_GCW, P_FCW, P_GA, P_GDT, NPRM = 50, 562, 658, 790, 798, 806


def _pack_params(inp):
    f = np.float32
    t = np.zeros((128, NPRM), f)
    g = lambda k: np.asarray(inp[k], f)
    t[:, P_LNM:P_LNM + 16] = g("ln_mix").reshape(DEPTH, 8, 128).transpose(2, 0, 1).reshape(128, 16)
    t[:, P_LNF:P_LNF + 16] = g("ln_ffn").reshape(DEPTH, 8, 128).transpose(2, 0, 1).reshape(128, 16)
    t[:, P_LBR:P_LBR + 8] = g("hgrn_lb").reshape(DEPTH, 4, 128).transpose(2, 0, 1).reshape(128, 8)
    t[:, P_HN:P_HN + 2] = g("hgrn_norm").T
    t[:, P_SUBLN:P_SUBLN + 2] = g("diff_subln").T
    t[:, P_GN:P_GN + 2] = g("gdn_norm").T
    qk = g("diff_qk_norm").transpose(2, 0, 1).reshape(64, 4)
    t[:, P_QKG:P_QKG + 4] = np.concatenate([qk, qk], 0)
    t[:, P_LAMR:P_LAMR + 512] = g("diff_lambda").reshape(1, 512)
    t[:, P_GCW:P_GCW + 96] = g("gdn_conv").reshape(DEPTH, 4, 12, 128).transpose(3, 0, 2, 1).reshape(128, 96)
    t[:, P_FCW:P_FCW + 132] = g("ffn_conv").reshape(DEPTH, 3, 22, 128).transpose(3, 0, 2, 1).reshape(128, 132)
    t[:, P_GA:P_GA + 8] = g("gdn_a_log").reshape(1, 8)
    t[:, P_GDT:P_GDT + 8] = g("gdn_dt_bias").reshape(1, 8)
    return t


class Cfg:
    def __init__(self, seq=SEQ, npool=NPOOL, depth=DEPTH, prompt=True, samples=True, upto="all",
                 dbg=()):
        self.seq, self.npool, self.depth = seq, npool, depth
        self.prompt, self.samples, self.upto, self.dbg = prompt, samples, upto, tuple(dbg)


class Grp:
    def __init__(self, kind, n, nseq, c, nch, tile=0):
        self.kind, self.n, self.nseq, self.c, self.nch, self.tile = kind, n, nseq, c, nch, tile
        self.L = n // nseq


class Builder:
    def __init__(self, cfg):
        self.cfg = cfg
        nc = bass.Bass("TRN2", target_bir_lowering=False)
        nc.allow_low_precision("bf16 matmul operands with fp32 PSUM accumulation (problem statement)")
        self.nc = nc
        self.P = Prog(nc)
        self.P.barrier_chans.add("const")
        specs = dict(IN_SPECS)
        specs["xT_p"] = (D, cfg.seq)
        specs["cache_k"] = (DEPTH, cfg.npool, PAGE, NH, 128)
        specs["cache_v"] = (DEPTH, cfg.npool, PAGE, NH, 128)
        self.I = {n: _decl(nc, n, s, F32, "ExternalInput") for n, s in specs.items()}
        self.I["page_table"] = _decl(nc, "page_table", (SPC, NPAGES), I32, "ExternalInput")
        self.I["consts"] = _decl(nc, "consts", (128, NCONST), F32, "ExternalInput")
        self.I["params"] = _decl(nc, "params", (128, NPRM), F32, "ExternalInput")
        self.I["rope_p"] = _decl(nc, "rope_p", (2, 128, cfg.seq), F32, "ExternalInput")
        self.I["rope_s"] = _decl(nc, "rope_s", (2, 128, ST), F32, "ExternalInput")
        ospecs = dict(OUT_SPECS)
        ospecs["yT_p"] = (D, cfg.seq)
        ospecs["p_kT"] = (DEPTH, NH, 128, cfg.seq)
        ospecs["p_v"] = (DEPTH, cfg.seq, 512)
        self.O = {n: _decl(nc, n, s, F32, "ExternalOutput") for n, s in ospecs.items()}
        self.dbg = {}
        self.out_chans = []
        self.wslot = 0
        self.psi = 0

    def sb(self, name, shape, dt=F32):
        return self.P.view(self.st.enter_context(self.nc.sbuf_tensor(name, list(shape), dt))[:], name)

    def psum(self, name, shape, dt=F32):
        return self.P.view(self.st.enter_context(self.nc.psum_tensor(name, list(shape), dt))[:], name)

    def ps(self):
        v = self.PS[self.psi % 4]
        self.psi += 1
        return v

    def dbg_out(self, name, view, shape):
        if name not in self.cfg.dbg:
            return
        t = _decl(self.nc, "dbg_" + name, shape, view.ap.dtype, "ExternalOutput")
        self.P.dma("sp", "dbg_" + name, t, view)
        self.out_chans.append("dbg_" + name)

    def out_dma(self, chan, dst, src, queue="sp"):
        if chan not in self.out_chans:
            self.out_chans.append(chan)
        self.P.dma(queue, chan, dst, src)

    def setup(self):
        P, I = self.P, self.I
        dp = DEPTH
        self.CONST = self.sb("CONST", [128, NCONST])
        P.dma("sp", "const", self.CONST, I["consts"])
        C = self.CONST
        self.ID = C[:, C_ID:C_ID + 128]
        self.ONES = C[:, C_ONES:C_ONES + 128]
        self.BLK = C[:, C_BLK:C_BLK + 128]
        self.ROT = C[:, C_ROT:C_ROT + 128]
        self.M128 = C[:, C_M128:C_M128 + 128]
        self.IDB = self.sb("IDB", [128, 128], BF16)
        P.copy("dve", self.IDB, self.ID)
        self.ONESB = self.sb("ONESB", [128, 128], BF16)
        P.copy("dve", self.ONESB, self.ONES)
        self.PS = [self.psum(f"PS{i}", [128, 512]) for i in range(8)]
        PRM = self.sb("PRM", [128, NPRM])
        P.dma("sp", "const", PRM, I["params"])
        self.LNM = PRM[:, P_LNM:P_LNM + 16].rs("p (l c) -> p l c", l=dp)
        self.LNF = PRM[:, P_LNF:P_LNF + 16].rs("p (l c) -> p l c", l=dp)
        LBR = PRM[:, P_LBR:P_LBR + 8].rs("p (l h) -> p l h", l=dp)
        self.HN = PRM[:, P_HN:P_HN + 2]
        self.SUBLN = PRM[:, P_SUBLN:P_SUBLN + 2]
        self.GN = PRM[:, P_GN:P_GN + 2]
        self.QKG = PRM[:, P_QKG:P_QKG + 4].rs("p (l j) -> p l j", l=dp)
        LAMR = PRM[:, P_LAMR:P_LAMR + 512].rs("p (l j d) -> p l j d", l=dp, j=4)
        self.GCW = PRM[:, P_GCW:P_GCW + 96].rs("p (l c j) -> p l c j", l=dp, c=12)
        self.FCW = PRM[:, P_FCW:P_FCW + 132].rs("p (l c j) -> p l c j", l=dp, c=22)
        self.GDT = PRM[:, P_GDT:P_GDT + 8]
        self.GA = self.sb("GA", [128, dp * 4])
        self.LB = self.sb("LB", [128, dp, 4])
        self.LB1M = self.sb("LB1M", [128, dp, 4])
        E = self.sb("LBE", [128, dp, 4])
        P.act(E, LBR, AF.Exp)
        S = self.sb("LBS", [128, 4])
        P.tt("dve", S, E[:, 0, :], E[:, 1, :], ALU.add)
        P.recip(S, S)
        P.memset("dve", self.LB[:, 0, :], 0.0)
        P.tt("dve", self.LB[:, 1, :], E[:, 1, :], S, ALU.mult)
        P.ts("dve", self.LB1M, self.LB, -1.0, ALU.mult, 1.0, ALU.add)
        self.LAM = self.sb("LAM", [128, dp])
        self.NLAM = self.sb("NLAM", [128, dp])
        PR = self.sb("LAMP", [128, dp, 2, 64])
        SM = self.sb("LAMS", [128, dp, 2])
        for l in range(dp):
            P.tt("dve", PR[:, l, 0, :], LAMR[:, l, 0, :], LAMR[:, l, 1, :], ALU.mult)
            P.tt("dve", PR[:, l, 1, :], LAMR[:, l, 2, :], LAMR[:, l, 3, :], ALU.mult)
            for j in range(2):
                P.reduce("dve", SM[:, l, j:j + 1], PR[:, l, j, :], ALU.add)
        P.act(SM, SM, AF.Exp)
        for l in range(dp):
            lam_init = 0.8 - 0.6 * float(np.exp(-0.3 * l))
            P.tt("dve", self.LAM[:, l:l + 1], SM[:, l, 0:1], SM[:, l, 1:2], ALU.subtract)
            P.ts("dve", self.LAM[:, l:l + 1], self.LAM[:, l:l + 1], lam_init, ALU.add)
        P.ts("dve", self.NLAM, self.LAM, -1.0, ALU.mult)
        P.act(self.GA, PRM[:, P_GA:P_GA + 8], AF.Exp)
        P.ts("dve", self.GA, self.GA, -1.0, ALU.mult)
        self.NW = 4
        self.W16 = [self.sb(f"W16_{i}", [128, 4096], BF16) for i in range(self.NW)]
        self.H = self.sb("H", [128, 8, TT], BF16)
        self.FT = [self.sb(f"FT{i}", [128, TT]) for i in range(10)]
        self.BT = [self.sb(f"BT{i}", [128, TT], BF16) for i in range(6)]
        scr = self.st.enter_context(self.nc.sbuf_tensor("SCR", [128, 3 * 4 * TT], F32))[:]
        stoks = [P.tok(f"SCR{i}") for i in range(3)]
        self.F4 = [View(scr[:, i * 4 * TT:(i + 1) * 4 * TT].rearrange("p (a b) -> p a b", a=4), [stoks[i]])
                   for i in range(3)]
        self.A16 = View(scr.bitcast(BF16)[:, 0:22 * TT].rearrange("p (a b) -> p a b", a=22), stoks)
        self.HST = View(scr[:, 2 * 4 * TT:3 * 4 * TT], [stoks[2]])
        self.OABC = [self.sb(f"O{x}", [128, 4, TT], BF16) for x in "abc"]

    def precast_weights(self):
        P, nc = self.P, self.nc
        self.WB = {}
        stage32 = [self.F4[0].rs("p a b -> p (a b)"), self.F4[1].rs("p a b -> p (a b)")]
        engs = ("act", "dve", "pool")
        i = 0
        for name in ("w_in", "w_branch_a", "w_branch_b", "w_branch_c", "w_out", "w_up", "w_down"):
            self.WB[name] = []
            for l in range(self.cfg.depth):
                src = self.I[name][l]
                rows, cols = src.shape
                t = nc.dram_tensor(f"wb_{name}_{l}", [rows, cols], BF16, kind="Internal").ap()
                rtoks = {r: P.tok(f"wb_{name}_{l}") for r in range(0, rows, 128)}
                wv = View(t, list(rtoks.values()))
                self.WB[name].append(wv)
                for r0 in range(0, rows, 128):
                    for c0 in range(0, cols, 2048):
                        w = min(2048, cols - c0)
                        s32 = stage32[i % 2][:, 0:w]
                        s16 = self.W16[i % self.NW][:, 0:w]
                        P.dma("sp", f"pcl{i % 2}", s32, src[r0:r0 + 128, c0:c0 + w])
                        P.copy(engs[i % 3], s16, s32)
                        P.dma("pool", f"pcs{i % self.NW}", View(t[r0:r0 + 128, c0:c0 + w], [rtoks[r0]]), s16)
                        i += 1

    def wpanel(self, wv, r0, nk, c0, ncols):
        P = self.P
        slot = self.wslot
        self.wslot = (self.wslot + 1) % self.NW
        w16 = self.W16[slot]
        v16 = w16.with_ap(w16.ap[:, 0:nk * ncols].rearrange("p (k n) -> p k n", k=nk))
        P.dma("sp", f"w{slot}", v16, wv[r0:r0 + nk * 128, c0:c0 + ncols].rs("(k p) n -> p k n", p=128))
        return v16

    def gemm_fm(self, w16, nk, rhs, n, consume, nm=None):
        ncols = w16.ap.shape[2]
        for m in range(nm if nm is not None else ncols // 128):
            pv = self.ps()[:, 0:n]
            self.P.mm(pv, [(w16[:, k, m * 128:(m + 1) * 128], rhs(k)) for k in range(nk)])
            consume(m, pv)

    def gemm_tm(self, w16, nk, lhs, g, consume):
        ncols = w16.ap.shape[2]
        for ch in range(g.nch):
            pv = self.ps()[0:g.c, 0:ncols]
            self.P.mm(pv, [(lhs(k, ch), w16[:, k, :]) for k in range(nk)])
            consume(ch, pv)

    def rmsnorm_fm(self, X, gain, Hout, n, nk=8, dim=D):
        P = self.P
        pv = self.ps()[:, 0:n]
        for k in range(nk):
            sq = self.FT[8 + k % 2][:, 0:n]
            P.act(sq, X[:, k, 0:n], AF.Square)
            P.mm(pv, [(self.ONES, sq)], start=(k == 0), stop=(k == nk - 1))
        rs = self.FT[7][:, 0:n]
        P.ts("dve", rs, pv, 1.0 / dim, ALU.mult, EPS, ALU.add)
        P.rsqrt(rs, rs)
        for k in range(nk):
            P.stt("dve", Hout[:, k, 0:n], X[:, k, 0:n], gain[:, k:k + 1], rs, ALU.mult, ALU.mult)


    def alloc_mixers(self):
        self.VT16 = self.sb("VT16", [64, 8, 512], BF16)
        self.KHT = self.sb("KHT", [64, 8, 128], BF16)
        dp = self.cfg.depth
        self.HS32 = self.sb("HS32", [128, dp, 4, 128])
        self.HS16 = self.sb("HS16", [128, dp, 4, 128], BF16)
        self.GS32 = self.sb("GS32", [128, dp, 4, 128])
        self.GS16 = self.sb("GS16", [128, dp, 4, 128], BF16)
        for t in (self.HS32, self.HS16, self.GS32, self.GS16):
            self.P.memset("pool", t, 0.0)
        self.KH16 = self.sb("KH16", [128, SEQ], BF16)
        self.VH16 = self.sb("VH16", [128, SEQ // 128, 128], BF16)
        self.ROPE = self.sb("ROPE", [128, 2, TT])
        self.VST = [self.sb(f"VST{i}", [128, 512]) for i in range(2)]
        self.KST = [self.sb(f"KST{i}", [128, TT]) for i in range(2)]
        self.Q16 = self.sb("Q16", [128, TT], BF16)
        self.vsti = 0
        self.TAILG = self.sb("TAILG", [128, DEPTH, 12, 3])
        self.TAILF = self.sb("TAILF", [128, DEPTH, 22, 2])
        self.P.memset("pool", self.TAILG, 0.0)
        self.P.memset("pool", self.TAILF, 0.0)
        self.CB = self.sb("CB", [128, 3 + TT])
        self.BG = self.sb("BG", [64, 8, 8])
        self.BETA = self.sb("BETA", [64, 8, 4])
        self.GTT = self.sb("GTT", [64, 8, 4])
        self.GT = self.sb("GT", [64, 8, 4])
        self.EGT = self.sb("EGT", [64, 8, 4])
        self.NBEG = self.sb("NBEG", [64, 8, 4])
        self.EH = self.sb("EH", [64, 8])
        self.KT16 = self.sb("KT16", [64, 8, 128], BF16)
        self.KHAT = self.sb("KHAT", [64, 8, 128], BF16)
        self.BV = self.sb("BV", [64, 8, 128])
        self.QKM = self.sb("QKM", [64, TT], BF16)
        self.RHS = [self.sb(f"RHS{i}", [64, 128]) for i in range(2)]
        self.U16 = [self.sb(f"U16_{i}", [64, 128], BF16) for i in range(2)]
        nt = self.cfg.seq // TT
        self.T_PK = [[[self.P.tok("pk") for _ in range(nt)] for h in range(4)] for l in range(DEPTH)]
        self.T_PV = [[self.P.tok("pv") for _ in range(nt)] for l in range(DEPTH)]

    def hgrn(self, g, l, S32, S16, pre=None, post=None):
        P = self.P
        n, c, nch = g.n, g.c, g.nch
        H, W = self.H, self.WB["w_in"][l]
        Q, SG, OG = self.F4
        VT, OA = self.VT16, self.OABC[0]
        C = self.CONST
        rst = C[:, C_RST512:C_RST512 + n] if g.kind == "p" else C[:, C_RST32:C_RST32 + n]
        mi0 = C_MI64 if g.kind == "p" else C_MI8
        maskT = C[0:c, mi0:mi0 + n]

        def Hn(k):
            return H[:, k, 0:n]

        def Hc(k, ch):
            return H[:, k, ch * c:(ch + 1) * c]

        w = self.wpanel(W, 0, 8, O_HI, 512)
        self.gemm_tm(w, 8, Hc, g, lambda ch, pv: P.copy("act", VT[0:c, ch, :], pv))
        for off, dst, fn in ((O_HQ, Q, AF.Silu), (O_HF, SG, AF.Sigmoid), (O_HOG, OG, AF.Silu)):
            w = self.wpanel(W, 0, 8, off, 512)
            self.gemm_fm(w, 8, Hn, n, lambda m, pv, dst=dst, fn=fn: P.act(dst[:, m, 0:n], pv, fn))
        cm = c // 2 - 1
        for h in range(4):
            fg, kk, lf, G, d1, EQ, e2 = [self.FT[i][:, 0:n] for i in range(7)]
            QG, QC, KC, KH = [self.BT[i][:, 0:n] for i in range(4)]
            AT = self.BT[4][0:c, 0:n]
            P.ts("dve", fg, SG[:, h, 0:n], self.LB1M[:, l, h:h + 1], ALU.mult, self.LB[:, l, h:h + 1], ALU.add)
            P.ts("dve", kk, fg, -1.0, ALU.mult, 1.0, ALU.add)
            P.act(lf, fg, AF.Ln)
            P.scan(G, rst, lf)
            G3 = G.r3(c)
            P.act(EQ, G, AF.Exp)
            P.tt("dve", QG, Q[:, h, 0:n], EQ, ALU.mult)
            P.tt("dve", d1.r3(c), G3, G3[:, :, cm:cm + 1].bc([128, nch, c]), ALU.subtract)
            P.act(e2, d1, AF.Exp)
            P.tt("dve", QC, Q[:, h, 0:n], e2, ALU.mult)
            P.act(e2, d1, AF.Exp, scale=-1.0)
            P.tt("dve", KC, kk, e2, ALU.mult)
            P.tt("dve", d1.r3(c), G3, G3[:, :, c - 1:c].bc([128, nch, c]), ALU.subtract)
            P.act(e2, d1, AF.Exp, scale=-1.0)
            P.tt("dve", KH, kk, e2, ALU.mult)
            PA = self.PS[4]
            P.mm_multi(PA, [(PA.ap[0:c, ch * c:(ch + 1) * c],
                             [(KC.ap[:, ch * c:(ch + 1) * c], QC.ap[:, ch * c:(ch + 1) * c])])
                            for ch in range(nch)], reads=[KC, QC])
            P.tt("dve", AT, PA[0:c, 0:n], maskT, ALU.mult)
            PTB = self.PS[7].bitcast(BF16)
            P.transpose_multi(PTB, [(PTB.ap[0:c, ch * 128:(ch + 1) * 128], KH.ap[:, ch * c:(ch + 1) * c],
                                     self.IDB.ap) for ch in range(nch)], reads=[KH, self.IDB])
            P.copy("act", self.KHT[0:c, 0:nch, :], PTB[0:c, 0:nch * 128].r3(128))
            PO = self.PS[5]
            if pre:
                pre(h)
            for ch in range(nch):
                cs = slice(ch * c, (ch + 1) * c)
                vch = VT[0:c, ch, h * 128:(h + 1) * 128]
                P.mm(PO[:, cs], [(S16(ch, h), QG[:, cs]), (vch, AT[:, cs])])
                su = self.PS[6][:, (ch % 4) * 128:(ch % 4 + 1) * 128]
                P.mm(su, [(self.KHT[0:c, ch, :], vch)])
                P.stt("dve", S32(ch, h), S32(ch, h), EQ[:, ch * c + c - 1:ch * c + c], su, ALU.mult, ALU.add)
                P.copy("act", S16(ch, h), S32(ch, h))
            if post:
                post(h)
            sq, rs, t1 = self.FT[8][:, 0:n], self.FT[7][:, 0:n], self.FT[9][:, 0:n]
            P.act(sq, PO[:, 0:n], AF.Square)
            pss = self.ps()[:, 0:n]
            P.mm(pss, [(self.ONES, sq)])
            P.ts("dve", rs, pss, 1.0 / 128, ALU.mult, EPS, ALU.add)
            P.rsqrt(rs, rs)
            P.stt("dve", t1, PO[:, 0:n], self.HN[:, l:l + 1], rs, ALU.mult, ALU.mult)
            P.tt("dve", OA[:, h, 0:n], t1, OG[:, h, 0:n], ALU.mult)

    def qknorm_rope(self, x, gain, out):
        P = self.P
        n = x.ap.shape[1]
        sq, rs, xn, t1 = [self.FT[i][:, 0:n] for i in (8, 7, 6, 5)]
        P.act(sq, x, AF.Square)
        pss = self.ps()[:, 0:n]
        P.mm(pss, [(self.BLK, sq)])
        P.ts("dve", rs, pss, 1.0 / 64, ALU.mult, EPS, ALU.add)
        P.rsqrt(rs, rs)
        P.stt("dve", xn, x, gain, rs, ALU.mult, ALU.mult)
        pr = self.ps()[:, 0:n]
        P.mm(pr, [(self.ROT, xn)])
        P.tt("dve", t1, xn, self.ROPE[:, 0, 0:n], ALU.mult)
        P.tt("dve", xn, pr, self.ROPE[:, 1, 0:n], ALU.mult)
        P.tt("dve", out, t1, xn, ALU.add)

    def attn_prompt(self, g, l):
        P = self.P
        n, t = g.n, g.tile
        H, W = self.H, self.WB["w_in"][l]
        DQ, DK = self.F4[0], self.F4[1]
        OB = self.OABC[1]
        tok0 = t * TT

        def Hn(k):
            return H[:, k, 0:n]

        P.dma("pool", "rope", self.ROPE[:, :, 0:n], self.I["rope_p"][:, :, tok0:tok0 + n].rearrange("a p t -> p a t"))
        for off, dst in ((O_DQ, DQ), (O_DK, DK)):
            w = self.wpanel(W, 0, 8, off, 512)
            self.gemm_fm(w, 8, Hn, n, lambda m, pv, dst=dst: P.copy("act", dst[:, m, 0:n], pv))
        w = self.wpanel(W, 0, 8, O_DV, 512)
        for b in range(n // 128):
            pv = self.ps()
            P.mm(pv, [(H[:, k, b * 128:(b + 1) * 128], w[:, k, :]) for k in range(8)])
            vs = self.VST[self.vsti % 2]
            self.vsti += 1
            P.copy("act", vs, pv)
            self.out_dma("pv", View(self.O["p_v"][l][tok0 + b * 128:tok0 + (b + 1) * 128, :], [self.T_PV[l][t]]), vs, queue="pool")
        nkb = (tok0 + n) // 128
        for h in range(4):
            ks = self.KST[h % 2][:, 0:n]
            self.qknorm_rope(DK[:, h, 0:n], self.QKG[:, l, 1:2], ks)
            self.out_dma(f"pk{h}", View(self.O["p_kT"][l, h][:, tok0:tok0 + n], [self.T_PK[l][h][t]]), ks, queue="pool")
            self.qknorm_rope(DQ[:, h, 0:n], self.QKG[:, l, 0:1], self.Q16[:, 0:n])
            P.copy("act", self.KH16[:, tok0:tok0 + n], ks)
            for p0 in range(0, tok0, 2048):
                p1 = min(tok0, p0 + 2048)
                st = self.HST[:, 0:p1 - p0]
                toks = [self.T_PK[l][h][j] for j in range(p0 // TT, (p1 - 1) // TT + 1)]
                P.dma("pool", "hist", st, View(self.O["p_kT"][l, h][:, p0:p1], toks))
                P.copy("act", self.KH16[:, p0:p1], st)
            for p0 in range(0, tok0 + n, 2048):
                p1 = min(tok0 + n, p0 + 2048)
                nb = (p1 - p0) // 128
                st3 = self.HST[:, 0:nb * 128].rs("p (b d) -> p b d", d=128)
                toks = [self.T_PV[l][j] for j in range(p0 // TT, (p1 - 1) // TT + 1)]
                P.dma("pool", "hist", st3, View(self.O["p_v"][l][p0:p1, h * 128:(h + 1) * 128].rearrange("(b p) d -> p b d", p=128), toks))
                P.copy("dve", self.VH16[:, p0 // 128:p0 // 128 + nb, :], st3)
            acc = [self.PS[4], self.PS[5], self.PS[6], self.PS[7]]
            ei = 0
            for half in range(2):
                hs = slice(half * 64, half * 64 + 64)
                O_, Z_ = acc[2 * half], acc[2 * half + 1]
                for kb in range(nkb):
                    r = kb - tok0 // 128
                    qlo = max(0, r * 128)
                    sp = self.ps()[:, 0:n - qlo]
                    P.mm(sp, [(self.KH16[hs, kb * 128:(kb + 1) * 128], self.Q16[hs, qlo:n])])
                    e = self.BT[ei % 6][:, 0:n - qlo]
                    ei += 1
                    P.act(e, sp, AF.Exp, scale=0.125)
                    if r >= 0:
                        P.tt("dve", e[:, 0:128], e[:, 0:128], self.M128, ALU.mult)
                    P.mm(O_[:, qlo:n], [(self.VH16[:, kb, :], e)], start=(kb == 0), stop=(kb == nkb - 1))
                    P.mm(Z_[:, qlo:n], [(self.ONESB, e)], start=(kb == 0), stop=(kb == nkb - 1))
            r1, r2, o1, o2 = [self.FT[i][:, 0:n] for i in range(4)]
            P.recip(r1, acc[1][:, 0:n])
            P.recip(r2, acc[3][:, 0:n])
            P.tt("dve", o1, acc[0][:, 0:n], r1, ALU.mult)
            P.tt("dve", o2, acc[2][:, 0:n], r2, ALU.mult)
            P.stt("dve", o1, o2, self.NLAM[:, l:l + 1], o1, ALU.mult, ALU.add)
            self.subln(o1, l, OB[:, h, 0:n])

    def subln(self, od, l, out):
        P = self.P
        n = od.ap.shape[1]
        sq, rs, t1 = self.FT[8][:, 0:n], self.FT[7][:, 0:n], self.FT[9][:, 0:n]
        P.act(sq, od, AF.Square)
        pss = self.ps()[:, 0:n]
        P.mm(pss, [(self.ONES, sq)])
        P.ts("dve", rs, pss, 1.0 / 128, ALU.mult, EPS, ALU.add)
        P.rsqrt(rs, rs)
        P.stt("dve", t1, od, self.SUBLN[:, l:l + 1], rs, ALU.mult, ALU.mult)
        lam_init = 0.8 - 0.6 * float(np.exp(-0.3 * l))
        P.ts("dve", out, t1, 1.0 - lam_init, ALU.mult)

    def conv_fm(self, g, pv, wcol, tail, ntap, dst, hist_src=None, tail_dst=None):
        P = self.P
        L, ns, hl = g.L, g.nseq, ntap - 1
        cb = self.CB[:, 0:ns * (hl + L)].rs("p (s j) -> p s j", s=ns)
        if g.kind == "p":
            P.copy("act", cb[:, 0, 0:hl], tail)
        else:
            P.dma("pool", "chist", cb[:, :, 0:hl], hist_src)
        P.copy("act", cb[:, :, hl:hl + L], pv.rs("p (s j) -> p s j", s=ns))
        d3 = dst.rs("p (s j) -> p s j", s=ns)
        P.ts("dve", d3, cb[:, :, 0:L], wcol(0), ALU.mult)
        for j in range(1, ntap):
            P.stt("dve", d3, cb[:, :, j:j + L], wcol(j), d3, ALU.mult, ALU.add)
        if g.kind == "p":
            P.copy("act", tail, cb[:, 0, L:L + hl])
        else:
            self.out_dma(tail_dst[0], tail_dst[1], cb[:, :, L:L + hl], queue="pool")

    def gdn(self, g, l, S32, S16, hist=None, tails=None, pre=None, post=None):
        P = self.P
        n, c, nch = g.n, g.c, g.nch
        H, W = self.H, self.WB["w_in"][l]
        C = self.CONST
        isp = g.kind == "p"
        MI = C[0:c, (C_MI64 if isp else C_MI8):][:, 0:n]
        MS = C[0:c, (C_MS64 if isp else C_MS8):][:, 0:n]
        LS = C[0:c, (C_LS64 if isp else C_LS8):][:, 0:n]
        IDb = C[0:c, C_ID:C_ID + c].rs("p (a t) -> p a t", a=1).bc([c, nch, c])
        OC = self.OABC[2]
        QF, KF, VF = self.F4

        def Hn(k):
            return H[:, k, 0:n]

        def Hc(k, ch):
            return H[:, k, ch * c:(ch + 1) * c]

        def cols(v, ch):
            return v.ap[:, ch * c:(ch + 1) * c]

        for j3, (off, dst) in enumerate(((O_GQ, QF), (O_GK, KF), (O_GV, VF))):
            w = self.wpanel(W, 0, 8, off, 512)

            def cons(m, pv, j3=j3, dst=dst):
                kc = j3 * 4 + m
                pre = self.FT[9][:, 0:n]
                self.conv_fm(g, pv, lambda j: self.GCW[:, l, kc, j:j + 1], self.TAILG[:, l, kc, :], 4, pre,
                             hist_src=None if isp else hist(kc), tail_dst=None if isp else tails(kc))
                P.act(dst[:, m, 0:n], pre, AF.Silu)
            self.gemm_fm(w, 8, Hn, n, cons)
        w = self.wpanel(W, 0, 8, O_GZ, 512)
        self.gemm_fm(w, 8, Hn, n, lambda m, pv: P.act(OC[:, m, 0:n], pv, AF.Silu))
        w = self.wpanel(W, 0, 8, O_GB, 8)
        BG, BETA, GTT, GT, EGT, NBEG = [x[0:c, 0:nch, :] for x in
                                        (self.BG, self.BETA, self.GTT, self.GT, self.EGT, self.NBEG)]
        self.gemm_tm(w, 8, Hc, g, lambda ch, pv: P.copy("act", BG[:, ch, :], pv))
        P.act(BETA, BG[:, :, 0:4], AF.Sigmoid)
        P.tt("dve", GTT, BG[:, :, 4:8], self.GDT[0:c, l * 4:l * 4 + 4].rs("p (a h) -> p a h", a=1).bc([c, nch, 4]), ALU.add)
        P.act(GTT, GTT, AF.Exp)
        P.act(GTT, GTT, AF.Ln, bias=1.0)
        P.tt("dve", GTT, GTT, self.GA[0:c, l * 4:l * 4 + 4].rs("p (a h) -> p a h", a=1).bc([c, nch, 4]), ALU.mult)
        pg = self.ps()[0:c, 0:nch * 4]
        P.mm(pg, [(MI[:, 0:c], GTT.rs("p a h -> p (a h)"))])
        P.copy("act", GT.rs("p a h -> p (a h)"), pg)
        P.act(EGT, GT, AF.Exp)
        P.stt("dve", NBEG, BETA, -1.0, EGT, ALU.mult, ALU.mult)
        nlev = int(round(np.log2(c)))
        EH2 = self.EH[0:c, 0:nch]
        EH3 = EH2.rs("p (a b) -> p a b", b=1)
        for h in range(4):
            qn, kn = self.BT[0][:, 0:n], self.BT[1][:, 0:n]
            for src, dst, sc in ((QF, qn, 128 ** -0.5), (KF, kn, 1.0)):
                sq, rs = self.FT[8][:, 0:n], self.FT[7][:, 0:n]
                P.act(sq, src[:, h, 0:n], AF.Square)
                pss = self.ps()[:, 0:n]
                P.mm(pss, [(self.ONES, sq)])
                P.ts("dve", rs, pss, EPS, ALU.add)
                P.rsqrt(rs, rs)
                P.stt("dve", dst, src[:, h, 0:n], sc, rs, ALU.mult, ALU.mult)
            v16 = self.BT[2][:, 0:n]
            P.copy("act", v16, VF[:, h, 0:n])
            PTB = self.ps().bitcast(BF16)
            P.transpose_multi(PTB, [(PTB.ap[0:c, ch * 128:(ch + 1) * 128], cols(kn, ch), self.IDB.ap)
                                    for ch in range(nch)], reads=[kn, self.IDB])
            KT = self.KT16[0:c, 0:nch, :]
            P.copy("act", KT, PTB[0:c, 0:nch * 128].r3(128))
            PTB = self.ps().bitcast(BF16)
            P.transpose_multi(PTB, [(PTB.ap[0:c, ch * 128:(ch + 1) * 128], cols(v16, ch), self.IDB.ap)
                                    for ch in range(nch)], reads=[v16, self.IDB])
            BV = self.BV[0:c, 0:nch, :]
            P.tt("dve", BV, PTB[0:c, 0:nch * 128].r3(128), BETA[:, :, h:h + 1].bc([c, nch, 128]), ALU.mult)
            GU, raw, dT, dL, Y, Yt = [self.FT[i][0:c, 0:n] for i in (1, 2, 3, 4, 5, 6)]
            tmp, Wm = GU, dL
            P.tt("dve", GU.r3(c), GTT[:, :, h:h + 1].bc([c, nch, c]), MI.r3(c), ALU.mult)
            R = self.PS[4][:, 0:n]
            P.mm(R, [(self.ONES[0:c, :], GU)])
            EG = self.FT[0][:, 0:n]
            P.act(EG, R, AF.Exp)
            qg = self.BT[3][:, 0:n]
            P.tt("dve", qg, qn, EG, ALU.mult)
            Rc = R[0:c, :].r3(c)
            P.tt("dve", raw.r3(c), Rc, GT[:, :, h:h + 1].bc([c, nch, c]), ALU.subtract)
            P.tt("dve", EH3, Rc[:, :, c - 1:c], GT[:, :, h:h + 1], ALU.subtract)
            P.act(EH2, EH2, AF.Exp)
            P.tt("dve", self.KHAT[0:c, 0:nch, :], KT, EH3.bc([c, nch, 128]), ALU.mult)
            P.ts("dve", dT, raw, 0.0, ALU.min)
            P.act(dT, dT, AF.Exp)
            P.tt("dve", dT, dT, MI, ALU.mult)
            P.ts("dve", dL, raw, -1.0, ALU.mult, 0.0, ALU.min)
            P.act(dL, dL, AF.Exp)
            P.tt("dve", dL, dL, LS, ALU.mult)
            PKK, PQK = self.PS[5], self.PS[6]
            P.mm_multi(PKK, [(PKK.ap[0:c, ch * c:(ch + 1) * c], [(cols(kn, ch), cols(kn, ch))]) for ch in range(nch)], reads=[kn])
            P.mm_multi(PQK, [(PQK.ap[0:c, ch * c:(ch + 1) * c], [(cols(kn, ch), cols(qn, ch))]) for ch in range(nch)], reads=[kn, qn])
            QKM = self.QKM[0:c, 0:n]
            P.tt("dve", QKM, PQK[0:c, 0:n], dT, ALU.mult)
            P.tt("dve", tmp.r3(c), BETA[:, :, h:h + 1].bc([c, nch, c]), IDb, ALU.mult)
            PB = self.PS[7][0:c, 0:n]
            P.mm(PB, [(self.ONES[0:c, 0:c], tmp)])
            P.tt("dve", Y, PKK[0:c, 0:n], dT, ALU.mult)
            P.tt("dve", Y, Y, MS, ALU.mult)
            P.stt("dve", Y, PB, -1.0, Y, ALU.mult, ALU.mult)
            P.tt("dve", Yt, PKK[0:c, 0:n], dL, ALU.mult)
            P.stt("dve", Yt.r3(c), BETA[:, :, h:h + 1].bc([c, nch, c]), -1.0, Yt.r3(c), ALU.mult, ALU.mult)
            P.tt("dve", Wm.r3(c), Y.r3(c), IDb, ALU.add)
            for lev in range(1, nlev):
                last = lev == nlev - 1
                pb = self.ps()
                P.mm_multi(pb, [(pb.ap[0:c, ch * c:(ch + 1) * c], [(cols(Y, ch), cols(Yt, ch))]) for ch in range(nch)], reads=[Y, Yt])
                if not last:
                    pa = self.ps()
                    P.mm_multi(pa, [(pa.ap[0:c, ch * c:(ch + 1) * c], [(cols(Yt, ch), cols(Y, ch))]) for ch in range(nch)], reads=[Y, Yt])
                    P.copy("act", Y, pa[0:c, 0:n])
                P.copy("act", Yt, pb[0:c, 0:n])
                pw = self.ps()
                P.mm_multi(pw, [(pw.ap[0:c, ch * c:(ch + 1) * c], [(cols(Yt, ch), cols(Wm, ch))]) for ch in range(nch)], reads=[Yt, Wm])
                P.tt("dve", Wm, Wm, pw[0:c, 0:n], ALU.add)
            PO = self.PS[4]
            if pre:
                pre(h)
            for ch in range(nch):
                cs = slice(ch * c, (ch + 1) * c)
                pk = self.ps()[0:c, 0:128]
                P.mm(pk, [(kn[:, cs], S16(ch, h))])
                rh = self.RHS[ch % 2][0:c, :]
                P.stt("dve", rh, pk, NBEG[:, ch, h:h + 1], BV[:, ch, :], ALU.mult, ALU.add)
                pu = self.ps()[0:c, 0:128]
                P.mm(pu, [(Wm[:, cs], rh)])
                u16 = self.U16[ch % 2][0:c, :]
                P.copy("act", u16, pu)
                P.mm(PO[:, cs], [(S16(ch, h), qg[:, cs]), (u16, QKM[:, cs])])
                su = self.ps()[:, 0:128]
                P.mm(su, [(self.KHAT[0:c, ch, :], u16)])
                P.stt("dve", S32(ch, h), S32(ch, h), EG[:, ch * c + c - 1:ch * c + c], su, ALU.mult, ALU.add)
                P.copy("act", S16(ch, h), S32(ch, h))
            if post:
                post(h)
            sq, rs, t1 = self.FT[8][:, 0:n], self.FT[7][:, 0:n], self.FT[9][:, 0:n]
            P.act(sq, PO[:, 0:n], AF.Square)
            pss = self.ps()[:, 0:n]
            P.mm(pss, [(self.ONES, sq)])
            P.ts("dve", rs, pss, 1.0 / 128, ALU.mult, EPS, ALU.add)
            P.rsqrt(rs, rs)
            P.stt("dve", t1, PO[:, 0:n], self.GN[:, l:l + 1], rs, ALU.mult, ALU.mult)
            P.tt("dve", OC[:, h, 0:n], t1, OC[:, h, 0:n], ALU.mult)

    def merge_out(self, g, l, XR):
        P = self.P
        n = g.n
        H, I = self.H, self.I
        MIX = [self.F4[0], self.F4[1]]

        def Hn(k):
            return H[:, k, 0:n]

        for c0 in (0, 512):
            for bi, (wname, goff) in enumerate((("w_branch_a", O_GTA), ("w_branch_b", O_GTB), ("w_branch_c", O_GTC))):
                O_x = self.OABC[bi]
                wg = self.wpanel(self.WB["w_in"][l], 0, 8, goff + c0, 512)
                wb = self.wpanel(self.WB[wname][l], 0, 4, c0, 512)
                for m in range(4):
                    pg = self.ps()[:, 0:n]
                    P.mm(pg, [(wg[:, k, m * 128:(m + 1) * 128], Hn(k)) for k in range(8)])
                    sg = self.FT[m % 2][:, 0:n]
                    P.act(sg, pg, AF.Sigmoid)
                    pb = self.ps()[:, 0:n]
                    P.mm(pb, [(wb[:, k, m * 128:(m + 1) * 128], O_x[:, k, 0:n]) for k in range(4)])
                    dst = MIX[c0 // 512][:, m, 0:n]
                    if bi == 0:
                        P.tt("dve", dst, pb, sg, ALU.mult)
                    else:
                        tmp = self.FT[2 + m % 2][:, 0:n]
                        P.tt("dve", tmp, pb, sg, ALU.mult)
                        P.tt("pool", dst, dst, tmp, ALU.add)
        for k in range(8):
            P.copy("act", H[:, k, 0:n], MIX[k // 4][:, k % 4, 0:n])
        for c0 in (0, 512):
            w = self.wpanel(self.WB["w_out"][l], 0, 8, c0, 512)
            self.gemm_fm(w, 8, Hn, n, lambda m, pv, c0=c0: P.tt("dve", XR[:, c0 // 128 + m, 0:n], XR[:, c0 // 128 + m, 0:n], pv, ALU.add))

    def ffn(self, g, l, XR, hist=None, tails=None):
        P = self.P
        n = g.n
        H, I = self.H, self.I
        isp = g.kind == "p"
        A16 = self.A16

        def Hn(k):
            return H[:, k, 0:n]

        self.rmsnorm_fm(XR, self.LNF[:, l, :], H, n)
        for c0 in range(0, DFF, 512):
            wd = min(512, DFF - c0)
            wg = self.wpanel(self.WB["w_up"][l], 0, 8, c0, wd)
            wu = self.wpanel(self.WB["w_up"][l], 0, 8, DFF + c0, wd)
            for m in range(wd // 128):
                j = c0 // 128 + m
                pg = self.ps()[:, 0:n]
                P.mm(pg, [(wg[:, k, m * 128:(m + 1) * 128], Hn(k)) for k in range(8)])
                pre = self.FT[9][:, 0:n]
                self.conv_fm(g, pg, lambda t, j=j: self.FCW[:, l, j, t:t + 1], self.TAILF[:, l, j, :], 3, pre,
                             hist_src=None if isp else hist(j), tail_dst=None if isp else tails(j))
                sg = self.FT[m % 2][:, 0:n]
                P.act(sg, pre, AF.Silu)
                pu = self.ps()[:, 0:n]
                P.mm(pu, [(wu[:, k, m * 128:(m + 1) * 128], Hn(k)) for k in range(8)])
                P.tt("dve", A16[:, j, 0:n], pu, sg, ALU.mult)
        for m in range(8):
            w = self.wpanel(self.WB["w_down"][l], 0, 22, m * 128, 128)
            pv = self.ps()[:, 0:n]
            P.mm(pv, [(w[:, k, :], A16[:, k, 0:n]) for k in range(22)])
            P.tt("dve", XR[:, m, 0:n], XR[:, m, 0:n], pv, ALU.add)

    def alloc_samples(self):
        P = self.P
        self.XS = self.sb("XS", [128, 8, ST])
        self.SST32 = self.sb("SST32", [128, SPC, 128])
        self.SST16 = self.sb("SST16", [128, SPC, 128], BF16)
        self.QS16 = self.sb("QS16", [128, 4, ST], BF16)
        self.ZB = self.sb("ZB", [128, 256], BF16)
        P.memset("dve", self.ZB, 0.0)
        self.KZ16 = self.sb("KZ16", [128, 2, 4, ST], BF16)
        P.memset("dve", self.KZ16, 0.0)
        self.VA16 = self.sb("VA16", [ST, 512], BF16)
        npt = SPC * NPAGES
        self.IDX = self.sb("IDX", [128, DEPTH, npt], I32)
        pti = self.FT[2][:, 0:npt].bitcast(I32)
        ptf, idf = self.FT[0][:, 0:npt], self.FT[1][:, 0:npt]
        P.dma("sp", "pt", pti, self.I["page_table"].rearrange("s j -> (s j)").partition_broadcast(128))
        P.copy("dve", ptf, pti)
        P.ts("dve", idf, ptf, float(PAGE), ALU.mult, self.CONST[:, C_IOTA:C_IOTA + 1], ALU.add)
        for l in range(DEPTH):
            if l:
                P.ts("dve", idf, idf, float(self.cfg.npool * PAGE), ALU.add)
            P.copy("dve", self.IDX[:, l, :], idf)
        self.dbg_out("idx", self.IDX, [128, DEPTH, npt])

    def gather(self, chan, dst, rows, idx_col, nrows):
        iap, dap = idx_col.ap, dst.ap

        def fn(eng, sem):
            return eng.indirect_dma_start(
                out=dap, out_offset=None, in_=rows,
                in_offset=bass.IndirectOffsetOnAxis(ap=iap, axis=0)).then_inc(sem, 16)
        return self.P.op("pool", fn, reads=[idx_col], writes=[dst], chan=chan, ndma=1)

    def attn_sample(self, g, l):
        P = self.P
        n = g.n
        H, W = self.H, self.WB["w_in"][l]
        DQ, DK = self.F4[0], self.F4[1]
        OB = self.OABC[1]
        cfg = self.cfg

        def Hn(k):
            return H[:, k, 0:n]

        P.dma("pool", "rope", self.ROPE[:, :, 0:n], self.I["rope_s"].rearrange("a p t -> p a t"))
        for off, dst in ((O_DQ, DQ), (O_DK, DK)):
            w = self.wpanel(W, 0, 8, off, 512)
            self.gemm_fm(w, 8, Hn, n, lambda m, pv, dst=dst: P.copy("act", dst[:, m, 0:n], pv))
        w = self.wpanel(W, 0, 8, O_DV, 512)

        pv = self.ps()[0:n, :]
        P.mm(pv, [(H[:, k, 0:n], w[:, k, :]) for k in range(8)])
        vs = self.VST[0][0:n, :]
        P.copy("act", vs, pv)
        P.copy("act", self.VA16, pv)
        self.out_dma("sv", self.O["s_v"][l], vs, queue="pool")
        for h in range(4):
            ks = self.KST[h % 2][:, 0:n]
            self.qknorm_rope(DK[:, h, 0:n], self.QKG[:, l, 1:2], ks)
            self.out_dma(f"sk{h % 2}", self.O["s_kT"][l, h], ks, queue="pool")
            P.copy("act", self.KZ16[0:64, 0, h, :], ks[0:64, :])
            P.copy("act", self.KZ16[64:128, 1, h, :], ks[64:128, :])
            self.qknorm_rope(DQ[:, h, 0:n], self.QKG[:, l, 0:1], self.QS16[:, h, :])
        step = int(os.environ.get("ATT_STEP", 6))
        if step <= 1:
            return
        krows = self.I["cache_k"].rearrange("l a s h d -> (l a s) (h d)")
        vrows = self.I["cache_v"].rearrange("l a s h d -> (l a s) (h d)")
        nrows = DEPTH * cfg.npool * PAGE
        PO, PZ = self.PS[4], self.PS[5]

        def col(s_, h_, half):
            return ((h_ * 2 + half) * SPC + s_) * SL

        for acc in (PO, PZ):
            P.mm(acc[:, 0:256], [(self.ZB[:, 0:128], self.ZB)], start=True, stop=False)
        ei = 0
        dbg_ns = int(os.environ.get("ATT_SEQS", SPC))
        dbg_np = int(os.environ.get("ATT_PAGES", NPAGES))
        for s_ in range(dbg_ns):
            qs = slice(s_ * SL, (s_ + 1) * SL)
            for j in range(dbg_np):
                kp, vp = self.KST[j % 2], self.VST[j % 2]
                ic = self.IDX[:, l, s_ * NPAGES + j:s_ * NPAGES + j + 1]
                self.gather(f"gk{j % 2}", kp, krows, ic, nrows)
                self.gather(f"gv{j % 2}", vp, vrows, ic, nrows)
                pt = self.ps()
                P.transpose_multi(pt, [(pt.ap[:, hh * 128:(hh + 1) * 128], kp.ap[:, hh * 128:(hh + 1) * 128], self.ID.ap)
                                       for hh in range(4)], reads=[kp, self.ID])
                ktp = self.KH16[:, (j % 2) * 512:(j % 2) * 512 + 512]
                P.copy("act", ktp, pt)
                vp16 = self.VH16[:, (j % 2) * 4:(j % 2) * 4 + 4, :].rs("p a d -> p (a d)")
                P.copy("dve", vp16, vp)
                if step <= 2:
                    continue
                px = self.ps()[:, 0:64]
                P.mm_multi(px, [(px.ap[:, (hh * 2 + hf) * SL:(hh * 2 + hf + 1) * SL],
                                 [(ktp.ap[hf * 64:(hf + 1) * 64, hh * 128:(hh + 1) * 128],
                                   self.QS16.ap[hf * 64:(hf + 1) * 64, hh, qs])])
                                for hh in range(4) for hf in range(2)], reads=[ktp, self.QS16])
                e = self.BT[ei % 6][:, 0:64]
                ei += 1
                P.act(e, px, AF.Exp, scale=0.125)
                if step <= 3:
                    continue
                P.mm_multi(PO, [(PO.ap[:, col(s_, hh, hf):col(s_, hh, hf) + SL],
                                 [(vp16.ap[:, hh * 128:(hh + 1) * 128], e.ap[:, (hh * 2 + hf) * SL:(hh * 2 + hf + 1) * SL])])
                                for hh in range(4) for hf in range(2)], reads=[vp16, e], start=False, stop=False)
                P.mm_multi(PZ, [(PZ.ap[:, col(s_, hh, hf):col(s_, hh, hf) + SL],
                                 [(self.ONESB.ap, e.ap[:, (hh * 2 + hf) * SL:(hh * 2 + hf + 1) * SL])])
                                for hh in range(4) for hf in range(2)], reads=[self.ONESB, e], start=False, stop=False)
        if step >= 5:
            px = self.ps()[0:n, 0:256]
            P.mm_multi(px, [(px.ap[:, (hh * 2 + hf) * n:(hh * 2 + hf + 1) * n],
                             [(self.KZ16.ap[:, hf, hh, :], self.QS16.ap[:, hh, :])])
                            for hh in range(4) for hf in range(2)], reads=[self.KZ16, self.QS16])
            e = self.BT[ei % 6][0:n, 0:256]
            ei += 1
            P.act(e, px, AF.Exp, scale=0.125)
            bd = self.CONST[0:n, C_BD32:C_BD32 + n].rs("p (a t) -> p a t", a=1).bc([n, 8, n])
            P.tt("dve", e.r3(n), e.r3(n), bd, ALU.mult)
            if os.environ.get("ATT_SUB") != "a":
                P.mm_multi(PO, [(PO.ap[:, col(0, hh, hf):col(0, hh, hf) + n],
                                 [(self.VA16.ap[:, hh * 128:(hh + 1) * 128], e.ap[:, (hh * 2 + hf) * n:(hh * 2 + hf + 1) * n])])
                                for hh in range(4) for hf in range(2)], reads=[self.VA16, e], start=False, stop=False)
                P.mm_multi(PZ, [(PZ.ap[:, col(0, hh, hf):col(0, hh, hf) + n],
                                 [(self.ONESB.ap[0:n, :], e.ap[:, (hh * 2 + hf) * n:(hh * 2 + hf + 1) * n])])
                                for hh in range(4) for hf in range(2)], reads=[self.ONESB, e], start=False, stop=False)
        for acc in (PO, PZ):
            P.mm(acc[:, 0:256], [(self.ZB[:, 0:128], self.ZB)], start=False, stop=True)
        if step <= 5:
            return
        for h in range(4):
            r1, r2, o1, o2 = [self.FT[i][:, 0:n] for i in range(4)]
            c0, c1 = col(0, h, 0), col(0, h, 1)
            P.recip(r1, PZ[:, c0:c0 + n])
            P.recip(r2, PZ[:, c1:c1 + n])
            P.tt("dve", o1, PO[:, c0:c0 + n], r1, ALU.mult)
            P.tt("dve", o2, PO[:, c1:c1 + n], r2, ALU.mult)
            P.stt("dve", o1, o2, self.NLAM[:, l:l + 1], o1, ALU.mult, ALU.add)
            self.subln(o1, l, OB[:, h, 0:n])

    def run_samples(self):
        cfg, P, I, O = self.cfg, self.P, self.I, self.O
        g = Grp("s", ST, SPC, SL, SPC)
        XS = self.XS
        P.dma("sp", "xs", XS, I["xT_s"].rearrange("(c p) t -> p c t", p=128))

        def S32(ch, h):
            return self.SST32[:, ch, :]

        def S16(ch, h):
            return self.SST16[:, ch, :]

        for l in range(cfg.depth):
            def mk(src, dst, chan):
                def pre(h):
                    P.dma("pool", "sst", self.SST32, I[src][l, :, h].rearrange("s p v -> p s v"))
                    P.copy("act", self.SST16, self.SST32)

                def post(h):
                    self.out_dma(chan, O[dst][l, :, h].rearrange("s p v -> p s v"), self.SST32, queue="pool")
                return pre, post
            self.rmsnorm_fm(XS, self.LNM[:, l, :], self.H, g.n)
            pre, post = mk("st_hgrn", "s_hgrn", "sst_o")
            self.hgrn(g, l, S32, S16, pre, post)
            if cfg.upto == "hgrn":
                self.dbg_out(f"soa_{l}", self.OABC[0], [128, 4, TT])
                continue
            self.attn_sample(g, l)
            if cfg.upto == "attn":
                self.dbg_out(f"sob_{l}", self.OABC[1], [128, 4, TT])
                continue
            pre, post = mk("st_gdn", "s_gdn", "sst_o")
            self.gdn(g, l, S32, S16,
                     hist=lambda kc: I["st_gconv"][l, :, kc * 128:(kc + 1) * 128, :].rearrange("s p j -> p s j"),
                     tails=lambda kc: ("sgc", O["s_gconv"][l, :, kc * 128:(kc + 1) * 128, :].rearrange("s p j -> p s j")),
                     pre=pre, post=post)
            if cfg.upto == "gdn":
                self.dbg_out(f"soc_{l}", self.OABC[2], [128, 4, TT])
                continue
            self.merge_out(g, l, XS)
            self.ffn(g, l, XS,
                     hist=lambda j: I["st_fconv"][l, :, j * 128:(j + 1) * 128, :].rearrange("s p j -> p s j"),
                     tails=lambda j: ("sfc", O["s_fconv"][l, :, j * 128:(j + 1) * 128, :].rearrange("s p j -> p s j")))
        if cfg.upto == "all":
            self.out_dma("ys", O["yT_s"].rearrange("(c p) t -> p c t", p=128), XS)

    def run(self):
        cfg, P, I = self.cfg, self.P, self.I
        with contextlib.ExitStack() as st:
            self.st = st
            self.setup()
            self.alloc_mixers()
            self.precast_weights()
            XR = self.sb("XR", [128, 8, TT])
            if cfg.samples:
                self.alloc_samples()
                self.run_samples()
            if cfg.prompt:
                for t in range(cfg.seq // TT):
                    g = Grp("p", TT, 1, 64, 8, tile=t)
                    P.dma("sp", "x", XR, I["xT_p"][:, t * TT:(t + 1) * TT].rearrange("(c p) t -> p c t", p=128))
                    for l in range(cfg.depth):
                        self.rmsnorm_fm(XR, self.LNM[:, l, :], self.H, g.n)
                        self.hgrn(g, l, lambda ch, h, l=l: self.HS32[:, l, h, :],
                                  lambda ch, h, l=l: self.HS16[:, l, h, :])
                        if cfg.upto == "hgrn":
                            self.dbg_out(f"oa_{t}_{l}", self.OABC[0], [128, 4, TT])
                            continue
                        self.attn_prompt(g, l)
                        if cfg.upto == "attn":
                            self.dbg_out(f"ob_{t}_{l}", self.OABC[1], [128, 4, TT])
                            continue
                        self.gdn(g, l, lambda ch, h, l=l: self.GS32[:, l, h, :],
                                 lambda ch, h, l=l: self.GS16[:, l, h, :])
                        if cfg.upto == "gdn":
                            self.dbg_out(f"oc_{t}_{l}", self.OABC[2], [128, 4, TT])
                            continue
                        self.merge_out(g, l, XR)
                        self.ffn(g, l, XR)
                    if cfg.upto == "all":
                        self.out_dma("yp", self.O["yT_p"][:, t * TT:(t + 1) * TT].rearrange("(c p) t -> p c t", p=128), XR, queue="pool")
                if cfg.upto == "all":
                    for l in range(cfg.depth):
                        self.out_dma("pst", self.O["p_hgrn"][l].rearrange("h p v -> p h v"), self.HS32[:, l])
                        self.out_dma("pst", self.O["p_gdn"][l].rearrange("h p v -> p h v"), self.GS32[:, l])
                        self.out_dma("pst", self.O["p_gconv"][l].rearrange("(k p) j -> p k j", p=128), self.TAILG[:, l])
                        self.out_dma("pst", self.O["p_fconv"][l].rearrange("(k p) j -> p k j", p=128), self.TAILF[:, l])
                if cfg.upto == "hgrn":
                    self.dbg_out("hs", self.HS32, [128, cfg.depth, 4, 128])
                if cfg.upto == "gdn":
                    self.dbg_out("gs", self.GS32, [128, cfg.depth, 4, 128])
                    self.dbg_out("tailg", self.TAILG, [128, DEPTH, 12, 3])
            P.emit(final_chans=self.out_chans)
        return self.nc


def _host_inputs(inp):
    c = np.ascontiguousarray
    f = np.float32
    shared = {k: c(np.asarray(inp[k], dtype=f)) for k in
              ("cache_k", "cache_v", "w_in", "w_branch_a", "w_branch_b", "w_branch_c", "w_out", "w_up", "w_down")}
    shared["params"] = _pack_params(inp)
    shared["consts"] = _const_table()
    shared["rope_p"] = _rope_table(np.arange(SEQ))
    shared["rope_s"] = _rope_table(np.tile(PAST + np.arange(SL), SPC))
    xp = np.asarray(inp["x_prompt"], dtype=f)
    xs = np.asarray(inp["x_sample"], dtype=f)
    maps = []
    for core in range(NCORES):
        b = core // 2
        s0 = core * SPC
        m = dict(shared)
        m["xT_p"] = c(xp[b].T)
        m["xT_s"] = c(xs[s0:s0 + SPC].reshape(ST, D).T)
        m["st_hgrn"] = c(np.asarray(inp["state_hgrn"], f)[:, s0:s0 + SPC])
        m["st_gdn"] = c(np.asarray(inp["state_gdn"], f)[:, s0:s0 + SPC])
        m["st_gconv"] = c(np.asarray(inp["state_gdn_conv"], f)[:, s0:s0 + SPC].transpose(0, 1, 3, 2))
        m["st_fconv"] = c(np.asarray(inp["state_ffn_conv"], f)[:, s0:s0 + SPC].transpose(0, 1, 3, 2))
        m["page_table"] = c(np.asarray(inp["page_table"], np.int32)[s0:s0 + SPC])
        maps.append(m)
    return maps


def _host_outputs(res):
    r = res
    f = np.float32
    y_p = np.stack([r[2 * b]["yT_p"].T for b in range(NB)]).astype(f)
    y_s = np.concatenate([r[ci]["yT_s"].T.reshape(SPC, SL, D) for ci in range(NCORES)]).astype(f)
    p_hgrn = np.stack([r[2 * b]["p_hgrn"] for b in range(NB)], axis=1).astype(f)
    p_k = np.stack([r[2 * b]["p_kT"].transpose(0, 3, 1, 2) for b in range(NB)], axis=1).astype(f)
    p_v = np.stack([r[2 * b]["p_v"].reshape(DEPTH, SEQ, NH, 128) for b in range(NB)], axis=1).astype(f)
    p_gdn = np.stack([r[2 * b]["p_gdn"] for b in range(NB)], axis=1).astype(f)
    p_gc = np.stack([r[2 * b]["p_gconv"].transpose(0, 2, 1) for b in range(NB)], axis=1).astype(f)
    p_fc = np.stack([r[2 * b]["p_fconv"].transpose(0, 2, 1) for b in range(NB)], axis=1).astype(f)
    s_hgrn = np.concatenate([r[ci]["s_hgrn"] for ci in range(NCORES)], axis=1).astype(f)
    s_k = np.concatenate([r[ci]["s_kT"].transpose(0, 3, 1, 2).reshape(DEPTH, SPC, SL, NH, 128)
                          for ci in range(NCORES)], axis=1).astype(f)
    s_v = np.concatenate([r[ci]["s_v"].reshape(DEPTH, SPC, SL, NH, 128) for ci in range(NCORES)], axis=1).astype(f)
    s_gdn = np.concatenate([r[ci]["s_gdn"] for ci in range(NCORES)], axis=1).astype(f)
    s_gc = np.concatenate([r[ci]["s_gconv"].transpose(0, 1, 3, 2) for ci in range(NCORES)], axis=1).astype(f)
    s_fc = np.concatenate([r[ci]["s_fconv"].transpose(0, 1, 3, 2) for ci in range(NCORES)], axis=1).astype(f)
    return (y_p, y_s, p_hgrn, p_k, p_v, p_gdn, p_gc, p_fc, s_hgrn, s_k, s_v, s_gdn, s_gc, s_fc)


def kernel(**inputs):
    nc = Builder(Cfg()).run()
    in_maps = _host_inputs(inputs)
    res = run_bass_kernel_spmd(nc, in_maps, core_ids=list(range(NCORES)))
    return _host_outputs(res.results)
```
